# Optimizing a Trainium2 kernel written in Bass

```python
import math
import jax, jax.numpy as jnp
from jax import lax
import numpy as np

D_MODEL = 1024
BATCH = 8
SEQ = 2048
DEPTH = 1

MIX_WIDTH = D_MODEL
SSM_WIDTH = MIX_WIDTH // 2
SSM_GROUP = 16
SSM_GROUPS = SSM_WIDTH // SSM_GROUP
SSM_STATE = 64
ATTN_WIDTH = MIX_WIDTH - SSM_WIDTH
HEAD_DIM = 64
N_Q_HEADS = ATTN_WIDTH // HEAD_DIM
N_KV_HEADS = 2
Q_PER_KV = N_Q_HEADS // N_KV_HEADS
KV_WIDTH = N_KV_HEADS * HEAD_DIM
IN_WIDTH = SSM_WIDTH + ATTN_WIDTH + 2 * KV_WIDTH
WINDOW = 128
BLOCK = 128
ROPE_DIM = HEAD_DIM // 4
ROPE_THETA = 500000.0
D_FF = -(-8 * D_MODEL // (3 * 256)) * 256
NORM_EPS = 1e-6
DT_MIN = 1e-3
DT_MAX = 1e-1
MASK_VALUE = -1e30

kernel_name = "hymba_s5_swa_sink_hybrid"


def rms_norm(x, g):
    xf = x.astype(jnp.float32)
    y = xf * lax.rsqrt(jnp.mean(xf * xf, axis=-1, keepdims=True) + NORM_EPS)
    return (y * g.astype(jnp.float32)).astype(x.dtype)


def _s5_combine(e1, e2):
    a1r, a1i, b1r, b1i = e1
    a2r, a2i, b2r, b2i = e2
    ar = a2r * a1r - a2i * a1i
    ai = a2r * a1i + a2i * a1r
    br = a2r * b1r - a2i * b1i + b2r
    bi = a2r * b1i + a2i * b1r + b2i
    return (ar, ai, br, bi)


def s5_mixer(u, lam_re, lam_im, log_dt, b_re, b_im, c_re, c_im, d, w_glu, b_glu):
    bsz, L, _ = u.shape
    uf = u.astype(jnp.float32).reshape(bsz, L, SSM_GROUPS, SSM_GROUP)
    lr = jnp.minimum(lam_re.astype(jnp.float32), -1e-4)
    li = lam_im.astype(jnp.float32)
    dt = jnp.exp(log_dt.astype(jnp.float32))[:, None]
    mag = jnp.exp(lr * dt)
    ar = mag * jnp.cos(li * dt)
    ai = mag * jnp.sin(li * dt)
    den = lr * lr + li * li
    fr = ((ar - 1.0) * lr + ai * li) / den
    fi = (ai * lr - (ar - 1.0) * li) / den
    br = b_re.astype(jnp.float32)
    bi = b_im.astype(jnp.float32)
    bbar_re = fr[..., None] * br - fi[..., None] * bi
    bbar_im = fr[..., None] * bi + fi[..., None] * br
    bu_re = jnp.einsum('blgh,gph->blgp', uf, bbar_re)
    bu_im = jnp.einsum('blgh,gph->blgp', uf, bbar_im)
    a_re = jnp.broadcast_to(ar, (1, L, SSM_GROUPS, SSM_STATE))
    a_im = jnp.broadcast_to(ai, (1, L, SSM_GROUPS, SSM_STATE))
    _, _, xr, xi = lax.associative_scan(_s5_combine, (a_re, a_im, bu_re, bu_im), axis=1)
    y = (jnp.einsum('blgp,ghp->blgh', xr, c_re.astype(jnp.float32))
         - jnp.einsum('blgp,ghp->blgh', xi, c_im.astype(jnp.float32))
         + d.astype(jnp.float32) * uf)
    y = jax.nn.gelu(y.reshape(bsz, L, SSM_WIDTH))
    z = y @ w_glu.astype(jnp.float32) + b_glu.astype(jnp.float32)
    out = z[..., :SSM_WIDTH] * jax.nn.sigmoid(z[..., SSM_WIDTH:])
    return out.astype(u.dtype)


def partial_rotary(t, positions):
    half = ROPE_DIM // 2
    inv_freq = ROPE_THETA ** (-jnp.arange(half, dtype=jnp.float32) * 2.0 / ROPE_DIM)
    ang = positions.astype(jnp.float32)[..., None] * inv_freq
    cos = jnp.cos(ang)[:, :, None, :]
    sin = jnp.sin(ang)[:, :, None, :]
    tf = t.astype(jnp.float32)
    t1 = tf[..., :half]
    t2 = tf[..., half:ROPE_DIM]
    rot = jnp.concatenate([t1 * cos - t2 * sin, t2 * cos + t1 * sin, tf[..., ROPE_DIM:]], axis=-1)
    return rot.astype(t.dtype)


def sliding_window_sink_attention(q, k, v, sinks):
    bsz, L = q.shape[0], q.shape[1]
    nb = L // BLOCK
    qb = q.reshape(bsz, nb, BLOCK, N_KV_HEADS, Q_PER_KV, HEAD_DIM).transpose(1, 0, 2, 3, 4, 5)

    def windows(t):
        tb = t.reshape(bsz, nb, BLOCK, N_KV_HEADS, HEAD_DIM)
        prev = jnp.pad(tb, ((0, 0), (1, 0), (0, 0), (0, 0), (0, 0)))[:, :-1]
        return jnp.concatenate([prev, tb], axis=2).transpose(1, 0, 2, 3, 4)

    kw = windows(k)
    vw = windows(v)
    qi = jnp.arange(BLOCK)[:, None]
    kj = jnp.arange(2 * BLOCK)[None, :]
    rel = qi + BLOCK - kj
    band = (rel >= 0) & (rel < WINDOW)
    sink = sinks.astype(jnp.float32).reshape(N_KV_HEADS, Q_PER_KV)[None, :, :, None, None]
    scale = HEAD_DIM ** -0.5

    def one_block(args):
        q_blk, k_blk, v_blk, blk = args
        valid = band & (blk * BLOCK - BLOCK + kj >= 0)
        s = jnp.einsum('bqhgd,bkhd->bhgqk', q_blk, k_blk).astype(jnp.float32) * scale
        s = jnp.where(valid, s, MASK_VALUE)
        m = jnp.maximum(jnp.max(s, axis=-1, keepdims=True), sink)
        p = jnp.exp(s - m)
        p = p / (jnp.sum(p, axis=-1, keepdims=True) + jnp.exp(sink - m))
        return jnp.einsum('bhgqk,bkhd->bqhgd', p.astype(v_blk.dtype), v_blk)

    o = lax.map(one_block, (qb, kw, vw, jnp.arange(nb)))
    return o.transpose(1, 0, 2, 3, 4, 5).reshape(bsz, L, ATTN_WIDTH)


def setup_inputs(seed: int = 0) -> dict:
    key = jax.random.key(seed)
    ks = jax.random.split(key, 24)
    f32 = jnp.float32

    def nrm(k, shape, std):
        return jax.random.normal(k, shape, f32) * std

    def gain(k, n):
        return 1.0 + 0.05 * jax.random.normal(k, (DEPTH, n), f32)

    n_idx = jnp.arange(SSM_STATE, dtype=f32)
    lam_re = -0.5 + 0.01 * jax.random.normal(ks[3], (DEPTH, SSM_GROUPS, SSM_STATE), f32)
    lam_im = math.pi * n_idx + 0.01 * jax.random.normal(ks[4], (DEPTH, SSM_GROUPS, SSM_STATE), f32)
    log_dt = jax.random.uniform(ks[5], (DEPTH, SSM_GROUPS), f32, math.log(DT_MIN), math.log(DT_MAX))
    return {
        "x": jax.random.normal(ks[0], (BATCH, SEQ, D_MODEL), f32),
        "positions": jnp.broadcast_to(jnp.arange(SEQ, dtype=jnp.int32), (BATCH, SEQ)),
        "g_pre_mix": gain(ks[1], D_MODEL),
        "w_in": nrm(ks[2], (DEPTH, D_MODEL, IN_WIDTH), D_MODEL ** -0.5),
        "ssm_lambda_re": lam_re,
        "ssm_lambda_im": lam_im,
        "ssm_log_dt": log_dt,
        "ssm_b_re": nrm(ks[6], (DEPTH, SSM_GROUPS, SSM_STATE, SSM_GROUP), (2 * SSM_GROUP) ** -0.5),
        "ssm_b_im": nrm(ks[7], (DEPTH, SSM_GROUPS, SSM_STATE, SSM_GROUP), (2 * SSM_GROUP) ** -0.5),
        "ssm_c_re": nrm(ks[8], (DEPTH, SSM_GROUPS, SSM_GROUP, SSM_STATE), (2 * SSM_STATE) ** -0.5),
        "ssm_c_im": nrm(ks[9], (DEPTH, SSM_GROUPS, SSM_GROUP, SSM_STATE), (2 * SSM_STATE) ** -0.5),
        "ssm_d": nrm(ks[10], (DEPTH, SSM_GROUPS, SSM_GROUP), 1.0),
        "w_glu": nrm(ks[11], (DEPTH, SSM_WIDTH, 2 * SSM_WIDTH), SSM_WIDTH ** -0.5),
        "b_glu": nrm(ks[12], (DEPTH, 2 * SSM_WIDTH), 0.01),
        "attn_sinks": nrm(ks[13], (DEPTH, N_Q_HEADS), 0.5),
        "g_ssm_out": gain(ks[14], SSM_WIDTH),
        "g_attn_out": gain(ks[15], ATTN_WIDTH),
        "w_out": nrm(ks[16], (DEPTH, MIX_WIDTH, D_MODEL), MIX_WIDTH ** -0.5),
        "g_post_mix": gain(ks[17], D_MODEL),
        "g_pre_ffn": gain(ks[18], D_MODEL),
        "w_gate_up": nrm(ks[19], (DEPTH, D_MODEL, 2 * D_FF), D_MODEL ** -0.5),
        "w_down": nrm(ks[20], (DEPTH, D_FF, D_MODEL), D_FF ** -0.5),
        "g_post_ffn": gain(ks[21], D_MODEL),
    }


def reference(x, positions, g_pre_mix, w_in, ssm_lambda_re, ssm_lambda_im, ssm_log_dt,
              ssm_b_re, ssm_b_im, ssm_c_re, ssm_c_im, ssm_d, w_glu, b_glu, attn_sinks,
              g_ssm_out, g_attn_out, w_out, g_post_mix, g_pre_ffn, w_gate_up, w_down, g_post_ffn):
    bsz, L, _ = x.shape
    h = x
    for l in range(DEPTH):
        hn = rms_norm(h, g_pre_mix[l])
        proj = hn @ w_in[l]
        u = proj[..., :SSM_WIDTH]
        q = proj[..., SSM_WIDTH:SSM_WIDTH + ATTN_WIDTH]
        k = proj[..., SSM_WIDTH + ATTN_WIDTH:SSM_WIDTH + ATTN_WIDTH + KV_WIDTH]
        v = proj[..., SSM_WIDTH + ATTN_WIDTH + KV_WIDTH:]

        y_ssm = s5_mixer(u, ssm_lambda_re[l], ssm_lambda_im[l], ssm_log_dt[l],
                         ssm_b_re[l], ssm_b_im[l], ssm_c_re[l], ssm_c_im[l], ssm_d[l],
                         w_glu[l], b_glu[l])

        q = partial_rotary(q.reshape(bsz, L, N_Q_HEADS, HEAD_DIM), positions)
        k = partial_rotary(k.reshape(bsz, L, N_KV_HEADS, HEAD_DIM), positions)
        v = v.reshape(bsz, L, N_KV_HEADS, HEAD_DIM)
        y_attn = sliding_window_sink_attention(q, k, v, attn_sinks[l])

        merged = jnp.concatenate([rms_norm(y_ssm, g_ssm_out[l]), rms_norm(y_attn, g_attn_out[l])], axis=-1)
        h = h + rms_norm(merged @ w_out[l], g_post_mix[l])

        hn = rms_norm(h, g_pre_ffn[l])
        gu = hn @ w_gate_up[l]
        ff = (jax.nn.silu(gu[..., :D_FF]) * gu[..., D_FF:]) @ w_down[l]
        h = h + rms_norm(ff, g_post_ffn[l])
    return h
```

```python
import numpy as np
from contextlib import ExitStack
import concourse.bass as bass
import concourse.mybir as mybir
from concourse.bass_utils import run_bass_kernel_spmd

F32, BF16, I32 = mybir.dt.float32, mybir.dt.bfloat16, mybir.dt.int32
AF = mybir.ActivationFunctionType
ALU = mybir.AluOpType

T = 2048
D = 1024
NT = 16
DFF = 2816
NF = 22
EPS = 1e-6
TWO_PI = 2.0 * np.pi


class Prog:
    ENG = ("pe", "act", "dve", "pool", "sp")

    def __init__(self, nc, plan):
        self.nc = nc
        self.plan = plan
        self.emit = plan is not None
        self.engs = {"pe": nc.tensor, "act": nc.scalar, "dve": nc.vector, "pool": nc.gpsimd, "sp": nc.sync}
        self.ops = []
        self.last_w = {}
        self.readers = {}
        self.need = set()
        self.group_total = {}
        self.sems = {}
        self.cnt = {}
        self.wm = {e: {} for e in self.ENG}
        self.ev = {}
        self.es = ExitStack()
        self.deferred = None
        self.backlog = []
        for e in ("pe", "act", "dve", "pool"):
            self.new_sem(e)

    def new_sem(self, key):
        self.sems[key] = self.es.enter_context(self.nc.semaphore("s_" + str(key)))
        self.cnt[key] = 0
        return key

    def _deps(self, engine, is_dma, reads, writes):
        raw = set()
        oth = set()
        for r in reads:
            if r in self.last_w:
                raw.add(self.last_w[r])
        for w in writes:
            if w in self.last_w:
                oth.add(self.last_w[w])
            oth.update(self.readers.get(w, ()))
        out = list(raw) if engine != "pe" else [d for d in raw if self.ops[d][0] != "pe" or self.ops[d][1]]
        for d in oth:
            if d in raw:
                continue
            pe, pd = self.ops[d]
            if (not is_dma) and (not pd) and pe == engine and engine == "pe":
                continue
            out.append(d)
        return out

    def _wait(self, engine, d):
        semkey, value, clock = self.ev[d]
        wm = self.wm[engine]
        if wm.get(semkey, 0) >= value:
            return
        self.engs[engine].wait_ge(self.sems[semkey], value)
        for k, v in clock.items():
            if wm.get(k, 0) < v:
                wm[k] = v
        wm[semkey] = value

    def op(self, engine, fn, reads=(), writes=(), dma_sem=None, group=False, _now=False, cost=0.3):
        if self.deferred is not None and not _now and dma_sem is None:
            self.deferred.append((engine, fn, reads, writes, cost))
            return None
        i = len(self.ops)
        is_dma = dma_sem is not None
        deps = self._deps(engine, is_dma, reads, writes)
        self.ops.append((engine, is_dma))
        for r in reads:
            self.readers.setdefault(r, []).append(i)
        for w in writes:
            self.last_w[w] = i
            self.readers[w] = []
        for d in deps:
            self.need.add(d)
        if is_dma:
            self.group_total[dma_sem] = self.group_total.get(dma_sem, 0) + 16
        if not self.emit:
            return i
        for d in sorted(deps):
            self._wait(engine, d)
        ins = fn()
        if is_dma:
            ins.then_inc(self.sems[dma_sem], 16)
            self.cnt[dma_sem] += 16
            val = self.plan["group_total"][dma_sem] if group else self.cnt[dma_sem]
            clock = dict(self.wm[engine])
            self.ev[i] = (dma_sem, val, clock)
        elif i in self.plan["need"]:
            ins.then_inc(self.sems[engine], 1)
            self.cnt[engine] += 1
            self.ev[i] = (engine, self.cnt[engine], dict(self.wm[engine]))
        return i

    def defer_begin(self):
        self.deferred = []

    def defer_end(self):
        self.backlog.extend(self.deferred)
        self.deferred = None

    def flush(self, k=None):
        budget = 1e9 if k is None else float(k)
        while self.backlog and budget > 0:
            engine, fn, reads, writes, cost = self.backlog.pop(0)
            self.op(engine, fn, reads, writes, _now=True)
            budget -= cost

    def barrier(self):
        last = {}
        for i, (e, is_dma) in enumerate(self.ops):
            if is_dma:
                last[("dma", i)] = i
            else:
                last[e] = i
        idxs = sorted(set(last.values()))
        for d in idxs:
            self.need.add(d)
        if not self.emit:
            return
        for e in self.ENG:
            for d in idxs:
                pe, pd = self.ops[d]
                if (not pd) and pe == e:
                    continue
                if d in self.ev:
                    self._wait(e, d)

    def finish(self):
        self.barrier()

    def get_plan(self):
        return {"need": set(self.need), "group_total": dict(self.group_total)}


def _pipeline(n_items, stages, hook=None):
    ns = len(stages)
    for it in range(n_items + ns - 1):
        for k, st in enumerate(stages):
            i = it - k
            if 0 <= i < n_items:
                st(i)
        if hook is not None:
            hook(it)


def _build(nc, plan, dbg=()):
    P = Prog(nc, plan)
    es = ExitStack()

    def dram_in(name, shape, dt=F32):
        return nc.dram_tensor(name, list(shape), dt, kind="ExternalInput").ap()

    x_d = dram_in("x", [T, D])
    pos_d = dram_in("pos", [128, NT], I32)
    gpm_d = dram_in("gpm", [128, D])
    gpf_d = dram_in("gpf", [128, D])
    gssm_d = dram_in("gssm", [128, 512])
    bglu_d = dram_in("bglu", [128, 1024])
    win_d = dram_in("win", [128, 8, 1280])
    wglu_d = dram_in("wglu", [128, 4, 1024])
    wout_d = dram_in("wout", [128, 8, 1024])
    wgu_d = dram_in("wgu", [11, 128, 2 * 8 * 256])
    wdn_d = dram_in("wdn", [128, NF, 1024])
    ident_d = dram_in("c_ident", [128, 128])
    pkA_d = dram_in("pkA", [128, 1112])
    pkB_d = dram_in("pkB", [128, 1072])
    amask_d = dram_in("c_amask", [128, 2, 2, 2, 128])
    out_d = nc.dram_tensor("out", [T, D], F32, kind="ExternalOutput").ap()
    dbg_d = {}

    def sb(name, shape, dt=F32, stack=None, side="left"):
        return (stack or es).enter_context(nc.sbuf_tensor("sb_" + name, list(shape), dt, side=side))

    def sbr(name, shape, dt=F32, stack=None):
        return sb(name, shape, dt, stack, side="right")

    def ps(name, shape, dt=F32, stack=None):
        return (stack or es).enter_context(nc.psum_tensor("ps_" + name, list(shape), dt))

    V, S, G, PE_ = nc.vector, nc.scalar, nc.gpsimd, nc.tensor

    def dump(name, ap_sb, shape, dt=F32, region=None):
        if name not in dbg:
            return
        d = nc.dram_tensor("dbg_" + name, list(shape), dt, kind="ExternalOutput").ap()
        dbg_d[name] = d
        sk = P.new_sem("dbg_" + name)
        P.barrier()
        P.op("sp", lambda: nc.sync.dma_start(out=d, in_=ap_sb), reads=[], writes=[("dbg", name)], dma_sem=sk)
        P.barrier()

    PKA = [("ident_f", 128), ("tmask", 128), ("kvc", 24), ("cidx", 256), ("invf", 8), ("gpre", 8), ("gpffn", 8), ("gattn", 512), ("sinkexp", 8), ("dl", 32)]
    pkA_sb = sb("pkA", [128, sum(w for _, w in PKA)])
    _v = {}
    _o = 0
    for _n, _w in PKA:
        _v[_n] = pkA_sb[:, _o:_o + _w]
        _o += _w
    ident_f, tmask, kvc, cidx, invf, gpre, gpffn, gattn, sinkexp, dl = [_v[n] for n, _ in PKA]
    ident_b = sb("ident_b", [128, 128], BF16)
    amask = sb("amask", [128, 2, 2, 2, 128], BF16)
    nhalf = sb("nhalf", [128, 1])
    posi = sb("posi", [128, NT], I32)
    actbuf = sb("actbuf", [128, 8, T], BF16)
    hnT = actbuf
    mergedT = actbuf
    stPO = ExitStack()
    WS_b = sb("WS_b", [128, 16, 2, 2, 64], BF16, stPO)
    Toep_b = sb("Toep_b", [128, 16, 2, 128], BF16, stPO)
    CA_b = sb("CA_b", [128, 16, 2, 128], BF16, stPO)
    Amag = sb("Amag", [128, 16], F32, stPO)
    phi = sb("phi", [128, 16], F32, stPO)
    stU = ExitStack()
    u_tm2 = sbr("u_tm2", [128, 2, 32, 8, 16], BF16, stU)
    stBC = ExitStack()
    qkT = sbr("qkT", [128, 6, T], BF16, stBC)
    Vsb = sbr("Vsb", [128, NT, 2, 65], BF16, stBC)

    sem_par = P.new_sem("par")

    def pload(dst, src, name, eng="sp"):
        P.op(eng, lambda: P.engs[eng].dma_start(out=dst, in_=src), writes=[name], dma_sem=sem_par, group=True)

    P.op("sp", lambda: nc.sync.dma_start(out=pkA_sb[:], in_=pkA_d), writes=[n for n, _ in PKA], dma_sem=sem_par, group=True)
    pload(posi[:], pos_d, "posi")
    sem_pw = P.new_sem("parw")

    def wload(dst, src, name):
        P.op("pool", lambda: nc.gpsimd.dma_start(out=dst, in_=src), writes=[name], dma_sem=sem_pw, group=True)

    wload(ident_b[:], ident_d, "ident_b")
    wload(amask[:], amask_d, "amask")

    stAB = ExitStack()
    win_b = sbr("win_b", [128, 8, 1280], BF16, stAB)
    for k in range(8):
        wload(win_b[:, k, 512:1280], win_d[:, k, 512:1280], ("win_b", k))
    sem_pw2 = P.new_sem("parw2")
    for k in range(8):
        P.op("pool", lambda: nc.gpsimd.dma_start(out=win_b[:, k, 0:512], in_=win_d[:, k, 0:512]), writes=[("win_u", k)], dma_sem=sem_pw2, group=True)

    P.op("dve", lambda: V.memset(nhalf[:], -0.5), writes=["nhalf"])
    P.op("act", lambda: S.activation(out=sinkexp[:], in_=sinkexp[:], func=AF.Exp), reads=["sinkexp"], writes=["sinkexp"])

    cs = sbr("cs", [128, 2, NT, 8], F32, stAB)
    dump("cs", cs[:], [128, 2, NT, 8])

    P.op("pool", lambda: G.memset(Vsb[:], 1.0), writes=["Vsb"])

    sem_pp = P.new_sem("parP")

    def ploadP(dst, src, name):
        P.op("sp", lambda: nc.sync.dma_start(out=dst, in_=src), writes=[name], dma_sem=sem_pp, group=True)

    stP = ExitStack()
    if True:
        pkB_sb = sb("pkB", [128, 1072], F32, stP, side="right")
        lre, lim, ldt = pkB_sb[:, 0:16], pkB_sb[:, 16:32], pkB_sb[:, 32:48]
        bre = pkB_sb[:, 48:304].rearrange("p (q h) -> p q h", q=16)
        bim = pkB_sb[:, 304:560].rearrange("p (q h) -> p q h", q=16)
        cre = pkB_sb[:, 560:816].rearrange("p (q h) -> p q h", q=16)
        cim = pkB_sb[:, 816:1072].rearrange("p (q h) -> p q h", q=16)
        P.op("sp", lambda: nc.sync.dma_start(out=pkB_sb[:], in_=pkB_d), writes=["lre", "lim", "ldt", "bre", "bim", "cre", "cim"], dma_sem=sem_pp, group=True)
        lr = sb("lr", [128, 16], F32, stP, side="right")
        dtt = sb("dtt", [128, 16], F32, stP, side="right")
        ldv = sb("ldv", [128, 16], F32, stP, side="right")
        th = sb("th", [128, 16], F32, stP, side="right")
        den = sb("denP", [128, 16], F32, stP, side="right")
        rdn = sb("rdnP", [128, 16], F32, stP, side="right")
        ski = sb("ski", [128, 16], I32, stP, side="right")
        s16 = [sb(f"s16_{i}", [128, 16], F32, stP, side="right") for i in range(4)]
        argm = sb("argm", [128, 16, 24], F32, stP, side="right")
        Emag = argm
        ett = sb("ett", [128, 2, 16, 24], F32, stP, side="right")
        eki = sb("eki", [128, 2, 16, 24], I32, stP, side="right")
        ekf = sb("ekf", [128, 2, 16, 24], F32, stP, side="right")
        Esc = ett
        Ere = sb("Ere", [128, 16, 24], F32, stP, side="right")
        Eim = sb("Eim", [128, 16, 24], F32, stP, side="right")
        fr = sb("fr", [128, 16], F32, stP, side="right")
        fi = sb("fi", [128, 16], F32, stP, side="right")
        Bre = sb("Bre", [128, 16, 16], F32, stP, side="right")
        Bim = sb("Bim", [128, 16, 16], F32, stP, side="right")
        b16 = [sb(f"b16_{i}", [128, 16, 16], F32, stP, side="right") for i in range(2)]
        t1 = sb("t1P", [128, 16, 8, 16], F32, stP, side="right")
        t2 = sb("t2P", [128, 16, 8, 16], F32, stP, side="right")
        WSt_b = sb("WSt_b", [128, 16, 2, 128], BF16, stP, side="right")
        CN_b = sb("CN_b", [128, 16, 2, 128], BF16, stP, side="right")
        tmpT = sb("tmpT", [128, 4, 128], F32, stP, side="right")

        posf = s16[0]
        tt = t1[:, 0:2].rearrange("p a b c -> p a (b c)").rearrange("p a (n f) -> p a n f", f=8)
        kf = t1[:, 2:4].rearrange("p a b c -> p a (b c)").rearrange("p a (n f) -> p a n f", f=8)
        ki = t2[:, 0:2].rearrange("p a b c -> p a (b c)").rearrange("p a (n f) -> p a n f", f=8).bitcast(I32)
        P.op("dve", lambda: V.tensor_copy(out=posf[:], in_=posi[:]), reads=["posi"], writes=["s16_0"])
        P.op("dve", lambda: V.tensor_tensor(out=tt[:, 0], in0=posf[:, :].unsqueeze(2).to_broadcast([128, NT, 8]),
                                             in1=invf[:, :].unsqueeze(1).to_broadcast([128, NT, 8]), op=ALU.mult),
             reads=["s16_0", "invf"], writes=["t1P"])
        P.op("dve", lambda: V.tensor_scalar(out=tt[:, 0], in0=tt[:, 0], scalar1=float(1.0 / TWO_PI), scalar2=None, op0=ALU.mult),
             reads=["t1P"], writes=["t1P"])
        P.op("dve", lambda: V.tensor_scalar(out=tt[:, 1], in0=tt[:, 0], scalar1=0.25, scalar2=None, op0=ALU.add),
             reads=["t1P"], writes=["t1P"])
        P.op("dve", lambda: V.tensor_copy(out=ki, in_=tt), reads=["t1P"], writes=["t2P"])
        P.op("dve", lambda: V.tensor_copy(out=kf, in_=ki), reads=["t2P"], writes=["t1P"])
        P.op("dve", lambda: V.tensor_tensor(out=tt, in0=tt, in1=kf, op=ALU.subtract), reads=["t1P"], writes=["t1P"])
        P.op("act", lambda: S.activation(out=cs[:], in_=tt, func=AF.Sin, scale=float(TWO_PI)), reads=["t1P"], writes=["cs"])

        P.defer_begin()

        def dv(fn, reads, writes, cost=0.3):
            P.op("dve", fn, reads=reads, writes=writes, cost=cost)

        def tt_(out, a, b, op, reads, writes, cost=0.3):
            dv(lambda: V.tensor_tensor(out=out, in0=a, in1=b, op=op), reads, writes, cost)

        dv(lambda: V.tensor_scalar(out=lr[:], in0=lre[:], scalar1=-1e-4, scalar2=None, op0=ALU.min), ["lre"], ["lr"])
        P.op("act", lambda: S.activation(out=dtt[:], in_=ldt[:], func=AF.Exp), reads=["ldt"], writes=["dtt"])
        tt_(ldv[:], lr[:], dtt[:], ALU.mult, ["lr", "dtt"], ["ldv"])
        tt_(th[:], lim[:], dtt[:], ALU.mult, ["lim", "dtt"], ["th"])
        tt_(s16[0][:], lr[:], lr[:], ALU.mult, ["lr"], ["s16_0"])
        tt_(s16[1][:], lim[:], lim[:], ALU.mult, ["lim"], ["s16_1"])
        tt_(den[:], s16[0][:], s16[1][:], ALU.add, ["s16_0", "s16_1"], ["denP"])
        dv(lambda: V.reciprocal(out=rdn[:], in_=den[:]), ["denP"], ["rdnP"])
        tt_(argm[:], ldv[:, :].unsqueeze(2).to_broadcast([128, 16, 24]), kvc[:, :].unsqueeze(1).to_broadcast([128, 16, 24]), ALU.mult,
            ["ldv", "kvc"], ["argm"])
        P.op("act", lambda: S.activation(out=Emag[:], in_=argm[:], func=AF.Exp), reads=["argm"], writes=["Emag"])
        tt_(ett[:, 0], th[:, :].unsqueeze(2).to_broadcast([128, 16, 24]), kvc[:, :].unsqueeze(1).to_broadcast([128, 16, 24]), ALU.mult,
            ["th", "kvc"], ["ett"])
        dv(lambda: V.tensor_scalar(out=ett[:, 0], in0=ett[:, 0], scalar1=float(1.0 / TWO_PI), scalar2=None, op0=ALU.mult), ["ett"], ["ett"])
        dv(lambda: V.tensor_scalar(out=ett[:, 1], in0=ett[:, 0], scalar1=0.25, scalar2=None, op0=ALU.add), ["ett"], ["ett"])
        dv(lambda: V.tensor_copy(out=eki[:], in_=ett[:]), ["ett"], ["eki"])
        dv(lambda: V.tensor_copy(out=ekf[:], in_=eki[:]), ["eki"], ["ekf"])
        tt_(ett[:], ett[:], ekf[:], ALU.subtract, ["ett", "ekf"], ["ett"])
        P.op("act", lambda: S.activation(out=Esc[:], in_=ett[:], func=AF.Sin, scale=float(TWO_PI)), reads=["ett"], writes=["Esc"])
        tt_(Ere[:], Emag[:], Esc[:, 1], ALU.mult, ["Emag", "Esc"], ["Ere"])
        tt_(Eim[:], Emag[:], Esc[:, 0], ALU.mult, ["Emag", "Esc"], ["Eim"])
        ar = Ere[:, :, 16]
        ai = Eim[:, :, 16]
        dv(lambda: V.tensor_scalar(out=s16[0][:], in0=ar, scalar1=-1.0, scalar2=None, op0=ALU.add), ["Ere"], ["s16_0"])
        tt_(s16[1][:], s16[0][:], lr[:], ALU.mult, ["s16_0", "lr"], ["s16_1"])
        tt_(s16[2][:], ai, lim[:], ALU.mult, ["Eim", "lim"], ["s16_2"])
        tt_(s16[1][:], s16[1][:], s16[2][:], ALU.add, ["s16_1", "s16_2"], ["s16_1"])
        tt_(fr[:], s16[1][:], rdn[:], ALU.mult, ["s16_1", "rdnP"], ["fr"])
        tt_(s16[2][:], ai, lr[:], ALU.mult, ["Eim", "lr"], ["s16_2"])
        tt_(s16[3][:], s16[0][:], lim[:], ALU.mult, ["s16_0", "lim"], ["s16_3"])
        tt_(s16[2][:], s16[2][:], s16[3][:], ALU.subtract, ["s16_2", "s16_3"], ["s16_2"])
        tt_(fi[:], s16[2][:], rdn[:], ALU.mult, ["s16_2", "rdnP"], ["fi"])
        frb = fr[:, :].unsqueeze(2).to_broadcast([128, 16, 16])
        fib = fi[:, :].unsqueeze(2).to_broadcast([128, 16, 16])
        tt_(b16[0][:], bre[:], frb, ALU.mult, ["bre", "fr"], ["b16_0"])
        tt_(b16[1][:], bim[:], fib, ALU.mult, ["bim", "fi"], ["b16_1"])
        tt_(Bre[:], b16[0][:], b16[1][:], ALU.subtract, ["b16_0", "b16_1"], ["Bre"])
        tt_(b16[0][:], bim[:], frb, ALU.mult, ["bim", "fr"], ["b16_0"])
        tt_(b16[1][:], bre[:], fib, ALU.mult, ["bre", "fi"], ["b16_1"])
        tt_(Bim[:], b16[0][:], b16[1][:], ALU.add, ["b16_0", "b16_1"], ["Bim"])

        def cprod(Er, Ei, Xr, Xi, out_re, out_im, neg_im, rn, wn):
            Erb = Er.unsqueeze(3).to_broadcast([128, 16, 8, 16])
            Eib = Ei.unsqueeze(3).to_broadcast([128, 16, 8, 16])
            Xrb = Xr.unsqueeze(2).to_broadcast([128, 16, 8, 16])
            Xib = Xi.unsqueeze(2).to_broadcast([128, 16, 8, 16])
            o_re = out_re.rearrange("p q (k h) -> p q k h", k=8)
            o_im = out_im.rearrange("p q (k h) -> p q k h", k=8)
            tt_(t1[:], Erb, Xrb, ALU.mult, rn, ["t1P"], 2.3)
            tt_(t2[:], Eib, Xib, ALU.mult, rn, ["t2P"], 2.3)
            tt_(o_re, t1[:], t2[:], ALU.subtract, ["t1P", "t2P"], [wn], 2.3)
            tt_(t1[:], Erb, Xib, ALU.mult, rn, ["t1P"], 2.3)
            tt_(t2[:], Eib, Xrb, ALU.mult, rn, ["t2P"], 2.3)
            if neg_im:
                dv(lambda: V.scalar_tensor_tensor(out=o_im, in0=t1[:], scalar=-1.0, in1=t2[:], op0=ALU.mult, op1=ALU.subtract), ["t1P", "t2P"], [wn], 2.3)
            else:
                tt_(o_im, t1[:], t2[:], ALU.add, ["t1P", "t2P"], [wn], 2.3)

        cprod(Ere[:, :, 0:8], Eim[:, :, 0:8], Bre[:], Bim[:], WSt_b[:, :, 0, :], WSt_b[:, :, 1, :], False, ["Ere", "Eim", "Bre", "Bim"], "WSt_b")
        cprod(Ere[:, :, 8:16], Eim[:, :, 8:16], cre[:], cim[:], CN_b[:, :, 0, :], CN_b[:, :, 1, :], True, ["Ere", "Eim", "cre", "cim"], "CN_b")
        cprod(Ere[:, :, 16:24], Eim[:, :, 16:24], cre[:], cim[:], CA_b[:, :, 0, :], CA_b[:, :, 1, :], True, ["Ere", "Eim", "cre", "cim"], "CA_b")
        dv(lambda: V.tensor_copy(out=Amag[:], in_=Emag[:, :, 23]), ["Emag"], ["Amag"])
        dv(lambda: V.tensor_scalar(out=s16[0][:], in0=th[:], scalar1=float(8.0 / TWO_PI), scalar2=None, op0=ALU.mult), ["th"], ["s16_0"])
        dv(lambda: V.tensor_copy(out=ski[:], in_=s16[0][:]), ["s16_0"], ["ski"])
        dv(lambda: V.tensor_copy(out=s16[1][:], in_=ski[:]), ["ski"], ["s16_1"])
        tt_(phi[:], s16[0][:], s16[1][:], ALU.subtract, ["s16_0", "s16_1"], ["phi"])
        P.defer_end()

    stA_ps = ExitStack()
    with ExitStack() as stA:
        NXB = 3
        x_t = [sb(f"x_t{i}", [128, D], F32, stA, side="right") for i in range(NXB)]
        xn = [sb(f"xn{i}", [128, D], BF16, stA, side="right") for i in range(2)]
        junk = sb("junkA", [128, D], BF16, stA, side="right")
        ssq = sb("ssqA", [128, NT], F32, stA, side="right")
        rstd = sb("rstdA", [128, NT], F32, stA, side="right")
        tp_ps = [ps(f"tp_ps{i}", [128, 8, 128], BF16, stA_ps) for i in range(2)]
        qkv_ps = [ps(f"qkv_ps{i}", [128, 1024], F32, stA_ps) for i in range(2)]
        tq_ps = [ps(f"tq_ps{i}", [128, 8, 128], BF16, stA_ps) for i in range(2)]
        qk_rot = [sb(f"qk_rot{i}", [128, 768], BF16, stA, side="right") for i in range(2)]
        rt = [sb(f"rope_t{i}", [128, 10, 8], F32, stA, side="right") for i in range(4)]
        semx = [P.new_sem(f"x{i}") for i in range(NXB)]

        def p0(n):
            s3 = n % NXB
            P.op("sp", lambda: nc.sync.dma_start(out=x_t[s3][:], in_=x_d[n * 128:(n + 1) * 128, :]),
                 writes=[("x_t", s3)], dma_sem=semx[s3])
            P.op("act", lambda: S.activation(out=junk[:], in_=x_t[s3][:], func=AF.Square, accum_out=ssq[:, n:n + 1]),
                 reads=[("x_t", s3)], writes=["junkA", ("ssqA", n)])

        def p0b(n):
            P.op("pool", lambda: G.tensor_scalar(out=rstd[:, n:n + 1], in0=ssq[:, n:n + 1], scalar1=float(1.0 / D), scalar2=float(EPS),
                                                  op0=ALU.mult, op1=ALU.add), reads=[("ssqA", n)], writes=[("rstdA", n)])
            P.op("pool", lambda: G.tensor_tensor(out=rstd[:, n:n + 1], in0=rstd[:, n:n + 1], in1=nhalf[:, 0:1], op=ALU.pow),
                 reads=[("rstdA", n), "nhalf"], writes=[("rstdA", n)])

        def p1(n):
            s3 = n % NXB
            s2 = n % 2
            P.op("act", lambda: S.activation(out=xn[s2][:], in_=x_t[s3][:], func=AF.Copy, scale=rstd[:, n:n + 1]),
                 reads=[("x_t", s3), ("rstdA", n)], writes=[("xn", s2)])

        def p1b(n):
            s2 = n % 2
            for k in range(8):
                P.op("pe", lambda: PE_.transpose(out=tp_ps[s2][:, k, :], in_=xn[s2][:, k * 128:(k + 1) * 128], identity=ident_b[:]),
                     reads=[("xn", s2), "ident_b"], writes=[("tp_ps", s2)])

        def p2(n):
            s2 = n % 2
            P.op("dve", lambda: V.tensor_tensor(out=hnT[:, :, n * 128:(n + 1) * 128], in0=tp_ps[s2][:],
                                                 in1=gpre[:, :].unsqueeze(2).to_broadcast([128, 8, 128]), op=ALU.mult),
                 reads=[("tp_ps", s2), "gpre"], writes=[("hnT", n)])

        def p3(n):
            s2 = n % 2
            tl = slice(n * 128, (n + 1) * 128)
            for k in range(8):
                P.op("pe", lambda: PE_.matmul(qkv_ps[s2][:, 0:512], lhsT=hnT[:, k, tl], rhs=win_b[:, k, 512:1024], start=(k == 0), stop=(k == 7)),
                     reads=[("hnT", n), ("win_b", k)], writes=[("qkv_ps", s2)])
                P.op("pe", lambda: PE_.matmul(qkv_ps[s2][:, 512:768], lhsT=hnT[:, k, tl], rhs=win_b[:, k, 1024:1280], start=(k == 0), stop=(k == 7)),
                     reads=[("hnT", n), ("win_b", k)], writes=[("qkv_ps", s2)])

        def p4(n):
            s2 = n % 2
            qk = qkv_ps[s2][:, 0:640].rearrange("p (h d) -> p h d", h=10)
            qr = qk_rot[s2][:, 0:640].rearrange("p (h d) -> p h d", h=10)
            cosb = cs[:, 1, n, :].unsqueeze(1).to_broadcast([128, 10, 8])
            sinb = cs[:, 0, n, :].unsqueeze(1).to_broadcast([128, 10, 8])
            rd = [("qkv_ps", s2), "cs"]
            P.op("dve", lambda: V.tensor_tensor(out=rt[0][:], in0=qk[:, :, 0:8], in1=cosb, op=ALU.mult), reads=rd, writes=["rt0"])
            P.op("dve", lambda: V.tensor_tensor(out=rt[1][:], in0=qk[:, :, 8:16], in1=sinb, op=ALU.mult), reads=rd, writes=["rt1"])
            P.op("dve", lambda: V.tensor_tensor(out=rt[2][:], in0=qk[:, :, 8:16], in1=cosb, op=ALU.mult), reads=rd, writes=["rt2"])
            P.op("dve", lambda: V.tensor_tensor(out=rt[3][:], in0=qk[:, :, 0:8], in1=sinb, op=ALU.mult), reads=rd, writes=["rt3"])
            P.op("dve", lambda: V.tensor_tensor(out=qr[:, :, 0:8], in0=rt[0][:], in1=rt[1][:], op=ALU.subtract),
                 reads=["rt0", "rt1"], writes=[("qk_rotA", s2)])
            P.op("dve", lambda: V.tensor_tensor(out=qr[:, :, 8:16], in0=rt[2][:], in1=rt[3][:], op=ALU.add),
                 reads=["rt2", "rt3"], writes=[("qk_rotB", s2)])
            P.op("act", lambda: S.activation(out=qr[:, :, 16:64], in_=qk[:, :, 16:64], func=AF.Copy),
                 reads=[("qkv_ps", s2), "rt0", "rt1", "rt2", "rt3"], writes=[("qk_rotC", s2)])
            P.op("act", lambda: S.activation(out=qk_rot[s2][:, 640:704], in_=qk_rot[s2][:, 512:576], func=AF.Copy),
                 reads=[("qk_rotA", s2), ("qk_rotB", s2), ("qk_rotC", s2)], writes=[("qk_rotD", s2)])
            P.op("act", lambda: S.activation(out=Vsb[:, n, :, 0:64], in_=qkv_ps[s2][:, 640:768].rearrange("p (h d) -> p h d", h=2), func=AF.Copy),
                 reads=[("qkv_ps", s2), "rt0", "rt1", "rt2", "rt3"], writes=["Vsb"])

        def p5a(n):
            s2 = n % 2
            tl = slice(n * 128, (n + 1) * 128)
            rdq = [("qk_rotA", s2), ("qk_rotB", s2), ("qk_rotC", s2), ("qk_rotD", s2), "ident_b"]
            for c in range(4):
                P.op("pe", lambda: PE_.transpose(out=tq_ps[s2][:, c, :], in_=qk_rot[s2][:, c * 128:(c + 1) * 128], identity=ident_b[:]),
                     reads=rdq, writes=[("tq_ps", s2)])
            P.op("pe", lambda: PE_.transpose(out=tq_ps[s2][:, 4, :], in_=qk_rot[s2][:, 512:640], identity=ident_b[:]),
                 reads=rdq, writes=[("tq_ps", s2)])
            P.op("pe", lambda: PE_.transpose(out=tq_ps[s2][:, 5, :], in_=qk_rot[s2][:, 576:704], identity=ident_b[:]),
                 reads=rdq, writes=[("tq_ps", s2)])

        def p5b(n):
            s2 = n % 2
            tl = slice(n * 128, (n + 1) * 128)
            P.op("act", lambda: S.activation(out=qkT[:, :, tl], in_=tq_ps[s2][:, 0:6, :], func=AF.Copy), reads=[("tq_ps", s2)], writes=[("qkT", n)])

        _pipeline(NT, [p0, p0b, p1, p1b, p2, p3, p4, p5a, p5b], hook=lambda it: P.flush(2.4))

        def u0(it):
            H, i = it // 8, it % 8
            s2 = it % 2
            for k in range(8):
                P.op("pe", lambda: PE_.matmul(qkv_ps[s2][:, 0:512], lhsT=hnT[:, k, H * 1024 + i:(H + 1) * 1024:8], rhs=win_b[:, k, 0:512],
                                              start=(k == 0), stop=(k == 7)),
                     reads=[("hnT", n2) for n2 in range(H * 8, H * 8 + 8)] + [("win_u", k)], writes=[("qkv_ps", s2)])

        def u1(it):
            H, i = it // 8, it % 8
            s2 = it % 2
            P.op("act", lambda: S.activation(out=u_tm2[:, H, :, i, :], in_=qkv_ps[s2][:, 0:512].rearrange("p (g h) -> p g h", g=32), func=AF.Copy),
                 reads=[("qkv_ps", s2)], writes=[("u_tm2", H)])

        _pipeline(16, [u0, u1], hook=lambda it: P.flush(2.6))
        P.flush()
        P.barrier()
    dump("hnT", hnT[:], [128, 8, T], BF16)
    stA_ps.close()
    with ExitStack() as stPP:
        T_ps = ps("T_ps", [128, 2, 4, 128], F32, stPP)
        W_ps = ps("W_ps", [128, 2, 8, 2, 64], BF16, stPP)
        for q0 in range(0, 16, 4):
            for ql in range(4):
                q = q0 + ql
                for e in range(2):
                    pp = slice(e * 64, (e + 1) * 64)
                    P.op("pe", lambda: PE_.matmul(T_ps[:, e, ql, :], lhsT=WSt_b[pp, q, 0, :], rhs=CN_b[pp, q, 0, :], start=True, stop=False),
                         reads=["WSt_b", "CN_b"], writes=["T_ps"])
                    P.op("pe", lambda: PE_.matmul(T_ps[:, e, ql, :], lhsT=WSt_b[pp, q, 1, :], rhs=CN_b[pp, q, 1, :], start=False, stop=True),
                         reads=["WSt_b", "CN_b"], writes=["T_ps"])
            for e in range(2):
                tt_(tmpT[:], T_ps[:, e], tmask[:, :].unsqueeze(1).to_broadcast([128, 4, 128]), ALU.mult, ["T_ps", "tmask"], ["tmpT"])
                for ql in range(4):
                    q = q0 + ql
                    g = 2 * q + e
                    dv(lambda: V.scalar_tensor_tensor(out=Toep_b[:, q, e, :], in0=ident_f[:], scalar=dl[:, g:g + 1], in1=tmpT[:, ql, :],
                                                      op0=ALU.mult, op1=ALU.add), ["tmpT", "ident_f", "dl"], ["Toep_b"])
        for q0 in range(0, 16, 8):
            for ql in range(8):
                q = q0 + ql
                for ri in range(2):
                    for e in range(2):
                        pp = slice(e * 64, (e + 1) * 64)
                        P.op("pe", lambda: PE_.transpose(out=W_ps[:, e, ql, ri, :], in_=WSt_b[pp, q, ri, :], identity=ident_b[pp, pp]),
                             reads=["WSt_b", "ident_b"], writes=["W_ps"])
            for e in range(2):
                P.op("act", lambda: S.activation(out=WS_b[:, q0:q0 + 8, e, :, :], in_=W_ps[:, e], func=AF.Copy), reads=["W_ps"], writes=["WS_b"])
        P.barrier()
    stP.close()
    dump("qkT", qkT[:], [128, 6, T], BF16)
    dump("Vsb", Vsb[:], [128, NT, 2, 65], BF16)
    dump("u_tm2", u_tm2[:], [128, 2, 32, 8, 16], BF16)
    stAB.close()

    stD = ExitStack()
    wglu_b = sb("wglu_b", [128, 4, 1024], BF16, stD)
    bglu = sb("bglu", [128, 1024], F32, stD)
    gssm = sb("gssm", [128, 512], F32, stD)
    sem_pd = P.new_sem("parD")
    sem_pdw = P.new_sem("parDw")

    def ploadD(dst, src, name):
        P.op("sp", lambda: nc.sync.dma_start(out=dst, in_=src), writes=[name], dma_sem=sem_pd, group=True)

    for k in range(4):
        P.op("pool", lambda: nc.gpsimd.dma_start(out=wglu_b[:, k, :], in_=wglu_d[:, k, :]), writes=[("wglu_b", k)], dma_sem=sem_pdw, group=True)
    ploadD(bglu[:], bglu_d, "bglu")
    ploadD(gssm[:], gssm_d, "gssm")

    stYT = ExitStack()
    yT = sb("yT", [128, 4, T], BF16, stYT)

    stTab = ExitStack()
    tab = sb("tab", [128, 2, 16, 256], F32, stTab)
    stTT = ExitStack()
    pki = sb("pki", [128, 2, 2, 256], I32, stTT)
    pkf = sb("pkf", [128, 2, 2, 256], F32, stTT)

    def tab_round(q0):
        tq = tab[:, :, q0:q0 + 2, :]
        rg = ("tab", q0)
        P.op("dve", lambda: V.tensor_tensor(out=tab[:, 0, q0:q0 + 2, :], in0=phi[:, q0:q0 + 2].unsqueeze(2).to_broadcast([128, 2, 256]),
                                             in1=cidx[:, :].unsqueeze(1).to_broadcast([128, 2, 256]), op=ALU.mult), reads=["phi", "cidx"], writes=[rg], cost=0.8)
        P.op("dve", lambda: V.tensor_scalar(out=tab[:, 1, q0:q0 + 2, :], in0=tab[:, 0, q0:q0 + 2, :], scalar1=0.25, scalar2=None, op0=ALU.add), reads=[rg], writes=[rg], cost=0.8)
        P.op("dve", lambda: V.tensor_copy(out=pki[:], in_=tq), reads=[rg], writes=["pki"], cost=1.1)
        P.op("dve", lambda: V.tensor_copy(out=pkf[:], in_=pki[:]), reads=["pki"], writes=["pkf"], cost=1.1)
        P.op("dve", lambda: V.tensor_tensor(out=tq, in0=tq, in1=pkf[:], op=ALU.subtract), reads=[rg, "pkf"], writes=[rg], cost=0.8)

    P.defer_begin()
    for q0_ in range(0, 16, 2):
        tab_round(q0_)
    P.defer_end()

    with ExitStack() as stC:
        st_ps = [ps(f"st_ps{i}", [128, 2, 2, 2, 128], F32, stC) for i in range(2)]
        o_ps = ps("o_ps", [128, 2, 512], F32, stC)
        ta_ps = [ps(f"ta_ps{i}", [128, 8, 128], BF16, stC) for i in range(2)]
        p_sb = [sb(f"p_sb{i}", [128, 2, 2, 2, 128], BF16, stC) for i in range(2)]
        den = sb("denC", [128, 8], F32, stC)
        rden = sb("rdenC", [128, NT, 8], F32, stC)
        ya = [sb(f"yaC{i}", [128, 512], F32, stC) for i in range(5)]
        yab = [sb(f"yabC{i}", [128, 512], BF16, stC) for i in range(2)]
        junkC = sb("junkC", [128, 512], BF16, stC)
        ssqC = sb("ssqC", [128, NT], F32, stC)
        rstdC = sb("rstdC", [128, NT], F32, stC)

        def c_scores(s_):
            n, Gk = s_ // 2, s_ % 2
            slot = s_ % 2
            tl = slice(n * 128, (n + 1) * 128)
            tp = slice((n - 1) * 128, n * 128)
            for hh in range(4):
                h = 4 * Gk + hh
                b0 = (h % 2) * 64
                kc = 4 if Gk == (h % 2) else 5
                first = (hh < 2)
                P.op("pe", lambda: PE_.matmul(st_ps[slot][:, hh % 2, 0, hh // 2, :], lhsT=qkT[b0:b0 + 64, kc, tl], rhs=qkT[b0:b0 + 64, h // 2, tl], start=first, stop=False,
                                              skip_group_check=True),
                     reads=[("qkT", n)], writes=[("st_ps", slot)])
                if n > 0:
                    P.op("pe", lambda: PE_.matmul(st_ps[slot][:, hh % 2, 1, hh // 2, :], lhsT=qkT[b0:b0 + 64, kc, tp], rhs=qkT[b0:b0 + 64, h // 2, tl], start=False, stop=False,
                                                  skip_group_check=True),
                         reads=[("qkT", n), ("qkT", n - 1)], writes=[("st_ps", slot)])
            for par in range(2):
                P.op("pe", lambda: PE_.matmul(st_ps[slot][:, par].rearrange("p c r q -> p (c r q)"), lhsT=ident_b[:], rhs=amask[:, par].rearrange("p c r q -> p (c r q)"),
                                              start=False, stop=True, skip_group_check=True),
                     reads=["amask", "ident_b"], writes=[("st_ps", slot)])

        def c_exp(s_):
            n, Gk = s_ // 2, s_ % 2
            slot = s_ % 2
            ncur = 1 if n == 0 else 2
            P.op("act", lambda: S.activation(out=p_sb[slot][:, :, 0:ncur], in_=st_ps[slot][:, :, 0:ncur], func=AF.Exp, scale=0.125),
                 reads=[("st_ps", slot)], writes=[("p_sb", slot)])

        def c_pv(s_):
            n, Gk = s_ // 2, s_ % 2
            slot = s_ % 2
            for hh in range(4):
                oo = o_ps[:, Gk, hh * 65:(hh + 1) * 65]
                P.op("pe", lambda: PE_.matmul(oo, lhsT=p_sb[slot][:, hh % 2, 0, hh // 2, :], rhs=Vsb[:, n, Gk, :], start=True, stop=(n == 0)),
                     reads=[("p_sb", slot), "Vsb"], writes=["o_ps"])
                if n > 0:
                    P.op("pe", lambda: PE_.matmul(oo, lhsT=p_sb[slot][:, hh % 2, 1, hh // 2, :], rhs=Vsb[:, n - 1, Gk, :], start=False, stop=True),
                         reads=[("p_sb", slot), "Vsb"], writes=["o_ps"])
            if Gk == 1:
                y5 = n % 5
                o4 = o_ps[:, :, 0:260].rearrange("p g (h d) -> p g h d", h=4)
                P.op("dve", lambda: V.tensor_tensor(out=den[:, :].rearrange("p (g h) -> p g h", g=2), in0=o4[:, :, :, 64],
                                                     in1=sinkexp[:, :].rearrange("p (g h) -> p g h", g=2), op=ALU.add),
                     reads=["o_ps", "sinkexp"], writes=["denC"])
                P.op("dve", lambda: V.reciprocal(out=rden[:, n, :], in_=den[:]), reads=["denC"], writes=[("rdenC", n)])
                P.op("dve", lambda: V.tensor_tensor(out=ya[y5][:, :].rearrange("p (g h d) -> p g h d", g=2, h=4), in0=o4[:, :, :, 0:64],
                                                     in1=rden[:, n, :].rearrange("p (g h) -> p g h", g=2).unsqueeze(3).to_broadcast([128, 2, 4, 64]), op=ALU.mult),
                     reads=["o_ps", ("rdenC", n)], writes=[("yaC", y5)])

        def odd(fn):
            def w(s_):
                if s_ % 2 == 1:
                    fn(s_ // 2)
            return w

        def c_sq(n):
            P.op("act", lambda: S.activation(out=junkC[:], in_=ya[n % 5][:], func=AF.Square, accum_out=ssqC[:, n:n + 1]),
                 reads=[("yaC", n % 5)], writes=["junkC", ("ssqC", n)])

        def c_r1(n):
            P.op("dve", lambda: V.tensor_scalar(out=rstdC[:, n:n + 1], in0=ssqC[:, n:n + 1], scalar1=float(1.0 / 512), scalar2=float(EPS),
                                                 op0=ALU.mult, op1=ALU.add), reads=[("ssqC", n)], writes=[("rstdC", n)])

        def c_r2(n):
            P.op("pool", lambda: G.tensor_tensor(out=rstdC[:, n:n + 1], in0=rstdC[:, n:n + 1], in1=nhalf[:, 0:1], op=ALU.pow),
                 reads=[("rstdC", n), "nhalf"], writes=[("rstdC", n)])

        def c_n(n):
            P.op("dve", lambda: V.scalar_tensor_tensor(out=yab[n % 2][:], in0=ya[n % 5][:], scalar=rstdC[:, n:n + 1], in1=gattn[:], op0=ALU.mult, op1=ALU.mult),
                 reads=[("yaC", n % 5), ("rstdC", n), "gattn"], writes=[("yabC", n % 2)])

        def c_t(n):
            y2 = n % 2
            for c in range(4):
                P.op("pe", lambda: PE_.transpose(out=ta_ps[y2][:, c, :], in_=yab[y2][:, c * 128:(c + 1) * 128], identity=ident_b[:]),
                     reads=[("yabC", y2), "ident_b"], writes=[("ta_ps", y2)])

        def c_e(n):
            y2 = n % 2
            tl = slice(n * 128, (n + 1) * 128)
            P.op("dve", lambda: V.tensor_copy(out=mergedT[:, 4:8, tl], in_=ta_ps[y2][:, 0:4, :]), reads=[("ta_ps", y2)], writes=[("mergedT_a", n)])

        _pipeline(2 * NT, [c_scores, c_exp, c_pv, odd(c_sq), odd(c_r1), odd(c_r2), odd(c_n), odd(c_t), odd(c_e)], hook=lambda it: P.flush(1.4))
        P.flush()
        for hh_ in range(2):
            P.op("act", lambda: S.activation(out=tab[:, hh_], in_=tab[:, hh_], func=AF.Sin, scale=float(TWO_PI)),
                 reads=[("tab", q) for q in range(0, 16, 2)], writes=["tab"])
        P.barrier()
    dump("mergedT", mergedT[:], [128, 8, T], BF16)
    stTT.close()
    stBC.close()

    dump("Toep_b", Toep_b[:], [128, 16, 2, 128], BF16)
    dump("WS_b", WS_b[:], [128, 16, 2, 2, 64], BF16)
    dump("CA_b", CA_b[:], [128, 16, 2, 128], BF16)
    dump("tab", tab[:], [128, 2, 16, 256])
    dump("Amag", Amag[:], [128, 16])
    stD2 = ExitStack()
    U8 = sb("U8", [128, 32, 256], BF16, stD2)
    with ExitStack() as stU8:
        tu_ps = [ps(f"tu_ps{i}", [128, 8, 128], BF16, stU8) for i in range(2)]
        cnt = 0
        for H in range(2):
            for g0 in range(0, 32, 8):
                sl = cnt % 2
                cnt += 1
                for gl in range(8):
                    g = g0 + gl
                    P.op("pe", lambda: PE_.transpose(out=tu_ps[sl][:, gl, :], in_=u_tm2[:, H, g, :, :].rearrange("p i h -> p (i h)"), identity=ident_b[:]),
                         reads=[("u_tm2", H), "ident_b"], writes=[("tu_ps", sl)])
                P.op("act", lambda: S.activation(out=U8[:, g0:g0 + 8, H * 128:(H + 1) * 128], in_=tu_ps[sl][:], func=AF.Copy),
                     reads=[("tu_ps", sl)], writes=["U8"])
        P.barrier()
    dump("U8", U8[:], [128, 32, 256], BF16)
    stU.close()
    stEW = ExitStack()
    wout_b = sbr("wout_b", [128, 8, 1024], BF16, stEW)
    gpm = sbr("gpm", [128, D], F32, stEW)
    gpf = sbr("gpf", [128, D], F32, stEW)
    sem_pe_ = P.new_sem("parE")
    sem_pew = P.new_sem("parEw")
    P.op("sp", lambda: nc.sync.dma_start(out=gpm[:], in_=gpm_d), writes=["gpm"], dma_sem=sem_pe_, group=True)
    P.op("sp", lambda: nc.sync.dma_start(out=gpf[:], in_=gpf_d), writes=["gpf"], dma_sem=sem_pe_, group=True)
    for k in range(8):
        P.op("pool", lambda: nc.gpsimd.dma_start(out=wout_b[:, k, :], in_=wout_d[:, k, :]), writes=[("wout_b", k)], dma_sem=sem_pew, group=True)

    with ExitStack() as stS:
        S_ps = [ps(f"S_ps{i}", [128, 2, 256], F32, stS) for i in range(2)]
        Y_ps = [ps(f"Y_ps{i}", [128, 2, 4, 128], F32, stS) for i in range(2)]
        ty_ps = [ps(f"ty_ps{i}", [128, 8, 128], BF16, stS) for i in range(2)]

        def mk(name, dt=F32):
            return [sb(f"{name}{i}", [128, 2, 256], dt, stS) for i in range(2)]

        tA, tB, Sm, Xt, dA, dB = mk("tA"), mk("tB"), mk("Sm"), mk("Xt"), mk("dA"), mk("dB")
        Xprev = mk("Xprev", BF16)
        y_tm2b = [sb(f"y_tm2b{i}", [128, 2, 8, 128], BF16, stS) for i in range(2)]
        for i in range(2):
            P.op("dve", lambda: V.memset(Xprev[i][:], 0.0), writes=[("Xprev", i)])

        def tabs(q):
            return (tab[:, 1, q, :].unsqueeze(1).to_broadcast([128, 2, 256]), tab[:, 0, q, :].unsqueeze(1).to_broadcast([128, 2, 256]))

        def s0(q):
            sl = q % 2
            for ri in range(2):
                for e in range(2):
                    g = 2 * q + e
                    P.op("pe", lambda: PE_.matmul(S_ps[sl][e * 64:(e + 1) * 64, ri, :], lhsT=WS_b[:, q, e, ri, :], rhs=U8[:, g, :], start=True, stop=True),
                         reads=["WS_b", "U8"], writes=[("S_ps", sl)])

        def s1(q):
            sl = q % 2
            cosb, sinb = tabs(q)
            P.op("dve", lambda: V.tensor_tensor(out=tA[sl][:], in0=S_ps[sl][:], in1=cosb, op=ALU.mult), reads=[("S_ps", sl), "tab"], writes=[("tA", sl)])
            P.op("dve", lambda: V.tensor_tensor(out=tB[sl][:], in0=S_ps[sl][:], in1=sinb, op=ALU.mult), reads=[("S_ps", sl), "tab"], writes=[("tB", sl)])

        def s2(q):
            sl = q % 2
            P.op("dve", lambda: V.tensor_tensor(out=Sm[sl][:, 0], in0=tA[sl][:, 0], in1=tB[sl][:, 1], op=ALU.add), reads=[("tA", sl), ("tB", sl)], writes=[("Sm", sl)])
            P.op("dve", lambda: V.tensor_tensor(out=Sm[sl][:, 1], in0=tA[sl][:, 1], in1=tB[sl][:, 0], op=ALU.subtract), reads=[("tA", sl), ("tB", sl)], writes=[("Sm", sl)])

        def s3(q):
            sl = q % 2
            for ri in range(2):
                P.op("dve", lambda: V.tensor_tensor_scan(out=Xt[sl][:, ri, :], data0=Amag[:, q:q + 1].to_broadcast([128, 256]), data1=Sm[sl][:, ri, :],
                                                         initial=0.0, op0=ALU.mult, op1=ALU.add), reads=[("Sm", sl), "Amag"], writes=[("Xt", sl)])

        def s4(q):
            sl = q % 2
            cosb, sinb = tabs(q)
            P.op("dve", lambda: V.tensor_tensor(out=dA[sl][:], in0=Xt[sl][:], in1=cosb, op=ALU.mult), reads=[("Xt", sl), "tab"], writes=[("dA", sl)])
            P.op("dve", lambda: V.tensor_tensor(out=dB[sl][:], in0=Xt[sl][:], in1=sinb, op=ALU.mult), reads=[("Xt", sl), "tab"], writes=[("dB", sl)])

        def s5(q):
            sl = q % 2
            P.op("dve", lambda: V.tensor_tensor(out=Xprev[sl][:, 0, 1:256], in0=dA[sl][:, 0, 0:255], in1=dB[sl][:, 1, 0:255], op=ALU.subtract),
                 reads=[("dA", sl), ("dB", sl)], writes=[("Xprev", sl)])
            P.op("dve", lambda: V.tensor_tensor(out=Xprev[sl][:, 1, 1:256], in0=dA[sl][:, 1, 0:255], in1=dB[sl][:, 0, 0:255], op=ALU.add),
                 reads=[("dA", sl), ("dB", sl)], writes=[("Xprev", sl)])

        def s6(q):
            sl = q % 2
            bt = q // 4
            yb = y_tm2b[bt % 2]
            for H in range(2):
                hs = slice(H * 128, (H + 1) * 128)
                for e in range(2):
                    g = 2 * q + e
                    pp = slice(e * 64, (e + 1) * 64)
                    yo = Y_ps[sl][:, e, H, :]
                    P.op("pe", lambda: PE_.matmul(yo, lhsT=U8[:, g, hs], rhs=Toep_b[:, q, e, :], start=True, stop=False),
                         reads=["U8", "Toep_b"], writes=[("Y_ps", sl)])
                    P.op("pe", lambda: PE_.matmul(yo, lhsT=Xprev[sl][pp, 0, hs], rhs=CA_b[pp, q, 0, :], start=False, stop=False),
                         reads=[("Xprev", sl), "CA_b"], writes=[("Y_ps", sl)])
                    P.op("pe", lambda: PE_.matmul(yo, lhsT=Xprev[sl][pp, 1, hs], rhs=CA_b[pp, q, 1, :], start=False, stop=True),
                         reads=[("Xprev", sl), "CA_b"], writes=[("Y_ps", sl)])
            for e in range(2):
                c0 = (q % 4) * 32 + e * 16
                P.op("act", lambda: S.activation(out=yb[:, :, :, c0:c0 + 16], in_=Y_ps[sl][:, e, 0:2].rearrange("p H (j h) -> p H j h", j=8), func=AF.Gelu_apprx_tanh),
                     reads=[("Y_ps", sl)], writes=[("y_tm2b", bt % 2)])

        def s7(q):
            if q % 4 != 3:
                return
            bt = q // 4
            yb = y_tm2b[bt % 2]
            for H in range(2):
                sl = H
                for j in range(8):
                    P.op("pe", lambda: PE_.transpose(out=ty_ps[sl][:, j, :], in_=yb[:, H, j, :], identity=ident_b[:]),
                         reads=[("y_tm2b", bt % 2), "ident_b"], writes=[("ty_ps", sl)])
                P.op("act", lambda: S.activation(out=yT[:, bt, H * 1024:(H + 1) * 1024], in_=ty_ps[sl][:].rearrange("p j c -> p (j c)"), func=AF.Copy),
                     reads=[("ty_ps", sl)], writes=[("yT", H)])

        _pipeline(16, [s0, s1, s2, s3, s4, s5, s6, s7])
        P.barrier()
    dump("yT", yT[:], [128, 4, T], BF16)
    stD2.close()
    stTab.close()
    stEW2 = ExitStack()
    wdn_b = sbr("wdn_b", [128, NF, 1024], BF16, stEW2)
    sem_wdn = P.new_sem("wdn")
    for f in range(NF):
        P.op("pool", lambda: nc.gpsimd.dma_start(out=wdn_b[:, f, :], in_=wdn_d[:, f, :]), writes=[("wdn_b", f)], dma_sem=sem_wdn, group=True)

    with ExitStack() as stG:
        z_ps = [ps(f"z_ps{i}", [128, 1024], F32, stG) for i in range(2)]
        tm_ps = [ps(f"tm_ps{i}", [128, 8, 128], BF16, stG) for i in range(2)]
        ga = [sb(f"ga{i}", [128, 512], F32, stG) for i in range(3)]
        gb = [sb(f"gb{i}", [128, 512], F32, stG) for i in range(2)]
        sg = [sb(f"sg{i}", [128, 512], F32, stG) for i in range(2)]
        go = [sb(f"go{i}", [128, 512], F32, stG) for i in range(5)]
        gon = [sb(f"gon{i}", [128, 512], BF16, stG) for i in range(2)]
        junkG = sb("junkG", [128, 512], BF16, stG)
        ssqG = sb("ssqG", [128, NT], F32, stG)
        rstdG = sb("rstdG", [128, NT], F32, stG)

        def g_mm(tix):
            H, j = tix // 8, tix % 8
            sl = tix % 2
            cols = slice(H * 1024 + j * 128, H * 1024 + (j + 1) * 128)
            for k in range(4):
                P.op("pe", lambda: PE_.matmul(z_ps[sl][:, 0:512], lhsT=yT[:, k, cols], rhs=wglu_b[:, k, 0:512], start=(k == 0), stop=(k == 3)),
                     reads=[("yT", H), ("wglu_b", k)], writes=[("z_ps", sl)])
                P.op("pe", lambda: PE_.matmul(z_ps[sl][:, 512:1024], lhsT=yT[:, k, cols], rhs=wglu_b[:, k, 512:1024], start=(k == 0), stop=(k == 3)),
                     reads=[("yT", H), ("wglu_b", k)], writes=[("z_ps", sl)])

        def g_a(tix):
            sl = tix % 2
            P.op("dve", lambda: V.tensor_tensor(out=ga[tix % 3][:], in0=z_ps[sl][:, 0:512], in1=bglu[:, 0:512], op=ALU.add), reads=[("z_ps", sl), "bglu"], writes=[("ga", tix % 3)])
            P.op("dve", lambda: V.tensor_tensor(out=gb[tix % 2][:], in0=z_ps[sl][:, 512:1024], in1=bglu[:, 512:1024], op=ALU.add), reads=[("z_ps", sl), "bglu"], writes=[("gb", tix % 2)])

        def g_sig(tix):
            P.op("act", lambda: S.activation(out=sg[tix % 2][:], in_=gb[tix % 2][:], func=AF.Sigmoid), reads=[("gb", tix % 2)], writes=[("sg", tix % 2)])

        def g_mul(tix):
            r = tix % 5
            P.op("dve", lambda: V.tensor_tensor(out=go[r][:], in0=ga[tix % 3][:], in1=sg[tix % 2][:], op=ALU.mult), reads=[("ga", tix % 3), ("sg", tix % 2)], writes=[("go", r)])

        def g_sq(tix):
            r = tix % 5
            P.op("act", lambda: S.activation(out=junkG[:], in_=go[r][:], func=AF.Square, accum_out=ssqG[:, tix:tix + 1]), reads=[("go", r)], writes=["junkG", ("ssqG", tix)])

        def g_r1(tix):
            P.op("dve", lambda: V.tensor_scalar(out=rstdG[:, tix:tix + 1], in0=ssqG[:, tix:tix + 1], scalar1=float(1.0 / 512), scalar2=float(EPS),
                                                 op0=ALU.mult, op1=ALU.add), reads=[("ssqG", tix)], writes=[("rstdG", tix)])

        def g_r2(tix):
            P.op("pool", lambda: G.tensor_tensor(out=rstdG[:, tix:tix + 1], in0=rstdG[:, tix:tix + 1], in1=nhalf[:, 0:1], op=ALU.pow),
                 reads=[("rstdG", tix), "nhalf"], writes=[("rstdG", tix)])

        def g_n(tix):
            r = tix % 5
            P.op("dve", lambda: V.scalar_tensor_tensor(out=gon[tix % 2][:], in0=go[r][:], scalar=rstdG[:, tix:tix + 1], in1=gssm[:], op0=ALU.mult, op1=ALU.mult),
                 reads=[("go", r), ("rstdG", tix), "gssm"], writes=[("gon", tix % 2)])

        def g_t(tix):
            sl = tix % 2
            for c in range(4):
                P.op("pe", lambda: PE_.transpose(out=tm_ps[sl][:, c, :], in_=gon[tix % 2][:, c * 128:(c + 1) * 128], identity=ident_b[:]),
                     reads=[("gon", tix % 2), "ident_b"], writes=[("tm_ps", sl)])

        def g_d(tix):
            H, j = tix // 8, tix % 8
            sl = tix % 2
            P.op("act", lambda: S.activation(out=mergedT[:, 0:4, tix * 128:(tix + 1) * 128], in_=tm_ps[sl][:, 0:4, :], func=AF.Copy),
                 reads=[("tm_ps", sl)], writes=[("mergedT_s", tix)])

        _pipeline(NT, [g_mm, g_a, g_sig, g_mul, g_sq, g_r1, g_r2, g_n, g_t, g_d])
        P.barrier()
    dump("mergedT2", mergedT[:], [128, 8, T], BF16)
    stYT.close()
    stD.close()
    stPO.close()

    stE = ExitStack()
    wgu_s = [sb(f"wgu_s{i}", [128, 2, 8, 256], BF16, stE) for i in range(3)]
    h1 = sb("h1", [128, 5, D], F32, stE)
    hn2 = [sb(f"hn2_{i}", [128, D], BF16, stE) for i in range(2)]
    hn2T = sb("hn2T", [128, 8, 512], BF16, stE)
    actT = sb("actT", [128, NF, 512], BF16, stE)
    sgt = [sb(f"sgt{i}", [128, 512], F32, stE) for i in range(2)]
    junkE = sb("junkE", [128, D], BF16, stE)
    ot = [sb(f"ot{i}", [128, D], F32, stE) for i in range(2)]
    ssqE = sb("ssqE", [128, 3 * NT], F32, stE)
    rstdE = sb("rstdE", [128, 3 * NT], F32, stE)
    eps_ = ps("eps", [128, 7, 512], F32, stE)
    tp2_ps = ps("tp2_ps", [128, 8, 128], BF16, stE)

    def tok(j):
        return eps_[:, 2 * j:2 * j + 2, :].rearrange("p a c -> p (a c)"), [("bank", 2 * j), ("bank", 2 * j + 1)]

    def gub(i):
        return eps_[:, 4 + i, :], [("bank", 4 + i)]

    sem_wgu = [P.new_sem(f"wgu{i}") for i in range(3)]
    sem_h1 = [P.new_sem(f"h1_{i}") for i in range(5)]
    sem_out = [P.new_sem(f"outst{i}") for i in range(2)]
    wgu_cnt = [0]
    wgu_q = []

    def issue_wgu(fp):
        slot = wgu_cnt[0] % 3
        wgu_cnt[0] += 1
        P.op("pool", lambda: nc.gpsimd.dma_start(out=wgu_s[slot][:].rearrange("p a k c -> p (a k c)"), in_=wgu_d[fp]),
             writes=[("wgu_s", slot)], dma_sem=sem_wgu[slot])
        wgu_q.append(slot)

    def rstd_of(col, dim):
        P.op("dve", lambda: V.tensor_scalar(out=rstdE[:, col:col + 1], in0=ssqE[:, col:col + 1], scalar1=float(1.0 / dim), scalar2=float(EPS),
                                             op0=ALU.mult, op1=ALU.add), reads=[("ssqE", col)], writes=[("rstdE", col)])
        P.op("pool", lambda: G.tensor_tensor(out=rstdE[:, col:col + 1], in0=rstdE[:, col:col + 1], in1=nhalf[:, 0:1], op=ALU.pow),
             reads=[("rstdE", col), "nhalf"], writes=[("rstdE", col)])

    tok_rot = [0]
    tok_of = {}
    ot_rot = [0]
    ot_of = {}
    hn_rot = [0]
    hn_of = {}
    gu_rot = [0]
    sg_rot = [0]

    def a1(B, t):
        n = 4 * B + t
        hs = n % 5
        H, j = n // 8, n % 8
        tl = slice(n * 128, (n + 1) * 128)
        ta = slice(H * 1024 + j, (H + 1) * 1024, 8)
        j3 = tok_rot[0] % 3
        tok_rot[0] += 1
        tok_of[("a", n)] = j3
        mo, mreg = tok(j3)
        P.op("sp", lambda: nc.sync.dma_start(out=h1[:, hs, :], in_=x_d[ta, :]), writes=[("h1", hs)], dma_sem=sem_h1[hs])
        rda = [("mergedT_a", n2) for n2 in range(H * 8, H * 8 + 8)]
        for k in range(8):
            cols = tl if k < 4 else ta
            rd = ([("mergedT_s", n)] if k < 4 else rda) + [("wout_b", k)]
            P.op("pe", lambda: PE_.matmul(mo[:, 0:512], lhsT=mergedT[:, k, cols], rhs=wout_b[:, k, 0:512], start=(k == 0), stop=(k == 7)),
                 reads=rd, writes=mreg)
            P.op("pe", lambda: PE_.matmul(mo[:, 512:1024], lhsT=mergedT[:, k, cols], rhs=wout_b[:, k, 512:1024], start=(k == 0), stop=(k == 7)),
                 reads=rd, writes=mreg)

    def a2(B, t):
        n = 4 * B + t
        hs = n % 5
        mo, mreg = tok(tok_of[("a", n)])
        r = ot_rot[0] % 2
        ot_rot[0] += 1
        c1 = n
        P.op("act", lambda: S.activation(out=junkE[:], in_=mo, func=AF.Square, accum_out=ssqE[:, c1:c1 + 1]),
             reads=mreg, writes=["junkE", ("ssqE", c1)])
        rstd_of(c1, D)
        P.op("dve", lambda: V.scalar_tensor_tensor(out=ot[r][:], in0=mo, scalar=rstdE[:, c1:c1 + 1], in1=gpm[:], op0=ALU.mult, op1=ALU.mult),
             reads=mreg + [("rstdE", c1), "gpm"], writes=[("ot", r)])
        P.op("dve", lambda: V.tensor_tensor(out=h1[:, hs, :], in0=h1[:, hs, :], in1=ot[r][:], op=ALU.add), reads=[("h1", hs), ("ot", r)], writes=[("h1", hs)])

    def a3(B, t):
        n = 4 * B + t
        hs = n % 5
        r = hn_rot[0] % 2
        hn_rot[0] += 1
        hn_of[n] = r
        c2 = NT + n
        P.op("act", lambda: S.activation(out=junkE[:], in_=h1[:, hs, :], func=AF.Square, accum_out=ssqE[:, c2:c2 + 1]),
             reads=[("h1", hs)], writes=["junkE", ("ssqE", c2)])
        rstd_of(c2, D)
        P.op("act", lambda: S.activation(out=hn2[r][:], in_=h1[:, hs, :], func=AF.Copy, scale=rstdE[:, c2:c2 + 1]),
             reads=[("h1", hs), ("rstdE", c2)], writes=[("hn2", r)])

    def a4(B, t):
        n = 4 * B + t
        r = hn_of[n]
        for k in range(8):
            P.op("pe", lambda: PE_.transpose(out=tp2_ps[:, k, :], in_=hn2[r][:, k * 128:(k + 1) * 128], identity=ident_b[:]),
                 reads=[("hn2", r), "ident_b"], writes=["tp2_ps"])
        P.op("dve", lambda: V.tensor_tensor(out=hn2T[:, :, t * 128:(t + 1) * 128], in0=tp2_ps[:],
                                             in1=gpffn[:, :].unsqueeze(2).to_broadcast([128, 8, 128]), op=ALU.mult),
             reads=["tp2_ps", "gpffn"], writes=[("hn2T", t)])

    def b1(B, t):
        n = 4 * B + t
        j = tok_rot[0] % 3
        tok_rot[0] += 1
        tok_of[("b", n)] = j
        dn, dreg = tok(j)
        for f in range(NF):
            P.op("pe", lambda: PE_.matmul(dn[:, 0:512], lhsT=actT[:, f, t * 128:(t + 1) * 128], rhs=wdn_b[:, f, 0:512], start=(f == 0), stop=(f == NF - 1)),
                 reads=[("actT", f), ("wdn_b", f)], writes=dreg)
            P.op("pe", lambda: PE_.matmul(dn[:, 512:1024], lhsT=actT[:, f, t * 128:(t + 1) * 128], rhs=wdn_b[:, f, 512:1024], start=(f == 0), stop=(f == NF - 1)),
                 reads=[("actT", f), ("wdn_b", f)], writes=dreg)

    def b2(B, t):
        n = 4 * B + t
        hs = n % 5
        tl = slice((n // 8) * 1024 + (n % 8), (n // 8 + 1) * 1024, 8)
        dn, dreg = tok(tok_of[("b", n)])
        r = ot_rot[0] % 2
        ot_rot[0] += 1
        c3 = 2 * NT + n
        P.op("act", lambda: S.activation(out=junkE[:], in_=dn, func=AF.Square, accum_out=ssqE[:, c3:c3 + 1]),
             reads=dreg, writes=["junkE", ("ssqE", c3)])
        rstd_of(c3, D)
        P.op("dve", lambda: V.scalar_tensor_tensor(out=ot[r][:], in0=dn, scalar=rstdE[:, c3:c3 + 1], in1=gpf[:], op0=ALU.mult, op1=ALU.mult),
             reads=dreg + [("rstdE", c3), "gpf"], writes=[("ot", r)])
        P.op("dve", lambda: V.tensor_tensor(out=ot[r][:], in0=ot[r][:], in1=h1[:, hs, :], op=ALU.add), reads=[("h1", hs), ("ot", r)], writes=[("ot", r)])
        P.op("sp", lambda: nc.sync.dma_start(out=out_d[tl, :], in_=ot[r][:]), reads=[("ot", r)], writes=[("out", n)], dma_sem=sem_out[r])

    def e2(B):
        for fp in range(11):
            if fp + 2 < 11:
                issue_wgu(fp + 2)
            elif B < 3:
                issue_wgu(fp + 2 - 11)
            slot = wgu_q.pop(0)
            for fl in range(2):
                f = 2 * fp + fl
                gi = gu_rot[0] % 3
                ui = (gu_rot[0] + 1) % 3
                gu_rot[0] += 2
                gps, greg = gub(gi)
                ups, ureg = gub(ui)
                for k in range(8):
                    P.op("pe", lambda: PE_.matmul(gps, lhsT=wgu_s[slot][:, fl, k, 0:128], rhs=hn2T[:, k, :], start=(k == 0), stop=(k == 7)),
                         reads=[("wgu_s", slot)] + [("hn2T", t) for t in range(4)], writes=greg)
                for k in range(8):
                    P.op("pe", lambda: PE_.matmul(ups, lhsT=wgu_s[slot][:, fl, k, 128:256], rhs=hn2T[:, k, :], start=(k == 0), stop=(k == 7)),
                         reads=[("wgu_s", slot)] + [("hn2T", t) for t in range(4)], writes=ureg)
                si = sg_rot[0] % 2
                sg_rot[0] += 1
                P.op("act", lambda: S.activation(out=sgt[si][:], in_=gps, func=AF.Silu), reads=greg, writes=[("sgt", si)])
                P.op("dve", lambda: V.tensor_tensor(out=actT[:, f, :], in0=ups, in1=sgt[si][:], op=ALU.mult),
                     reads=ureg + [("sgt", si)], writes=[("actT", f)])

    issue_wgu(0)
    issue_wgu(1)
    _pipeline(4, [lambda t: a1(0, t), lambda t: a2(0, t), lambda t: a3(0, t), lambda t: a4(0, t)])
    for B in range(4):
        e2(B)
        if B < 3:
            for it in range(6):
                if 1 <= it < 5:
                    b2(B, it - 1)
                if it < 4:
                    a1(B + 1, it)
                    b1(B, it)
                    a2(B + 1, it)
                if 1 <= it < 5:
                    a3(B + 1, it - 1)
                if 2 <= it < 6:
                    a4(B + 1, it - 2)
        else:
            _pipeline(4, [lambda t: b1(B, t), lambda t: b2(B, t)])
    P.barrier()
    stE.close()
    stEW2.close()
    stEW.close()

    P.finish()
    es.close()
    P.es.close()
    return P, dbg_d


def _prep_shared(inp):
    f = np.float32
    sh = {}

    def fm(v, k):
        return np.ascontiguousarray(np.asarray(v, f).reshape(k, 128).T)

    def rep(v):
        v = np.asarray(v, f).reshape(1, -1)
        return np.ascontiguousarray(np.broadcast_to(v, (128, v.shape[1])))

    gpre_h = fm(inp["g_pre_mix"][0], 8)
    gpffn_h = fm(inp["g_pre_ffn"][0], 8)
    sh["gpm"] = rep(inp["g_post_mix"][0])
    sh["gpf"] = rep(inp["g_post_ffn"][0])
    sh["gssm"] = rep(inp["g_ssm_out"][0])
    gattn_h = rep(inp["g_attn_out"][0])
    sh["bglu"] = rep(inp["b_glu"][0])
    sink_h = rep(inp["attn_sinks"][0])
    sh["win"] = np.ascontiguousarray(np.asarray(inp["w_in"][0], f).reshape(8, 128, 1280).transpose(1, 0, 2))
    sh["wglu"] = np.ascontiguousarray(np.asarray(inp["w_glu"][0], f).reshape(4, 128, 1024).transpose(1, 0, 2))
    sh["wout"] = np.ascontiguousarray(np.asarray(inp["w_out"][0], f).reshape(8, 128, 1024).transpose(1, 0, 2))
    wgu = np.asarray(inp["w_gate_up"][0], f)
    wg = wgu[:, :DFF].reshape(8, 128, NF, 128)
    wu = wgu[:, DFF:].reshape(8, 128, NF, 128)
    w2 = np.concatenate([wg, wu], axis=3)
    w2 = w2.transpose(2, 1, 0, 3).reshape(11, 2, 128, 8, 256)
    sh["wgu"] = np.ascontiguousarray(w2.transpose(0, 2, 1, 3, 4).reshape(11, 128, 2 * 8 * 256))
    sh["wdn"] = np.ascontiguousarray(np.asarray(inp["w_down"][0], f).reshape(NF, 128, 1024).transpose(1, 0, 2))

    def gl(v):
        v = np.asarray(v, f)
        rest = v.shape[2:]
        v = v.reshape(16, 2, 64, *rest)
        v = np.moveaxis(v, 0, 2)
        return np.ascontiguousarray(v.reshape(128, 16, *rest))

    lre_h = gl(inp["ssm_lambda_re"][0])
    lim_h = gl(inp["ssm_lambda_im"][0])
    ldt_h = gl(np.broadcast_to(np.asarray(inp["ssm_log_dt"][0], f)[:, None], (32, 64)))
    bre_h = gl(inp["ssm_b_re"][0])
    bim_h = gl(inp["ssm_b_im"][0])
    cre_h = gl(np.asarray(inp["ssm_c_re"][0], f).transpose(0, 2, 1))
    cim_h = gl(np.asarray(inp["ssm_c_im"][0], f).transpose(0, 2, 1))
    d = np.asarray(inp["ssm_d"][0], f)
    dl_h = np.ascontiguousarray(np.broadcast_to(d.T[None, :, :], (8, 16, 32)).reshape(128, 32))
    sh["pkB"] = np.ascontiguousarray(np.concatenate([lre_h, lim_h, ldt_h, bre_h.reshape(128, 256), bim_h.reshape(128, 256),
                                                     cre_h.reshape(128, 256), cim_h.reshape(128, 256)], axis=1))
    sh["c_ident"] = np.eye(128, dtype=f)
    kk = np.arange(128)[:, None]
    qq = np.arange(128)[None, :]
    am = np.zeros((128, 2, 2, 2, 128), f)
    NEG = -30000.0
    am[:, :, 0, :, :] = np.where(qq >= kk, 0.0, NEG)[:, None, None, :]
    am[:, :, 1, :, :] = np.where(kk > qq, 0.0, NEG)[:, None, None, :]
    sh["c_amask"] = am
    ii = (np.arange(128) // 16)[:, None]
    jj = (np.arange(128) // 16)[None, :]
    tmask_h = (jj >= ii).astype(f)
    kv = np.concatenate([7 - np.arange(8), np.arange(8) - 7, np.arange(8) + 1]).astype(f)
    kv_h = np.broadcast_to(kv[None, :], (128, 24))
    cidx_h = np.broadcast_to(np.arange(256, dtype=f)[None, :], (128, 256))
    invf = (np.float32(500000.0) ** (-(np.arange(8, dtype=f) * np.float32(2.0) / np.float32(16.0)))).astype(f)
    invf_h = np.broadcast_to(invf[None, :], (128, 8))
    sh["pkA"] = np.ascontiguousarray(np.concatenate([sh["c_ident"], tmask_h, kv_h, cidx_h, invf_h, gpre_h, gpffn_h, gattn_h, sink_h, dl_h], axis=1).astype(f))
    return sh


def _in_maps(inp):
    sh = _prep_shared(inp)
    x = np.asarray(inp["x"], np.float32)
    pos = np.asarray(inp["positions"], np.int32)
    maps = []
    for b in range(8):
        m = dict(sh)
        m["x"] = np.ascontiguousarray(x[b])
        m["pos"] = np.ascontiguousarray(pos[b].reshape(NT, 128).T)
        maps.append(m)
    return maps


_CACHE = {}


def _get_nc(dbg=()):
    key = tuple(dbg)
    if key not in _CACHE:
        nc0 = bass.Bass("TRN2", target_bir_lowering=False)
        P0, _ = _build(nc0, None, dbg)
        plan = P0.get_plan()
        nc = bass.Bass("TRN2", target_bir_lowering=False)
        _, dbg_d = _build(nc, plan, dbg)
        _CACHE[key] = (nc, dbg_d)
    return _CACHE[key]


def kernel(**inputs):
    nc, _ = _get_nc()
    maps = _in_maps(inputs)
    res = run_bass_kernel_spmd(nc, maps, core_ids=list(range(8)))
    out = np.stack([np.asarray(r["out"], np.float32) for r in res.results], axis=0)
    return out
```

```python
import numpy as np
from contextlib import ExitStack
import concourse.bass as bass
import concourse.mybir as mybir
from concourse.bass_utils import run_bass_kernel_spmd

F32, BF16, I32 = mybir.dt.float32, mybir.dt.bfloat16, mybir.dt.int32
AF = mybir.ActivationFunctionType
ALU = mybir.AluOpType

T = 2048
D = 1024
NT = 16
DFF = 2816
NF = 22
EPS = 1e-6
TWO_PI = 2.0 * np.pi


class Prog:
    ENG = ("pe", "act", "dve", "pool", "sp")

    def __init__(self, nc, plan):
        self.nc = nc
        self.plan = plan
        self.emit = plan is not None
        self.engs = {"pe": nc.tensor, "act": nc.scalar, "dve": nc.vector, "pool": nc.gpsimd, "sp": nc.sync}
        self.ops = []
        self.last_w = {}
        self.readers = {}
        self.need = set()
        self.group_total = {}
        self.sems = {}
        self.cnt = {}
        self.wm = {e: {} for e in self.ENG}
        self.ev = {}
        self.es = ExitStack()
        self.deferred = None
        self.backlog = []
        for e in ("pe", "act", "dve", "pool"):
            self.new_sem(e)

    def new_sem(self, key):
        self.sems[key] = self.es.enter_context(self.nc.semaphore("s_" + str(key)))
        self.cnt[key] = 0
        return key

    def _deps(self, engine, is_dma, reads, writes):
        raw = set()
        oth = set()
        for r in reads:
            if r in self.last_w:
                raw.add(self.last_w[r])
        for w in writes:
            if w in self.last_w:
                oth.add(self.last_w[w])
            oth.update(self.readers.get(w, ()))
        out = list(raw) if engine != "pe" else [d for d in raw if self.ops[d][0] != "pe" or self.ops[d][1]]
        for d in oth:
            if d in raw:
                continue
            pe, pd = self.ops[d]
            if (not is_dma) and (not pd) and pe == engine and engine == "pe":
                continue
            out.append(d)
        return out

    def _wait(self, engine, d):
        semkey, value, clock = self.ev[d]
        wm = self.wm[engine]
        if wm.get(semkey, 0) >= value:
            return
        self.engs[engine].wait_ge(self.sems[semkey], value)
        for k, v in clock.items():
            if wm.get(k, 0) < v:
                wm[k] = v
        wm[semkey] = value

    def op(self, engine, fn, reads=(), writes=(), dma_sem=None, group=False, _now=False, cost=0.3):
        if self.deferred is not None and not _now and dma_sem is None:
            self.deferred.append((engine, fn, reads, writes, cost))
            return None
        i = len(self.ops)
        is_dma = dma_sem is not None
        deps = self._deps(engine, is_dma, reads, writes)
        self.ops.append((engine, is_dma))
        for r in reads:
            self.readers.setdefault(r, []).append(i)
        for w in writes:
            self.last_w[w] = i
            self.readers[w] = []
        for d in deps:
            self.need.add(d)
        if is_dma:
            self.group_total[dma_sem] = self.group_total.get(dma_sem, 0) + 16
        if not self.emit:
            return i
        for d in sorted(deps):
            self._wait(engine, d)
        ins = fn()
        if is_dma:
            ins.then_inc(self.sems[dma_sem], 16)
            self.cnt[dma_sem] += 16
            val = self.plan["group_total"][dma_sem] if group else self.cnt[dma_sem]
            clock = dict(self.wm[engine])
            self.ev[i] = (dma_sem, val, clock)
        elif i in self.plan["need"]:
            ins.then_inc(self.sems[engine], 1)
            self.cnt[engine] += 1
            self.ev[i] = (engine, self.cnt[engine], dict(self.wm[engine]))
        return i

    def defer_begin(self):
        self.deferred = []

    def defer_end(self):
        self.backlog.extend(self.deferred)
        self.deferred = None

    def flush(self, k=None):
        budget = 1e9 if k is None else float(k)
        while self.backlog and budget > 0:
            engine, fn, reads, writes, cost = self.backlog.pop(0)
            self.op(engine, fn, reads, writes, _now=True)
            budget -= cost

    def barrier(self):
        last = {}
        for i, (e, is_dma) in enumerate(self.ops):
            if is_dma:
                last[("dma", i)] = i
            else:
                last[e] = i
        idxs = sorted(set(last.values()))
        for d in idxs:
            self.need.add(d)
        if not self.emit:
            return
        for e in self.ENG:
            for d in idxs:
                pe, pd = self.ops[d]
                if (not pd) and pe == e:
                    continue
                if d in self.ev:
                    self._wait(e, d)

    def finish(self):
        self.barrier()

    def get_plan(self):
        return {"need": set(self.need), "group_total": dict(self.group_total)}


def _pipeline(n_items, stages, hook=None):
    ns = len(stages)
    for it in range(n_items + ns - 1):
        for k, st in enumerate(stages):
            i = it - k
            if 0 <= i < n_items:
                st(i)
        if hook is not None:
            hook(it)


def _build(nc, plan, dbg=()):
    P = Prog(nc, plan)
    es = ExitStack()

    def dram_in(name, shape, dt=F32):
        return nc.dram_tensor(name, list(shape), dt, kind="ExternalInput").ap()

    x_d = dram_in("x", [T, D])
    pos_d = dram_in("pos", [128, NT], I32)
    gpm_d = dram_in("gpm", [128, D])
    gpf_d = dram_in("gpf", [128, D])
    gssm_d = dram_in("gssm", [128, 512])
    bglu_d = dram_in("bglu", [128, 1024])
    win_d = dram_in("win", [128, 8, 1280])
    wglu_d = dram_in("wglu", [128, 4, 1024])
    wout_d = dram_in("wout", [128, 8, 1024])
    wgu_d = dram_in("wgu", [11, 128, 2 * 8 * 256])
    wdn_d = dram_in("wdn", [128, NF, 1024])
    ident_d = dram_in("c_ident", [128, 128])
    pkA_d = dram_in("pkA", [128, 1112])
    pkB_d = dram_in("pkB", [128, 1072])
    amask_d = dram_in("c_amask", [128, 2, 2, 2, 128])
    out_d = nc.dram_tensor("out", [T, D], F32, kind="ExternalOutput").ap()
    dbg_d = {}

    def sb(name, shape, dt=F32, stack=None, side="left"):
        return (stack or es).enter_context(nc.sbuf_tensor("sb_" + name, list(shape), dt, side=side))

    def sbr(name, shape, dt=F32, stack=None):
        return sb(name, shape, dt, stack, side="right")

    def ps(name, shape, dt=F32, stack=None):
        return (stack or es).enter_context(nc.psum_tensor("ps_" + name, list(shape), dt))

    V, S, G, PE_ = nc.vector, nc.scalar, nc.gpsimd, nc.tensor

    def dump(name, ap_sb, shape, dt=F32, region=None):
        if name not in dbg:
            return
        d = nc.dram_tensor("dbg_" + name, list(shape), dt, kind="ExternalOutput").ap()
        dbg_d[name] = d
        sk = P.new_sem("dbg_" + name)
        P.barrier()
        P.op("sp", lambda: nc.sync.dma_start(out=d, in_=ap_sb), reads=[], writes=[("dbg", name)], dma_sem=sk)
        P.barrier()

    PKA = [("ident_f", 128), ("tmask", 128), ("kvc", 24), ("cidx", 256), ("invf", 8), ("gpre", 8), ("gpffn", 8), ("gattn", 512), ("sinkexp", 8), ("dl", 32)]
    pkA_sb = sb("pkA", [128, sum(w for _, w in PKA)])
    _v = {}
    _o = 0
    for _n, _w in PKA:
        _v[_n] = pkA_sb[:, _o:_o + _w]
        _o += _w
    ident_f, tmask, kvc, cidx, invf, gpre, gpffn, gattn, sinkexp, dl = [_v[n] for n, _ in PKA]
    ident_b = sb("ident_b", [128, 128], BF16)
    amask = sb("amask", [128, 2, 2, 2, 128], BF16)
    nhalf = sb("nhalf", [128, 1])
    posi = sb("posi", [128, NT], I32)
    actbuf = sb("actbuf", [128, 8, T], BF16)
    hnT = actbuf
    mergedT = actbuf
    stPO = ExitStack()
    WS_b = sb("WS_b", [128, 16, 2, 2, 64], BF16, stPO)
    Toep_b = sb("Toep_b", [128, 16, 2, 128], BF16, stPO)
    CA_b = sb("CA_b", [128, 16, 2, 128], BF16, stPO)
    Amag = sb("Amag", [128, 16], F32, stPO)
    phi = sb("phi", [128, 16], F32, stPO)
    stU = ExitStack()
    u_tm2 = sbr("u_tm2", [128, 2, 32, 8, 16], BF16, stU)
    stBC = ExitStack()
    qkT = sbr("qkT", [128, 6, T], BF16, stBC)
    Vsb = sbr("Vsb", [128, NT, 2, 65], BF16, stBC)

    sem_par = P.new_sem("par")

    def pload(dst, src, name, eng="sp"):
        P.op(eng, lambda: P.engs[eng].dma_start(out=dst, in_=src), writes=[name], dma_sem=sem_par, group=True)

    P.op("sp", lambda: nc.sync.dma_start(out=pkA_sb[:], in_=pkA_d), writes=[n for n, _ in PKA], dma_sem=sem_par, group=True)
    pload(posi[:], pos_d, "posi")
    sem_pw = P.new_sem("parw")

    def wload(dst, src, name):
        P.op("pool", lambda: nc.gpsimd.dma_start(out=dst, in_=src), writes=[name], dma_sem=sem_pw, group=True)

    wload(ident_b[:], ident_d, "ident_b")
    wload(amask[:], amask_d, "amask")

    stAB = ExitStack()
    win_b = sbr("win_b", [128, 8, 1280], BF16, stAB)
    for k in range(8):
        wload(win_b[:, k, 512:1280], win_d[:, k, 512:1280], ("win_b", k))
    sem_pw2 = P.new_sem("parw2")
    for k in range(8):
        P.op("pool", lambda: nc.gpsimd.dma_start(out=win_b[:, k, 0:512], in_=win_d[:, k, 0:512]), writes=[("win_u", k)], dma_sem=sem_pw2, group=True)

    P.op("dve", lambda: V.memset(nhalf[:], -0.5), writes=["nhalf"])
    P.op("act", lambda: S.activation(out=sinkexp[:], in_=sinkexp[:], func=AF.Exp), reads=["sinkexp"], writes=["sinkexp"])

    cs = sbr("cs", [128, 2, NT, 8], F32, stAB)
    dump("cs", cs[:], [128, 2, NT, 8])

    P.op("pool", lambda: G.memset(Vsb[:], 1.0), writes=["Vsb"])

    sem_pp = P.new_sem("parP")

    def ploadP(dst, src, name):
        P.op("sp", lambda: nc.sync.dma_start(out=dst, in_=src), writes=[name], dma_sem=sem_pp, group=True)

    stP = ExitStack()
    if True:
        pkB_sb = sb("pkB", [128, 1072], F32, stP, side="right")
        lre, lim, ldt = pkB_sb[:, 0:16], pkB_sb[:, 16:32], pkB_sb[:, 32:48]
        bre = pkB_sb[:, 48:304].rearrange("p (q h) -> p q h", q=16)
        bim = pkB_sb[:, 304:560].rearrange("p (q h) -> p q h", q=16)
        cre = pkB_sb[:, 560:816].rearrange("p (q h) -> p q h", q=16)
        cim = pkB_sb[:, 816:1072].rearrange("p (q h) -> p q h", q=16)
        P.op("sp", lambda: nc.sync.dma_start(out=pkB_sb[:], in_=pkB_d), writes=["lre", "lim", "ldt", "bre", "bim", "cre", "cim"], dma_sem=sem_pp, group=True)
        lr = sb("lr", [128, 16], F32, stP, side="right")
        dtt = sb("dtt", [128, 16], F32, stP, side="right")
        ldv = sb("ldv", [128, 16], F32, stP, side="right")
        th = sb("th", [128, 16], F32, stP, side="right")
        den = sb("denP", [128, 16], F32, stP, side="right")
        rdn = sb("rdnP", [128, 16], F32, stP, side="right")
        ski = sb("ski", [128, 16], I32, stP, side="right")
        s16 = [sb(f"s16_{i}", [128, 16], F32, stP, side="right") for i in range(4)]
        argm = sb("argm", [128, 16, 24], F32, stP, side="right")
        Emag = argm
        ett = sb("ett", [128, 2, 16, 24], F32, stP, side="right")
        eki = sb("eki", [128, 2, 16, 24], I32, stP, side="right")
        ekf = sb("ekf", [128, 2, 16, 24], F32, stP, side="right")
        Esc = ett
        Ere = sb("Ere", [128, 16, 24], F32, stP, side="right")
        Eim = sb("Eim", [128, 16, 24], F32, stP, side="right")
        fr = sb("fr", [128, 16], F32, stP, side="right")
        fi = sb("fi", [128, 16], F32, stP, side="right")
        Bre = sb("Bre", [128, 16, 16], F32, stP, side="right")
        Bim = sb("Bim", [128, 16, 16], F32, stP, side="right")
        b16 = [sb(f"b16_{i}", [128, 16, 16], F32, stP, side="right") for i in range(2)]
        t1 = sb("t1P", [128, 16, 8, 16], F32, stP, side="right")
        t2 = sb("t2P", [128, 16, 8, 16], F32, stP, side="right")
        WSt_b = sb("WSt_b", [128, 16, 2, 128], BF16, stP, side="right")
        CN_b = sb("CN_b", [128, 16, 2, 128], BF16, stP, side="right")
        tmpT = sb("tmpT", [128, 4, 128], F32, stP, side="right")

        posf = s16[0]
        tt = t1[:, 0:2].rearrange("p a b c -> p a (b c)").rearrange("p a (n f) -> p a n f", f=8)
        kf = t1[:, 2:4].rearrange("p a b c -> p a (b c)").rearrange("p a (n f) -> p a n f", f=8)
        ki = t2[:, 0:2].rearrange("p a b c -> p a (b c)").rearrange("p a (n f) -> p a n f", f=8).bitcast(I32)
        P.op("dve", lambda: V.tensor_copy(out=posf[:], in_=posi[:]), reads=["posi"], writes=["s16_0"])
        P.op("dve", lambda: V.tensor_tensor(out=tt[:, 0], in0=posf[:, :].unsqueeze(2).to_broadcast([128, NT, 8]),
                                             in1=invf[:, :].unsqueeze(1).to_broadcast([128, NT, 8]), op=ALU.mult),
             reads=["s16_0", "invf"], writes=["t1P"])
        P.op("dve", lambda: V.tensor_scalar(out=tt[:, 0], in0=tt[:, 0], scalar1=float(1.0 / TWO_PI), scalar2=None, op0=ALU.mult),
             reads=["t1P"], writes=["t1P"])
        P.op("dve", lambda: V.tensor_scalar(out=tt[:, 1], in0=tt[:, 0], scalar1=0.25, scalar2=None, op0=ALU.add),
             reads=["t1P"], writes=["t1P"])
        P.op("dve", lambda: V.tensor_copy(out=ki, in_=tt), reads=["t1P"], writes=["t2P"])
        P.op("dve", lambda: V.tensor_copy(out=kf, in_=ki), reads=["t2P"], writes=["t1P"])
        P.op("dve", lambda: V.tensor_tensor(out=tt, in0=tt, in1=kf, op=ALU.subtract), reads=["t1P"], writes=["t1P"])
        P.op("act", lambda: S.activation(out=cs[:], in_=tt, func=AF.Sin, scale=float(TWO_PI)), reads=["t1P"], writes=["cs"])

        P.defer_begin()

        def dv(fn, reads, writes, cost=0.3):
            P.op("dve", fn, reads=reads, writes=writes, cost=cost)

        def tt_(out, a, b, op, reads, writes, cost=0.3):
            dv(lambda: V.tensor_tensor(out=out, in0=a, in1=b, op=op), reads, writes, cost)

        dv(lambda: V.tensor_scalar(out=lr[:], in0=lre[:], scalar1=-1e-4, scalar2=None, op0=ALU.min), ["lre"], ["lr"])
        P.op("act", lambda: S.activation(out=dtt[:], in_=ldt[:], func=AF.Exp), reads=["ldt"], writes=["dtt"])
        tt_(ldv[:], lr[:], dtt[:], ALU.mult, ["lr", "dtt"], ["ldv"])
        tt_(th[:], lim[:], dtt[:], ALU.mult, ["lim", "dtt"], ["th"])
        tt_(s16[0][:], lr[:], lr[:], ALU.mult, ["lr"], ["s16_0"])
        tt_(s16[1][:], lim[:], lim[:], ALU.mult, ["lim"], ["s16_1"])
        tt_(den[:], s16[0][:], s16[1][:], ALU.add, ["s16_0", "s16_1"], ["denP"])
        dv(lambda: V.reciprocal(out=rdn[:], in_=den[:]), ["denP"], ["rdnP"])
        tt_(argm[:], ldv[:, :].unsqueeze(2).to_broadcast([128, 16, 24]), kvc[:, :].unsqueeze(1).to_broadcast([128, 16, 24]), ALU.mult,
            ["ldv", "kvc"], ["argm"])
        P.op("act", lambda: S.activation(out=Emag[:], in_=argm[:], func=AF.Exp), reads=["argm"], writes=["Emag"])
        tt_(ett[:, 0], th[:, :].unsqueeze(2).to_broadcast([128, 16, 24]), kvc[:, :].unsqueeze(1).to_broadcast([128, 16, 24]), ALU.mult,
            ["th", "kvc"], ["ett"])
        dv(lambda: V.tensor_scalar(out=ett[:, 0], in0=ett[:, 0], scalar1=float(1.0 / TWO_PI), scalar2=None, op0=ALU.mult), ["ett"], ["ett"])
        dv(lambda: V.tensor_scalar(out=ett[:, 1], in0=ett[:, 0], scalar1=0.25, scalar2=None, op0=ALU.add), ["ett"], ["ett"])
        dv(lambda: V.tensor_copy(out=eki[:], in_=ett[:]), ["ett"], ["eki"])
        dv(lambda: V.tensor_copy(out=ekf[:], in_=eki[:]), ["eki"], ["ekf"])
        tt_(ett[:], ett[:], ekf[:], ALU.subtract, ["ett", "ekf"], ["ett"])
        P.op("act", lambda: S.activation(out=Esc[:], in_=ett[:], func=AF.Sin, scale=float(TWO_PI)), reads=["ett"], writes=["Esc"])
        tt_(Ere[:], Emag[:], Esc[:, 1], ALU.mult, ["Emag", "Esc"], ["Ere"])
        tt_(Eim[:], Emag[:], Esc[:, 0], ALU.mult, ["Emag", "Esc"], ["Eim"])
        ar = Ere[:, :, 16]
        ai = Eim[:, :, 16]
        dv(lambda: V.tensor_scalar(out=s16[0][:], in0=ar, scalar1=-1.0, scalar2=None, op0=ALU.add), ["Ere"], ["s16_0"])
        tt_(s16[1][:], s16[0][:], lr[:], ALU.mult, ["s16_0", "lr"], ["s16_1"])
        tt_(s16[2][:], ai, lim[:], ALU.mult, ["Eim", "lim"], ["s16_2"])
        tt_(s16[1][:], s16[1][:], s16[2][:], ALU.add, ["s16_1", "s16_2"], ["s16_1"])
        tt_(fr[:], s16[1][:], rdn[:], ALU.mult, ["s16_1", "rdnP"], ["fr"])
        tt_(s16[2][:], ai, lr[:], ALU.mult, ["Eim", "lr"], ["s16_2"])
        tt_(s16[3][:], s16[0][:], lim[:], ALU.mult, ["s16_0", "lim"], ["s16_3"])
        tt_(s16[2][:], s16[2][:], s16[3][:], ALU.subtract, ["s16_2", "s16_3"], ["s16_2"])
        tt_(fi[:], s16[2][:], rdn[:], ALU.mult, ["s16_2", "rdnP"], ["fi"])
        frb = fr[:, :].unsqueeze(2).to_broadcast([128, 16, 16])
        fib = fi[:, :].unsqueeze(2).to_broadcast([128, 16, 16])
        tt_(b16[0][:], bre[:], frb, ALU.mult, ["bre", "fr"], ["b16_0"])
        tt_(b16[1][:], bim[:], fib, ALU.mult, ["bim", "fi"], ["b16_1"])
        tt_(Bre[:], b16[0][:], b16[1][:], ALU.subtract, ["b16_0", "b16_1"], ["Bre"])
        tt_(b16[0][:], bim[:], frb, ALU.mult, ["bim", "fr"], ["b16_0"])
        tt_(b16[1][:], bre[:], fib, ALU.mult, ["bre", "fi"], ["b16_1"])
        tt_(Bim[:], b16[0][:], b16[1][:], ALU.add, ["b16_0", "b16_1"], ["Bim"])

        def cprod(Er, Ei, Xr, Xi, out_re, out_im, neg_im, rn, wn):
            Erb = Er.unsqueeze(3).to_broadcast([128, 16, 8, 16])
            Eib = Ei.unsqueeze(3).to_broadcast([128, 16, 8, 16])
            Xrb = Xr.unsqueeze(2).to_broadcast([128, 16, 8, 16])
            Xib = Xi.unsqueeze(2).to_broadcast([128, 16, 8, 16])
            o_re = out_re.rearrange("p q (k h) -> p q k h", k=8)
            o_im = out_im.rearrange("p q (k h) -> p q k h", k=8)
            tt_(t1[:], Erb, Xrb, ALU.mult, rn, ["t1P"], 2.3)
            tt_(t2[:], Eib, Xib, ALU.mult, rn, ["t2P"], 2.3)
            tt_(o_re, t1[:], t2[:], ALU.subtract, ["t1P", "t2P"], [wn], 2.3)
            tt_(t1[:], Erb, Xib, ALU.mult, rn, ["t1P"], 2.3)
            tt_(t2[:], Eib, Xrb, ALU.mult, rn, ["t2P"], 2.3)
            if neg_im:
                dv(lambda: V.scalar_tensor_tensor(out=o_im, in0=t1[:], scalar=-1.0, in1=t2[:], op0=ALU.mult, op1=ALU.subtract), ["t1P", "t2P"], [wn], 2.3)
            else:
                tt_(o_im, t1[:], t2[:], ALU.add, ["t1P", "t2P"], [wn], 2.3)

        cprod(Ere[:, :, 0:8], Eim[:, :, 0:8], Bre[:], Bim[:], WSt_b[:, :, 0, :], WSt_b[:, :, 1, :], False, ["Ere", "Eim", "Bre", "Bim"], "WSt_b")
        cprod(Ere[:, :, 8:16], Eim[:, :, 8:16], cre[:], cim[:], CN_b[:, :, 0, :], CN_b[:, :, 1, :], True, ["Ere", "Eim", "cre", "cim"], "CN_b")
        cprod(Ere[:, :, 16:24], Eim[:, :, 16:24], cre[:], cim[:], CA_b[:, :, 0, :], CA_b[:, :, 1, :], True, ["Ere", "Eim", "cre", "cim"], "CA_b")
        dv(lambda: V.tensor_copy(out=Amag[:], in_=Emag[:, :, 23]), ["Emag"], ["Amag"])
        dv(lambda: V.tensor_scalar(out=s16[0][:], in0=th[:], scalar1=float(8.0 / TWO_PI), scalar2=None, op0=ALU.mult), ["th"], ["s16_0"])
        dv(lambda: V.tensor_copy(out=ski[:], in_=s16[0][:]), ["s16_0"], ["ski"])
        dv(lambda: V.tensor_copy(out=s16[1][:], in_=ski[:]), ["ski"], ["s16_1"])
        tt_(phi[:], s16[0][:], s16[1][:], ALU.subtract, ["s16_0", "s16_1"], ["phi"])
        P.defer_end()

    stA_ps = ExitStack()
    with ExitStack() as stA:
        NXB = 3
        x_t = [sb(f"x_t{i}", [128, D], F32, stA, side="right") for i in range(NXB)]
        xn = [sb(f"xn{i}", [128, D], BF16, stA, side="right") for i in range(2)]
        junk = sb("junkA", [128, D], BF16, stA, side="right")
        ssq = sb("ssqA", [128, NT], F32, stA, side="right")
        rstd = sb("rstdA", [128, NT], F32, stA, side="right")
        tp_ps = [ps(f"tp_ps{i}", [128, 8, 128], BF16, stA_ps) for i in range(2)]
        qkv_ps = [ps(f"qkv_ps{i}", [128, 1024], F32, stA_ps) for i in range(2)]
        tq_ps = [ps(f"tq_ps{i}", [128, 8, 128], BF16, stA_ps) for i in range(2)]
        qk_rot = [sb(f"qk_rot{i}", [128, 768], BF16, stA, side="right") for i in range(2)]
        rt = [sb(f"rope_t{i}", [128, 10, 8], F32, stA, side="right") for i in range(4)]
        semx = [P.new_sem(f"x{i}") for i in range(NXB)]

        def p0(n):
            s3 = n % NXB
            P.op("sp", lambda: nc.sync.dma_start(out=x_t[s3][:], in_=x_d[n * 128:(n + 1) * 128, :]),
                 writes=[("x_t", s3)], dma_sem=semx[s3])
            P.op("act", lambda: S.activation(out=junk[:], in_=x_t[s3][:], func=AF.Square, accum_out=ssq[:, n:n + 1]),
                 reads=[("x_t", s3)], writes=["junkA", ("ssqA", n)])

        def p0b(n):
            P.op("pool", lambda: G.tensor_scalar(out=rstd[:, n:n + 1], in0=ssq[:, n:n + 1], scalar1=float(1.0 / D), scalar2=float(EPS),
                                                  op0=ALU.mult, op1=ALU.add), reads=[("ssqA", n)], writes=[("rstdA", n)])
            P.op("pool", lambda: G.tensor_tensor(out=rstd[:, n:n + 1], in0=rstd[:, n:n + 1], in1=nhalf[:, 0:1], op=ALU.pow),
                 reads=[("rstdA", n), "nhalf"], writes=[("rstdA", n)])

        def p1(n):
            s3 = n % NXB
            s2 = n % 2
            P.op("act", lambda: S.activation(out=xn[s2][:], in_=x_t[s3][:], func=AF.Copy, scale=rstd[:, n:n + 1]),
                 reads=[("x_t", s3), ("rstdA", n)], writes=[("xn", s2)])

        def p1b(n):
            s2 = n % 2
            for k in range(8):
                P.op("pe", lambda: PE_.transpose(out=tp_ps[s2][:, k, :], in_=xn[s2][:, k * 128:(k + 1) * 128], identity=ident_b[:]),
                     reads=[("xn", s2), "ident_b"], writes=[("tp_ps", s2)])

        def p2(n):
            s2 = n % 2
            P.op("dve", lambda: V.tensor_tensor(out=hnT[:, :, n * 128:(n + 1) * 128], in0=tp_ps[s2][:],
                                                 in1=gpre[:, :].unsqueeze(2).to_broadcast([128, 8, 128]), op=ALU.mult),
                 reads=[("tp_ps", s2), "gpre"], writes=[("hnT", n)])

        def p3(n):
            s2 = n % 2
            tl = slice(n * 128, (n + 1) * 128)
            for k in range(8):
                P.op("pe", lambda: PE_.matmul(qkv_ps[s2][:, 0:512], lhsT=hnT[:, k, tl], rhs=win_b[:, k, 512:1024], start=(k == 0), stop=(k == 7)),
                     reads=[("hnT", n), ("win_b", k)], writes=[("qkv_ps", s2)])
                P.op("pe", lambda: PE_.matmul(qkv_ps[s2][:, 512:768], lhsT=hnT[:, k, tl], rhs=win_b[:, k, 1024:1280], start=(k == 0), stop=(k == 7)),
                     reads=[("hnT", n), ("win_b", k)], writes=[("qkv_ps", s2)])

        def p4(n):
            s2 = n % 2
            qk = qkv_ps[s2][:, 0:640].rearrange("p (h d) -> p h d", h=10)
            qr = qk_rot[s2][:, 0:640].rearrange("p (h d) -> p h d", h=10)
            cosb = cs[:, 1, n, :].unsqueeze(1).to_broadcast([128, 10, 8])
            sinb = cs[:, 0, n, :].unsqueeze(1).to_broadcast([128, 10, 8])
            rd = [("qkv_ps", s2), "cs"]
            P.op("dve", lambda: V.tensor_tensor(out=rt[0][:], in0=qk[:, :, 0:8], in1=cosb, op=ALU.mult), reads=rd, writes=["rt0"])
            P.op("dve", lambda: V.tensor_tensor(out=rt[1][:], in0=qk[:, :, 8:16], in1=sinb, op=ALU.mult), reads=rd, writes=["rt1"])
            P.op("dve", lambda: V.tensor_tensor(out=rt[2][:], in0=qk[:, :, 8:16], in1=cosb, op=ALU.mult), reads=rd, writes=["rt2"])
            P.op("dve", lambda: V.tensor_tensor(out=rt[3][:], in0=qk[:, :, 0:8], in1=sinb, op=ALU.mult), reads=rd, writes=["rt3"])
            P.op("dve", lambda: V.tensor_tensor(out=qr[:, :, 0:8], in0=rt[0][:], in1=rt[1][:], op=ALU.subtract),
                 reads=["rt0", "rt1"], writes=[("qk_rotA", s2)])
            P.op("dve", lambda: V.tensor_tensor(out=qr[:, :, 8:16], in0=rt[2][:], in1=rt[3][:], op=ALU.add),
                 reads=["rt2", "rt3"], writes=[("qk_rotB", s2)])
            P.op("act", lambda: S.activation(out=qr[:, :, 16:64], in_=qk[:, :, 16:64], func=AF.Copy),
                 reads=[("qkv_ps", s2), "rt0", "rt1", "rt2", "rt3"], writes=[("qk_rotC", s2)])
            P.op("act", lambda: S.activation(out=qk_rot[s2][:, 640:704], in_=qk_rot[s2][:, 512:576], func=AF.Copy),
                 reads=[("qk_rotA", s2), ("qk_rotB", s2), ("qk_rotC", s2)], writes=[("qk_rotD", s2)])
            P.op("act", lambda: S.activation(out=Vsb[:, n, :, 0:64], in_=qkv_ps[s2][:, 640:768].rearrange("p (h d) -> p h d", h=2), func=AF.Copy),
                 reads=[("qkv_ps", s2), "rt0", "rt1", "rt2", "rt3"], writes=["Vsb"])

        def p5a(n):
            s2 = n % 2
            tl = slice(n * 128, (n + 1) * 128)
            rdq = [("qk_rotA", s2), ("qk_rotB", s2), ("qk_rotC", s2), ("qk_rotD", s2), "ident_b"]
            for c in range(4):
                P.op("pe", lambda: PE_.transpose(out=tq_ps[s2][:, c, :], in_=qk_rot[s2][:, c * 128:(c + 1) * 128], identity=ident_b[:]),
                     reads=rdq, writes=[("tq_ps", s2)])
            P.op("pe", lambda: PE_.transpose(out=tq_ps[s2][:, 4, :], in_=qk_rot[s2][:, 512:640], identity=ident_b[:]),
                 reads=rdq, writes=[("tq_ps", s2)])
            P.op("pe", lambda: PE_.transpose(out=tq_ps[s2][:, 5, :], in_=qk_rot[s2][:, 576:704], identity=ident_b[:]),
                 reads=rdq, writes=[("tq_ps", s2)])

        def p5b(n):
            s2 = n % 2
            tl = slice(n * 128, (n + 1) * 128)
            P.op("act", lambda: S.activation(out=qkT[:, :, tl], in_=tq_ps[s2][:, 0:6, :], func=AF.Copy), reads=[("tq_ps", s2)], writes=[("qkT", n)])

        _pipeline(NT, [p0, p0b, p1, p1b, p2, p3, p4, p5a, p5b], hook=lambda it: P.flush(2.4))

        def u0(it):
            H, i = it // 8, it % 8
            s2 = it % 2
            for k in range(8):
                P.op("pe", lambda: PE_.matmul(qkv_ps[s2][:, 0:512], lhsT=hnT[:, k, H * 1024 + i:(H + 1) * 1024:8], rhs=win_b[:, k, 0:512],
                                              start=(k == 0), stop=(k == 7)),
                     reads=[("hnT", n2) for n2 in range(H * 8, H * 8 + 8)] + [("win_u", k)], writes=[("qkv_ps", s2)])

        def u1(it):
            H, i = it // 8, it % 8
            s2 = it % 2
            P.op("act", lambda: S.activation(out=u_tm2[:, H, :, i, :], in_=qkv_ps[s2][:, 0:512].rearrange("p (g h) -> p g h", g=32), func=AF.Copy),
                 reads=[("qkv_ps", s2)], writes=[("u_tm2", H)])

        _pipeline(16, [u0, u1], hook=lambda it: P.flush(2.6))
        P.flush()
        P.barrier()
    dump("hnT", hnT[:], [128, 8, T], BF16)
    stA_ps.close()
    with ExitStack() as stPP:
        T_ps = ps("T_ps", [128, 2, 4, 128], F32, stPP)
        W_ps = ps("W_ps", [128, 2, 8, 2, 64], BF16, stPP)
        for q0 in range(0, 16, 4):
            for ql in range(4):
                q = q0 + ql
                for e in range(2):
                    pp = slice(e * 64, (e + 1) * 64)
                    P.op("pe", lambda: PE_.matmul(T_ps[:, e, ql, :], lhsT=WSt_b[pp, q, 0, :], rhs=CN_b[pp, q, 0, :], start=True, stop=False),
                         reads=["WSt_b", "CN_b"], writes=["T_ps"])
                    P.op("pe", lambda: PE_.matmul(T_ps[:, e, ql, :], lhsT=WSt_b[pp, q, 1, :], rhs=CN_b[pp, q, 1, :], start=False, stop=True),
                         reads=["WSt_b", "CN_b"], writes=["T_ps"])
            for e in range(2):
                tt_(tmpT[:], T_ps[:, e], tmask[:, :].unsqueeze(1).to_broadcast([128, 4, 128]), ALU.mult, ["T_ps", "tmask"], ["tmpT"])
                for ql in range(4):
                    q = q0 + ql
                    g = 2 * q + e
                    dv(lambda: V.scalar_tensor_tensor(out=Toep_b[:, q, e, :], in0=ident_f[:], scalar=dl[:, g:g + 1], in1=tmpT[:, ql, :],
                                                      op0=ALU.mult, op1=ALU.add), ["tmpT", "ident_f", "dl"], ["Toep_b"])
        for q0 in range(0, 16, 8):
            for ql in range(8):
                q = q0 + ql
                for ri in range(2):
                    for e in range(2):
                        pp = slice(e * 64, (e + 1) * 64)
                        P.op("pe", lambda: PE_.transpose(out=W_ps[:, e, ql, ri, :], in_=WSt_b[pp, q, ri, :], identity=ident_b[pp, pp]),
                             reads=["WSt_b", "ident_b"], writes=["W_ps"])
            for e in range(2):
                P.op("act", lambda: S.activation(out=WS_b[:, q0:q0 + 8, e, :, :], in_=W_ps[:, e], func=AF.Copy), reads=["W_ps"], writes=["WS_b"])
        P.barrier()
    stP.close()
    dump("qkT", qkT[:], [128, 6, T], BF16)
    dump("Vsb", Vsb[:], [128, NT, 2, 65], BF16)
    dump("u_tm2", u_tm2[:], [128, 2, 32, 8, 16], BF16)
    stAB.close()

    stD = ExitStack()
    wglu_b = sb("wglu_b", [128, 4, 1024], BF16, stD)
    bglu = sb("bglu", [128, 1024], F32, stD)
    gssm = sb("gssm", [128, 512], F32, stD)
    sem_pd = P.new_sem("parD")
    sem_pdw = P.new_sem("parDw")

    def ploadD(dst, src, name):
        P.op("sp", lambda: nc.sync.dma_start(out=dst, in_=src), writes=[name], dma_sem=sem_pd, group=True)

    for k in range(4):
        P.op("pool", lambda: nc.gpsimd.dma_start(out=wglu_b[:, k, :], in_=wglu_d[:, k, :]), writes=[("wglu_b", k)], dma_sem=sem_pdw, group=True)
    ploadD(bglu[:], bglu_d, "bglu")
    ploadD(gssm[:], gssm_d, "gssm")

    stYT = ExitStack()
    yT = sb("yT", [128, 4, T], BF16, stYT)

    stTab = ExitStack()
    tab = sb("tab", [128, 2, 16, 256], F32, stTab)
    stTT = ExitStack()
    pki = sb("pki", [128, 2, 2, 256], I32, stTT)
    pkf = sb("pkf", [128, 2, 2, 256], F32, stTT)

    def tab_round(q0):
        tq = tab[:, :, q0:q0 + 2, :]
        rg = ("tab", q0)
        P.op("dve", lambda: V.tensor_tensor(out=tab[:, 0, q0:q0 + 2, :], in0=phi[:, q0:q0 + 2].unsqueeze(2).to_broadcast([128, 2, 256]),
                                             in1=cidx[:, :].unsqueeze(1).to_broadcast([128, 2, 256]), op=ALU.mult), reads=["phi", "cidx"], writes=[rg], cost=0.8)
        P.op("dve", lambda: V.tensor_scalar(out=tab[:, 1, q0:q0 + 2, :], in0=tab[:, 0, q0:q0 + 2, :], scalar1=0.25, scalar2=None, op0=ALU.add), reads=[rg], writes=[rg], cost=0.8)
        P.op("dve", lambda: V.tensor_copy(out=pki[:], in_=tq), reads=[rg], writes=["pki"], cost=1.1)
        P.op("dve", lambda: V.tensor_copy(out=pkf[:], in_=pki[:]), reads=["pki"], writes=["pkf"], cost=1.1)
        P.op("dve", lambda: V.tensor_tensor(out=tq, in0=tq, in1=pkf[:], op=ALU.subtract), reads=[rg, "pkf"], writes=[rg], cost=0.8)

    P.defer_begin()
    for q0_ in range(0, 16, 2):
        tab_round(q0_)
    P.defer_end()

    with ExitStack() as stC:
        st_ps = [ps(f"st_ps{i}", [128, 2, 2, 2, 128], F32, stC) for i in range(2)]
        o_ps = ps("o_ps", [128, 2, 512], F32, stC)
        ta_ps = [ps(f"ta_ps{i}", [128, 8, 128], BF16, stC) for i in range(2)]
        p_sb = [sb(f"p_sb{i}", [128, 2, 2, 2, 128], BF16, stC) for i in range(2)]
        den = sb("denC", [128, 8], F32, stC)
        rden = sb("rdenC", [128, NT, 8], F32, stC)
        ya = [sb(f"yaC{i}", [128, 512], F32, stC) for i in range(5)]
        yab = [sb(f"yabC{i}", [128, 512], BF16, stC) for i in range(2)]
        junkC = sb("junkC", [128, 512], BF16, stC)
        ssqC = sb("ssqC", [128, NT], F32, stC)
        rstdC = sb("rstdC", [128, NT], F32, stC)

        def c_scores(s_):
            n, Gk = s_ // 2, s_ % 2
            slot = s_ % 2
            tl = slice(n * 128, (n + 1) * 128)
            tp = slice((n - 1) * 128, n * 128)
            for hh in range(4):
                h = 4 * Gk + hh
                b0 = (h % 2) * 64
                kc = 4 if Gk == (h % 2) else 5
                first = (hh < 2)
                P.op("pe", lambda: PE_.matmul(st_ps[slot][:, hh % 2, 0, hh // 2, :], lhsT=qkT[b0:b0 + 64, kc, tl], rhs=qkT[b0:b0 + 64, h // 2, tl], start=first, stop=False,
                                              skip_group_check=True),
                     reads=[("qkT", n)], writes=[("st_ps", slot)])
                if n > 0:
                    P.op("pe", lambda: PE_.matmul(st_ps[slot][:, hh % 2, 1, hh // 2, :], lhsT=qkT[b0:b0 + 64, kc, tp], rhs=qkT[b0:b0 + 64, h // 2, tl], start=False, stop=False,
                                                  skip_group_check=True),
                         reads=[("qkT", n), ("qkT", n - 1)], writes=[("st_ps", slot)])
            for par in range(2):
                P.op("pe", lambda: PE_.matmul(st_ps[slot][:, par].rearrange("p c r q -> p (c r q)"), lhsT=ident_b[:], rhs=amask[:, par].rearrange("p c r q -> p (c r q)"),
                                              start=False, stop=True, skip_group_check=True),
                     reads=["amask", "ident_b"], writes=[("st_ps", slot)])

        def c_exp(s_):
            n, Gk = s_ // 2, s_ % 2
            slot = s_ % 2
            ncur = 1 if n == 0 else 2
            P.op("act", lambda: S.activation(out=p_sb[slot][:, :, 0:ncur], in_=st_ps[slot][:, :, 0:ncur], func=AF.Exp, scale=0.125),
                 reads=[("st_ps", slot)], writes=[("p_sb", slot)])

        def c_pv(s_):
            n, Gk = s_ // 2, s_ % 2
            slot = s_ % 2
            for hh in range(4):
                oo = o_ps[:, Gk, hh * 65:(hh + 1) * 65]
                P.op("pe", lambda: PE_.matmul(oo, lhsT=p_sb[slot][:, hh % 2, 0, hh // 2, :], rhs=Vsb[:, n, Gk, :], start=True, stop=(n == 0)),
                     reads=[("p_sb", slot), "Vsb"], writes=["o_ps"])
                if n > 0:
                    P.op("pe", lambda: PE_.matmul(oo, lhsT=p_sb[slot][:, hh % 2, 1, hh // 2, :], rhs=Vsb[:, n - 1, Gk, :], start=False, stop=True),
                         reads=[("p_sb", slot), "Vsb"], writes=["o_ps"])
            if Gk == 1:
                y5 = n % 5
                o4 = o_ps[:, :, 0:260].rearrange("p g (h d) -> p g h d", h=4)
                P.op("dve", lambda: V.tensor_tensor(out=den[:, :].rearrange("p (g h) -> p g h", g=2), in0=o4[:, :, :, 64],
                                                     in1=sinkexp[:, :].rearrange("p (g h) -> p g h", g=2), op=ALU.add),
                     reads=["o_ps", "sinkexp"], writes=["denC"])
                P.op("dve", lambda: V.reciprocal(out=rden[:, n, :], in_=den[:]), reads=["denC"], writes=[("rdenC", n)])
                P.op("dve", lambda: V.tensor_tensor(out=ya[y5][:, :].rearrange("p (g h d) -> p g h d", g=2, h=4), in0=o4[:, :, :, 0:64],
                                                     in1=rden[:, n, :].rearrange("p (g h) -> p g h", g=2).unsqueeze(3).to_broadcast([128, 2, 4, 64]), op=ALU.mult),
                     reads=["o_ps", ("rdenC", n)], writes=[("yaC", y5)])

        def odd(fn):
            def w(s_):
                if s_ % 2 == 1:
                    fn(s_ // 2)
            return w

        def c_sq(n):
            P.op("act", lambda: S.activation(out=junkC[:], in_=ya[n % 5][:], func=AF.Square, accum_out=ssqC[:, n:n + 1]),
                 reads=[("yaC", n % 5)], writes=["junkC", ("ssqC", n)])

        def c_r1(n):
            P.op("dve", lambda: V.tensor_scalar(out=rstdC[:, n:n + 1], in0=ssqC[:, n:n + 1], scalar1=float(1.0 / 512), scalar2=float(EPS),
                                                 op0=ALU.mult, op1=ALU.add), reads=[("ssqC", n)], writes=[("rstdC", n)])

        def c_r2(n):
            P.op("pool", lambda: G.tensor_tensor(out=rstdC[:, n:n + 1], in0=rstdC[:, n:n + 1], in1=nhalf[:, 0:1], op=ALU.pow),
                 reads=[("rstdC", n), "nhalf"], writes=[("rstdC", n)])

        def c_n(n):
            P.op("dve", lambda: V.scalar_tensor_tensor(out=yab[n % 2][:], in0=ya[n % 5][:], scalar=rstdC[:, n:n + 1], in1=gattn[:], op0=ALU.mult, op1=ALU.mult),
                 reads=[("yaC", n % 5), ("rstdC", n), "gattn"], writes=[("yabC", n % 2)])

        def c_t(n):
            y2 = n % 2
            for c in range(4):
                P.op("pe", lambda: PE_.transpose(out=ta_ps[y2][:, c, :], in_=yab[y2][:, c * 128:(c + 1) * 128], identity=ident_b[:]),
                     reads=[("yabC", y2), "ident_b"], writes=[("ta_ps", y2)])

        def c_e(n):
            y2 = n % 2
            tl = slice(n * 128, (n + 1) * 128)
            P.op("dve", lambda: V.tensor_copy(out=mergedT[:, 4:8, tl], in_=ta_ps[y2][:, 0:4, :]), reads=[("ta_ps", y2)], writes=[("mergedT_a", n)])

        _pipeline(2 * NT, [c_scores, c_exp, c_pv, odd(c_sq), odd(c_r1), odd(c_r2), odd(c_n), odd(c_t), odd(c_e)], hook=lambda it: P.flush(0.9))
        P.flush()
        for hh_ in range(2):
            P.op("act", lambda: S.activation(out=tab[:, hh_], in_=tab[:, hh_], func=AF.Sin, scale=float(TWO_PI)),
                 reads=[("tab", q) for q in range(0, 16, 2)], writes=["tab"])
        P.barrier()
    dump("mergedT", mergedT[:], [128, 8, T], BF16)
    stTT.close()
    stBC.close()

    dump("Toep_b", Toep_b[:], [128, 16, 2, 128], BF16)
    dump("WS_b", WS_b[:], [128, 16, 2, 2, 64], BF16)
    dump("CA_b", CA_b[:], [128, 16, 2, 128], BF16)
    dump("tab", tab[:], [128, 2, 16, 256])
    dump("Amag", Amag[:], [128, 16])
    stD2 = ExitStack()
    U8 = sb("U8", [128, 32, 256], BF16, stD2)
    with ExitStack() as stU8:
        tu_ps = [ps(f"tu_ps{i}", [128, 8, 128], BF16, stU8) for i in range(2)]
        cnt = 0
        for H in range(2):
            for g0 in range(0, 32, 8):
                sl = cnt % 2
                cnt += 1
                for gl in range(8):
                    g = g0 + gl
                    P.op("pe", lambda: PE_.transpose(out=tu_ps[sl][:, gl, :], in_=u_tm2[:, H, g, :, :].rearrange("p i h -> p (i h)"), identity=ident_b[:]),
                         reads=[("u_tm2", H), "ident_b"], writes=[("tu_ps", sl)])
                P.op("act", lambda: S.activation(out=U8[:, g0:g0 + 8, H * 128:(H + 1) * 128], in_=tu_ps[sl][:], func=AF.Copy),
                     reads=[("tu_ps", sl)], writes=["U8"])
        P.barrier()
    dump("U8", U8[:], [128, 32, 256], BF16)
    stU.close()
    stEW = ExitStack()
    wout_b = sbr("wout_b", [128, 8, 1024], BF16, stEW)
    gpm = sbr("gpm", [128, D], F32, stEW)
    gpf = sbr("gpf", [128, D], F32, stEW)
    sem_pe_ = P.new_sem("parE")
    sem_pew = P.new_sem("parEw")
    P.op("sp", lambda: nc.sync.dma_start(out=gpm[:], in_=gpm_d), writes=["gpm"], dma_sem=sem_pe_, group=True)
    P.op("sp", lambda: nc.sync.dma_start(out=gpf[:], in_=gpf_d), writes=["gpf"], dma_sem=sem_pe_, group=True)
    for k in range(8):
        P.op("pool", lambda: nc.gpsimd.dma_start(out=wout_b[:, k, :], in_=wout_d[:, k, :]), writes=[("wout_b", k)], dma_sem=sem_pew, group=True)

    with ExitStack() as stS:
        S_ps = [ps(f"S_ps{i}", [128, 2, 256], F32, stS) for i in range(2)]
        Y_ps = [ps(f"Y_ps{i}", [128, 2, 4, 128], F32, stS) for i in range(2)]
        ty_ps = [ps(f"ty_ps{i}", [128, 8, 128], BF16, stS) for i in range(2)]

        def mk(name, dt=F32):
            return [sb(f"{name}{i}", [128, 2, 256], dt, stS) for i in range(2)]

        tA, tB, Sm, Xt, dA, dB = mk("tA"), mk("tB"), mk("Sm"), mk("Xt"), mk("dA"), mk("dB")
        Xprev = mk("Xprev", BF16)
        y_tm2b = [sb(f"y_tm2b{i}", [128, 2, 8, 128], BF16, stS) for i in range(2)]
        for i in range(2):
            P.op("dve", lambda: V.memset(Xprev[i][:], 0.0), writes=[("Xprev", i)])

        def tabs(q):
            return (tab[:, 1, q, :].unsqueeze(1).to_broadcast([128, 2, 256]), tab[:, 0, q, :].unsqueeze(1).to_broadcast([128, 2, 256]))

        def s0(q):
            sl = q % 2
            for ri in range(2):
                for e in range(2):
                    g = 2 * q + e
                    P.op("pe", lambda: PE_.matmul(S_ps[sl][e * 64:(e + 1) * 64, ri, :], lhsT=WS_b[:, q, e, ri, :], rhs=U8[:, g, :], start=True, stop=True),
                         reads=["WS_b", "U8"], writes=[("S_ps", sl)])

        def s1(q):
            sl = q % 2
            cosb, sinb = tabs(q)
            P.op("dve", lambda: V.tensor_tensor(out=tA[sl][:], in0=S_ps[sl][:], in1=cosb, op=ALU.mult), reads=[("S_ps", sl), "tab"], writes=[("tA", sl)])
            P.op("dve", lambda: V.tensor_tensor(out=tB[sl][:], in0=S_ps[sl][:], in1=sinb, op=ALU.mult), reads=[("S_ps", sl), "tab"], writes=[("tB", sl)])

        def s2(q):
            sl = q % 2
            P.op("dve", lambda: V.tensor_tensor(out=Sm[sl][:, 0], in0=tA[sl][:, 0], in1=tB[sl][:, 1], op=ALU.add), reads=[("tA", sl), ("tB", sl)], writes=[("Sm", sl)])
            P.op("dve", lambda: V.tensor_tensor(out=Sm[sl][:, 1], in0=tA[sl][:, 1], in1=tB[sl][:, 0], op=ALU.subtract), reads=[("tA", sl), ("tB", sl)], writes=[("Sm", sl)])

        def s3(q):
            sl = q % 2
            for ri in range(2):
                P.op("dve", lambda: V.tensor_tensor_scan(out=Xt[sl][:, ri, :], data0=Amag[:, q:q + 1].to_broadcast([128, 256]), data1=Sm[sl][:, ri, :],
                                                         initial=0.0, op0=ALU.mult, op1=ALU.add), reads=[("Sm", sl), "Amag"], writes=[("Xt", sl)])

        def s4(q):
            sl = q % 2
            cosb, sinb = tabs(q)
            P.op("dve", lambda: V.tensor_tensor(out=dA[sl][:], in0=Xt[sl][:], in1=cosb, op=ALU.mult), reads=[("Xt", sl), "tab"], writes=[("dA", sl)])
            P.op("dve", lambda: V.tensor_tensor(out=dB[sl][:], in0=Xt[sl][:], in1=sinb, op=ALU.mult), reads=[("Xt", sl), "tab"], writes=[("dB", sl)])

        def s5(q):
            sl = q % 2
            P.op("dve", lambda: V.tensor_tensor(out=Xprev[sl][:, 0, 1:256], in0=dA[sl][:, 0, 0:255], in1=dB[sl][:, 1, 0:255], op=ALU.subtract),
                 reads=[("dA", sl), ("dB", sl)], writes=[("Xprev", sl)])
            P.op("dve", lambda: V.tensor_tensor(out=Xprev[sl][:, 1, 1:256], in0=dA[sl][:, 1, 0:255], in1=dB[sl][:, 0, 0:255], op=ALU.add),
                 reads=[("dA", sl), ("dB", sl)], writes=[("Xprev", sl)])

        def s6(q):
            sl = q % 2
            bt = q // 4
            yb = y_tm2b[bt % 2]
            for H in range(2):
                hs = slice(H * 128, (H + 1) * 128)
                for e in range(2):
                    g = 2 * q + e
                    pp = slice(e * 64, (e + 1) * 64)
                    yo = Y_ps[sl][:, e, H, :]
                    P.op("pe", lambda: PE_.matmul(yo, lhsT=U8[:, g, hs], rhs=Toep_b[:, q, e, :], start=True, stop=False),
                         reads=["U8", "Toep_b"], writes=[("Y_ps", sl)])
                    P.op("pe", lambda: PE_.matmul(yo, lhsT=Xprev[sl][pp, 0, hs], rhs=CA_b[pp, q, 0, :], start=False, stop=False),
                         reads=[("Xprev", sl), "CA_b"], writes=[("Y_ps", sl)])
                    P.op("pe", lambda: PE_.matmul(yo, lhsT=Xprev[sl][pp, 1, hs], rhs=CA_b[pp, q, 1, :], start=False, stop=True),
                         reads=[("Xprev", sl), "CA_b"], writes=[("Y_ps", sl)])
            for e in range(2):
                c0 = (q % 4) * 32 + e * 16
                P.op("act", lambda: S.activation(out=yb[:, :, :, c0:c0 + 16], in_=Y_ps[sl][:, e, 0:2].rearrange("p H (j h) -> p H j h", j=8), func=AF.Gelu_apprx_tanh),
                     reads=[("Y_ps", sl)], writes=[("y_tm2b", bt % 2)])

        def s7(q):
            if q % 4 != 3:
                return
            bt = q // 4
            yb = y_tm2b[bt % 2]
            for H in range(2):
                sl = H
                for j in range(8):
                    P.op("pe", lambda: PE_.transpose(out=ty_ps[sl][:, j, :], in_=yb[:, H, j, :], identity=ident_b[:]),
                         reads=[("y_tm2b", bt % 2), "ident_b"], writes=[("ty_ps", sl)])
                P.op("act", lambda: S.activation(out=yT[:, bt, H * 1024:(H + 1) * 1024], in_=ty_ps[sl][:].rearrange("p j c -> p (j c)"), func=AF.Copy),
                     reads=[("ty_ps", sl)], writes=[("yT", H)])

        _pipeline(16, [s0, s1, s2, s3, s4, s5, s6, s7])
        P.barrier()
    dump("yT", yT[:], [128, 4, T], BF16)
    stD2.close()
    stTab.close()
    stEW2 = ExitStack()
    wdn_b = sbr("wdn_b", [128, NF, 1024], BF16, stEW2)
    sem_wdn = P.new_sem("wdn")
    for f in range(NF):
        P.op("pool", lambda: nc.gpsimd.dma_start(out=wdn_b[:, f, :], in_=wdn_d[:, f, :]), writes=[("wdn_b", f)], dma_sem=sem_wdn, group=True)

    with ExitStack() as stG:
        z_ps = [ps(f"z_ps{i}", [128, 1024], F32, stG) for i in range(2)]
        tm_ps = [ps(f"tm_ps{i}", [128, 8, 128], BF16, stG) for i in range(2)]
        ga = [sb(f"ga{i}", [128, 512], F32, stG) for i in range(3)]
        gb = [sb(f"gb{i}", [128, 512], F32, stG) for i in range(2)]
        sg = [sb(f"sg{i}", [128, 512], F32, stG) for i in range(2)]
        go = [sb(f"go{i}", [128, 512], F32, stG) for i in range(5)]
        gon = [sb(f"gon{i}", [128, 512], BF16, stG) for i in range(2)]
        junkG = sb("junkG", [128, 512], BF16, stG)
        ssqG = sb("ssqG", [128, NT], F32, stG)
        rstdG = sb("rstdG", [128, NT], F32, stG)

        def g_mm(tix):
            H, j = tix // 8, tix % 8
            sl = tix % 2
            cols = slice(H * 1024 + j * 128, H * 1024 + (j + 1) * 128)
            for k in range(4):
                P.op("pe", lambda: PE_.matmul(z_ps[sl][:, 0:512], lhsT=yT[:, k, cols], rhs=wglu_b[:, k, 0:512], start=(k == 0), stop=(k == 3)),
                     reads=[("yT", H), ("wglu_b", k)], writes=[("z_ps", sl)])
                P.op("pe", lambda: PE_.matmul(z_ps[sl][:, 512:1024], lhsT=yT[:, k, cols], rhs=wglu_b[:, k, 512:1024], start=(k == 0), stop=(k == 3)),
                     reads=[("yT", H), ("wglu_b", k)], writes=[("z_ps", sl)])

        def g_a(tix):
            sl = tix % 2
            P.op("dve", lambda: V.tensor_tensor(out=ga[tix % 3][:], in0=z_ps[sl][:, 0:512], in1=bglu[:, 0:512], op=ALU.add), reads=[("z_ps", sl), "bglu"], writes=[("ga", tix % 3)])
            P.op("dve", lambda: V.tensor_tensor(out=gb[tix % 2][:], in0=z_ps[sl][:, 512:1024], in1=bglu[:, 512:1024], op=ALU.add), reads=[("z_ps", sl), "bglu"], writes=[("gb", tix % 2)])

        def g_sig(tix):
            P.op("act", lambda: S.activation(out=sg[tix % 2][:], in_=gb[tix % 2][:], func=AF.Sigmoid), reads=[("gb", tix % 2)], writes=[("sg", tix % 2)])

        def g_mul(tix):
            r = tix % 5
            P.op("dve", lambda: V.tensor_tensor(out=go[r][:], in0=ga[tix % 3][:], in1=sg[tix % 2][:], op=ALU.mult), reads=[("ga", tix % 3), ("sg", tix % 2)], writes=[("go", r)])

        def g_sq(tix):
            r = tix % 5
            P.op("act", lambda: S.activation(out=junkG[:], in_=go[r][:], func=AF.Square, accum_out=ssqG[:, tix:tix + 1]), reads=[("go", r)], writes=["junkG", ("ssqG", tix)])

        def g_r1(tix):
            P.op("dve", lambda: V.tensor_scalar(out=rstdG[:, tix:tix + 1], in0=ssqG[:, tix:tix + 1], scalar1=float(1.0 / 512), scalar2=float(EPS),
                                                 op0=ALU.mult, op1=ALU.add), reads=[("ssqG", tix)], writes=[("rstdG", tix)])

        def g_r2(tix):
            P.op("pool", lambda: G.tensor_tensor(out=rstdG[:, tix:tix + 1], in0=rstdG[:, tix:tix + 1], in1=nhalf[:, 0:1], op=ALU.pow),
                 reads=[("rstdG", tix), "nhalf"], writes=[("rstdG", tix)])

        def g_n(tix):
            r = tix % 5
            P.op("dve", lambda: V.scalar_tensor_tensor(out=gon[tix % 2][:], in0=go[r][:], scalar=rstdG[:, tix:tix + 1], in1=gssm[:], op0=ALU.mult, op1=ALU.mult),
                 reads=[("go", r), ("rstdG", tix), "gssm"], writes=[("gon", tix % 2)])

        def g_t(tix):
            sl = tix % 2
            for c in range(4):
                P.op("pe", lambda: PE_.transpose(out=tm_ps[sl][:, c, :], in_=gon[tix % 2][:, c * 128:(c + 1) * 128], identity=ident_b[:]),
                     reads=[("gon", tix % 2), "ident_b"], writes=[("tm_ps", sl)])

        def g_d(tix):
            H, j = tix // 8, tix % 8
            sl = tix % 2
            P.op("act", lambda: S.activation(out=mergedT[:, 0:4, tix * 128:(tix + 1) * 128], in_=tm_ps[sl][:, 0:4, :], func=AF.Copy),
                 reads=[("tm_ps", sl)], writes=[("mergedT_s", tix)])

        _pipeline(NT, [g_mm, g_a, g_sig, g_mul, g_sq, g_r1, g_r2, g_n, g_t, g_d])
        P.barrier()
    dump("mergedT2", mergedT[:], [128, 8, T], BF16)
    stYT.close()
    stD.close()
    stPO.close()

    stE = ExitStack()
    wgu_s = [sb(f"wgu_s{i}", [128, 2, 8, 256], BF16, stE) for i in range(3)]
    h1 = sb("h1", [128, 5, D], F32, stE)
    hn2 = [sb(f"hn2_{i}", [128, D], BF16, stE) for i in range(2)]
    hn2T = sb("hn2T", [128, 8, 512], BF16, stE)
    actT = sb("actT", [128, NF, 512], BF16, stE)
    sgt = [sb(f"sgt{i}", [128, 512], F32, stE) for i in range(2)]
    junkE = sb("junkE", [128, D], BF16, stE)
    ot = [sb(f"ot{i}", [128, D], F32, stE) for i in range(2)]
    ssqE = sb("ssqE", [128, 3 * NT], F32, stE)
    rstdE = sb("rstdE", [128, 3 * NT], F32, stE)
    eps_ = ps("eps", [128, 7, 512], F32, stE)
    tp2_ps = ps("tp2_ps", [128, 8, 128], BF16, stE)

    def tok(j):
        return eps_[:, 2 * j:2 * j + 2, :].rearrange("p a c -> p (a c)"), [("bank", 2 * j), ("bank", 2 * j + 1)]

    def gub(i):
        return eps_[:, 4 + i, :], [("bank", 4 + i)]

    sem_wgu = [P.new_sem(f"wgu{i}") for i in range(3)]
    sem_h1 = [P.new_sem(f"h1_{i}") for i in range(5)]
    sem_out = [P.new_sem(f"outst{i}") for i in range(2)]
    wgu_cnt = [0]
    wgu_q = []

    def issue_wgu(fp):
        slot = wgu_cnt[0] % 3
        wgu_cnt[0] += 1
        P.op("pool", lambda: nc.gpsimd.dma_start(out=wgu_s[slot][:].rearrange("p a k c -> p (a k c)"), in_=wgu_d[fp]),
             writes=[("wgu_s", slot)], dma_sem=sem_wgu[slot])
        wgu_q.append(slot)

    def rstd_of(col, dim):
        P.op("dve", lambda: V.tensor_scalar(out=rstdE[:, col:col + 1], in0=ssqE[:, col:col + 1], scalar1=float(1.0 / dim), scalar2=float(EPS),
                                             op0=ALU.mult, op1=ALU.add), reads=[("ssqE", col)], writes=[("rstdE", col)])
        P.op("pool", lambda: G.tensor_tensor(out=rstdE[:, col:col + 1], in0=rstdE[:, col:col + 1], in1=nhalf[:, 0:1], op=ALU.pow),
             reads=[("rstdE", col), "nhalf"], writes=[("rstdE", col)])

    tok_rot = [0]
    tok_of = {}
    ot_rot = [0]
    ot_of = {}
    hn_rot = [0]
    hn_of = {}
    gu_rot = [0]
    sg_rot = [0]

    def a1(B, t):
        n = 4 * B + t
        hs = n % 5
        H, j = n // 8, n % 8
        tl = slice(n * 128, (n + 1) * 128)
        ta = slice(H * 1024 + j, (H + 1) * 1024, 8)
        j3 = tok_rot[0] % 3
        tok_rot[0] += 1
        tok_of[("a", n)] = j3
        mo, mreg = tok(j3)
        P.op("sp", lambda: nc.sync.dma_start(out=h1[:, hs, :], in_=x_d[ta, :]), writes=[("h1", hs)], dma_sem=sem_h1[hs])
        rda = [("mergedT_a", n2) for n2 in range(H * 8, H * 8 + 8)]
        for k in range(8):
            cols = tl if k < 4 else ta
            rd = ([("mergedT_s", n)] if k < 4 else rda) + [("wout_b", k)]
            P.op("pe", lambda: PE_.matmul(mo[:, 0:512], lhsT=mergedT[:, k, cols], rhs=wout_b[:, k, 0:512], start=(k == 0), stop=(k == 7)),
                 reads=rd, writes=mreg)
            P.op("pe", lambda: PE_.matmul(mo[:, 512:1024], lhsT=mergedT[:, k, cols], rhs=wout_b[:, k, 512:1024], start=(k == 0), stop=(k == 7)),
                 reads=rd, writes=mreg)

    def a2(B, t):
        n = 4 * B + t
        hs = n % 5
        mo, mreg = tok(tok_of[("a", n)])
        r = ot_rot[0] % 2
        ot_rot[0] += 1
        c1 = n
        P.op("act", lambda: S.activation(out=junkE[:], in_=mo, func=AF.Square, accum_out=ssqE[:, c1:c1 + 1]),
             reads=mreg, writes=["junkE", ("ssqE", c1)])
        rstd_of(c1, D)
        P.op("dve", lambda: V.scalar_tensor_tensor(out=ot[r][:], in0=mo, scalar=rstdE[:, c1:c1 + 1], in1=gpm[:], op0=ALU.mult, op1=ALU.mult),
             reads=mreg + [("rstdE", c1), "gpm"], writes=[("ot", r)])
        P.op("dve", lambda: V.tensor_tensor(out=h1[:, hs, :], in0=h1[:, hs, :], in1=ot[r][:], op=ALU.add), reads=[("h1", hs), ("ot", r)], writes=[("h1", hs)])

    def a3(B, t):
        n = 4 * B + t
        hs = n % 5
        r = hn_rot[0] % 2
        hn_rot[0] += 1
        hn_of[n] = r
        c2 = NT + n
        P.op("act", lambda: S.activation(out=junkE[:], in_=h1[:, hs, :], func=AF.Square, accum_out=ssqE[:, c2:c2 + 1]),
             reads=[("h1", hs)], writes=["junkE", ("ssqE", c2)])
        rstd_of(c2, D)
        P.op("act", lambda: S.activation(out=hn2[r][:], in_=h1[:, hs, :], func=AF.Copy, scale=rstdE[:, c2:c2 + 1]),
             reads=[("h1", hs), ("rstdE", c2)], writes=[("hn2", r)])

    def a4(B, t):
        n = 4 * B + t
        r = hn_of[n]
        for k in range(8):
            P.op("pe", lambda: PE_.transpose(out=tp2_ps[:, k, :], in_=hn2[r][:, k * 128:(k + 1) * 128], identity=ident_b[:]),
                 reads=[("hn2", r), "ident_b"], writes=["tp2_ps"])
        P.op("dve", lambda: V.tensor_tensor(out=hn2T[:, :, t * 128:(t + 1) * 128], in0=tp2_ps[:],
                                             in1=gpffn[:, :].unsqueeze(2).to_broadcast([128, 8, 128]), op=ALU.mult),
             reads=["tp2_ps", "gpffn"], writes=[("hn2T", t)])

    def b1(B, t):
        n = 4 * B + t
        j = tok_rot[0] % 3
        tok_rot[0] += 1
        tok_of[("b", n)] = j
        dn, dreg = tok(j)
        for f in range(NF):
            P.op("pe", lambda: PE_.matmul(dn[:, 0:512], lhsT=actT[:, f, t * 128:(t + 1) * 128], rhs=wdn_b[:, f, 0:512], start=(f == 0), stop=(f == NF - 1)),
                 reads=[("actT", f), ("wdn_b", f)], writes=dreg)
            P.op("pe", lambda: PE_.matmul(dn[:, 512:1024], lhsT=actT[:, f, t * 128:(t + 1) * 128], rhs=wdn_b[:, f, 512:1024], start=(f == 0), stop=(f == NF - 1)),
                 reads=[("actT", f), ("wdn_b", f)], writes=dreg)

    def b2(B, t):
        n = 4 * B + t
        hs = n % 5
        tl = slice((n // 8) * 1024 + (n % 8), (n // 8 + 1) * 1024, 8)
        dn, dreg = tok(tok_of[("b", n)])
        r = ot_rot[0] % 2
        ot_rot[0] += 1
        c3 = 2 * NT + n
        P.op("act", lambda: S.activation(out=junkE[:], in_=dn, func=AF.Square, accum_out=ssqE[:, c3:c3 + 1]),
             reads=dreg, writes=["junkE", ("ssqE", c3)])
        rstd_of(c3, D)
        P.op("dve", lambda: V.scalar_tensor_tensor(out=ot[r][:], in0=dn, scalar=rstdE[:, c3:c3 + 1], in1=gpf[:], op0=ALU.mult, op1=ALU.mult),
             reads=dreg + [("rstdE", c3), "gpf"], writes=[("ot", r)])
        P.op("dve", lambda: V.tensor_tensor(out=ot[r][:], in0=ot[r][:], in1=h1[:, hs, :], op=ALU.add), reads=[("h1", hs), ("ot", r)], writes=[("ot", r)])
        P.op("sp", lambda: nc.sync.dma_start(out=out_d[tl, :], in_=ot[r][:]), reads=[("ot", r)], writes=[("out", n)], dma_sem=sem_out[r])

    def e2(B):
        for fp in range(11):
            if fp + 2 < 11:
                issue_wgu(fp + 2)
            elif B < 3:
                issue_wgu(fp + 2 - 11)
            slot = wgu_q.pop(0)
            for fl in range(2):
                f = 2 * fp + fl
                gi = gu_rot[0] % 3
                ui = (gu_rot[0] + 1) % 3
                gu_rot[0] += 2
                gps, greg = gub(gi)
                ups, ureg = gub(ui)
                for k in range(8):
                    P.op("pe", lambda: PE_.matmul(gps, lhsT=wgu_s[slot][:, fl, k, 0:128], rhs=hn2T[:, k, :], start=(k == 0), stop=(k == 7)),
                         reads=[("wgu_s", slot)] + [("hn2T", t) for t in range(4)], writes=greg)
                for k in range(8):
                    P.op("pe", lambda: PE_.matmul(ups, lhsT=wgu_s[slot][:, fl, k, 128:256], rhs=hn2T[:, k, :], start=(k == 0), stop=(k == 7)),
                         reads=[("wgu_s", slot)] + [("hn2T", t) for t in range(4)], writes=ureg)
                si = sg_rot[0] % 2
                sg_rot[0] += 1
                P.op("act", lambda: S.activation(out=sgt[si][:], in_=gps, func=AF.Silu), reads=greg, writes=[("sgt", si)])
                P.op("dve", lambda: V.tensor_tensor(out=actT[:, f, :], in0=ups, in1=sgt[si][:], op=ALU.mult),
                     reads=ureg + [("sgt", si)], writes=[("actT", f)])

    issue_wgu(0)
    issue_wgu(1)
    _pipeline(4, [lambda t: a1(0, t), lambda t: a2(0, t), lambda t: a3(0, t), lambda t: a4(0, t)])
    for B in range(4):
        e2(B)
        if B < 3:
            for it in range(6):
                if 1 <= it < 5:
                    b2(B, it - 1)
                if it < 4:
                    a1(B + 1, it)
                    b1(B, it)
                    a2(B + 1, it)
                if 1 <= it < 5:
                    a3(B + 1, it - 1)
                if 2 <= it < 6:
                    a4(B + 1, it - 2)
        else:
            _pipeline(4, [lambda t: b1(B, t), lambda t: b2(B, t)])
    P.barrier()
    stE.close()
    stEW2.close()
    stEW.close()

    P.finish()
    es.close()
    P.es.close()
    return P, dbg_d


def _prep_shared(inp):
    f = np.float32
    sh = {}

    def fm(v, k):
        return np.ascontiguousarray(np.asarray(v, f).reshape(k, 128).T)

    def rep(v):
        v = np.asarray(v, f).reshape(1, -1)
        return np.ascontiguousarray(np.broadcast_to(v, (128, v.shape[1])))

    gpre_h = fm(inp["g_pre_mix"][0], 8)
    gpffn_h = fm(inp["g_pre_ffn"][0], 8)
    sh["gpm"] = rep(inp["g_post_mix"][0])
    sh["gpf"] = rep(inp["g_post_ffn"][0])
    sh["gssm"] = rep(inp["g_ssm_out"][0])
    gattn_h = rep(inp["g_attn_out"][0])
    sh["bglu"] = rep(inp["b_glu"][0])
    sink_h = rep(inp["attn_sinks"][0])
    sh["win"] = np.ascontiguousarray(np.asarray(inp["w_in"][0], f).reshape(8, 128, 1280).transpose(1, 0, 2))
    sh["wglu"] = np.ascontiguousarray(np.asarray(inp["w_glu"][0], f).reshape(4, 128, 1024).transpose(1, 0, 2))
    sh["wout"] = np.ascontiguousarray(np.asarray(inp["w_out"][0], f).reshape(8, 128, 1024).transpose(1, 0, 2))
    wgu = np.asarray(inp["w_gate_up"][0], f)
    wg = wgu[:, :DFF].reshape(8, 128, NF, 128)
    wu = wgu[:, DFF:].reshape(8, 128, NF, 128)
    w2 = np.concatenate([wg, wu], axis=3)
    w2 = w2.transpose(2, 1, 0, 3).reshape(11, 2, 128, 8, 256)
    sh["wgu"] = np.ascontiguousarray(w2.transpose(0, 2, 1, 3, 4).reshape(11, 128, 2 * 8 * 256))
    sh["wdn"] = np.ascontiguousarray(np.asarray(inp["w_down"][0], f).reshape(NF, 128, 1024).transpose(1, 0, 2))

    def gl(v):
        v = np.asarray(v, f)
        rest = v.shape[2:]
        v = v.reshape(16, 2, 64, *rest)
        v = np.moveaxis(v, 0, 2)
        return np.ascontiguousarray(v.reshape(128, 16, *rest))

    lre_h = gl(inp["ssm_lambda_re"][0])
    lim_h = gl(inp["ssm_lambda_im"][0])
    ldt_h = gl(np.broadcast_to(np.asarray(inp["ssm_log_dt"][0], f)[:, None], (32, 64)))
    bre_h = gl(inp["ssm_b_re"][0])
    bim_h = gl(inp["ssm_b_im"][0])
    cre_h = gl(np.asarray(inp["ssm_c_re"][0], f).transpose(0, 2, 1))
    cim_h = gl(np.asarray(inp["ssm_c_im"][0], f).transpose(0, 2, 1))
    d = np.asarray(inp["ssm_d"][0], f)
    dl_h = np.ascontiguousarray(np.broadcast_to(d.T[None, :, :], (8, 16, 32)).reshape(128, 32))
    sh["pkB"] = np.ascontiguousarray(np.concatenate([lre_h, lim_h, ldt_h, bre_h.reshape(128, 256), bim_h.reshape(128, 256),
                                                     cre_h.reshape(128, 256), cim_h.reshape(128, 256)], axis=1))
    sh["c_ident"] = np.eye(128, dtype=f)
    kk = np.arange(128)[:, None]
    qq = np.arange(128)[None, :]
    am = np.zeros((128, 2, 2, 2, 128), f)
    NEG = -30000.0
    am[:, :, 0, :, :] = np.where(qq >= kk, 0.0, NEG)[:, None, None, :]
    am[:, :, 1, :, :] = np.where(kk > qq, 0.0, NEG)[:, None, None, :]
    sh["c_amask"] = am
    ii = (np.arange(128) // 16)[:, None]
    jj = (np.arange(128) // 16)[None, :]
    tmask_h = (jj >= ii).astype(f)
    kv = np.concatenate([7 - np.arange(8), np.arange(8) - 7, np.arange(8) + 1]).astype(f)
    kv_h = np.broadcast_to(kv[None, :], (128, 24))
    cidx_h = np.broadcast_to(np.arange(256, dtype=f)[None, :], (128, 256))
    invf = (np.float32(500000.0) ** (-(np.arange(8, dtype=f) * np.float32(2.0) / np.float32(16.0)))).astype(f)
    invf_h = np.broadcast_to(invf[None, :], (128, 8))
    sh["pkA"] = np.ascontiguousarray(np.concatenate([sh["c_ident"], tmask_h, kv_h, cidx_h, invf_h, gpre_h, gpffn_h, gattn_h, sink_h, dl_h], axis=1).astype(f))
    return sh


def _in_maps(inp):
    sh = _prep_shared(inp)
    x = np.asarray(inp["x"], np.float32)
    pos = np.asarray(inp["positions"], np.int32)
    maps = []
    for b in range(8):
        m = dict(sh)
        m["x"] = np.ascontiguousarray(x[b])
        m["pos"] = np.ascontiguousarray(pos[b].reshape(NT, 128).T)
        maps.append(m)
    return maps


_CACHE = {}


def _get_nc(dbg=()):
    key = tuple(dbg)
    if key not in _CACHE:
        nc0 = bass.Bass("TRN2", target_bir_lowering=False)
        P0, _ = _build(nc0, None, dbg)
        plan = P0.get_plan()
        nc = bass.Bass("TRN2", target_bir_lowering=False)
        _, dbg_d = _build(nc, plan, dbg)
        _CACHE[key] = (nc, dbg_d)
    return _CACHE[key]


def kernel(**inputs):
    nc, _ = _get_nc()
    maps = _in_maps(inputs)
    res = run_bass_kernel_spmd(nc, maps, core_ids=list(range(8)))
    out = np.stack([np.asarray(r["out"], np.float32) for r in res.results], axis=0)
    return out
```

```python
import numpy as np
from contextlib import ExitStack
import concourse.bass as bass
import concourse.mybir as mybir
from concourse.bass_utils import run_bass_kernel_spmd

F32, BF16, I32 = mybir.dt.float32, mybir.dt.bfloat16, mybir.dt.int32
AF = mybir.ActivationFunctionType
ALU = mybir.AluOpType

T = 2048
D = 1024
NT = 16
DFF = 2816
NF = 22
EPS = 1e-6
TWO_PI = 2.0 * np.pi


class Prog:
    ENG = ("pe", "act", "dve", "pool", "sp")

    def __init__(self, nc, plan):
        self.nc = nc
        self.plan = plan
        self.emit = plan is not None
        self.engs = {"pe": nc.tensor, "act": nc.scalar, "dve": nc.vector, "pool": nc.gpsimd, "sp": nc.sync}
        self.ops = []
        self.last_w = {}
        self.readers = {}
        self.need = set()
        self.group_total = {}
        self.sems = {}
        self.cnt = {}
        self.wm = {e: {} for e in self.ENG}
        self.ev = {}
        self.es = ExitStack()
        self.deferred = None
        self.backlog = []
        for e in ("pe", "act", "dve", "pool"):
            self.new_sem(e)

    def new_sem(self, key):
        self.sems[key] = self.es.enter_context(self.nc.semaphore("s_" + str(key)))
        self.cnt[key] = 0
        return key

    def _deps(self, engine, is_dma, reads, writes):
        raw = set()
        oth = set()
        for r in reads:
            if r in self.last_w:
                raw.add(self.last_w[r])
        for w in writes:
            if w in self.last_w:
                oth.add(self.last_w[w])
            oth.update(self.readers.get(w, ()))
        out = list(raw) if engine != "pe" else [d for d in raw if self.ops[d][0] != "pe" or self.ops[d][1]]
        for d in oth:
            if d in raw:
                continue
            pe, pd = self.ops[d]
            if (not is_dma) and (not pd) and pe == engine and engine == "pe":
                continue
            out.append(d)
        return out

    def _wait(self, engine, d):
        semkey, value, clock = self.ev[d]
        wm = self.wm[engine]
        if wm.get(semkey, 0) >= value:
            return
        self.engs[engine].wait_ge(self.sems[semkey], value)
        for k, v in clock.items():
            if wm.get(k, 0) < v:
                wm[k] = v
        wm[semkey] = value

    def op(self, engine, fn, reads=(), writes=(), dma_sem=None, group=False, _now=False, cost=0.3):
        if self.deferred is not None and not _now and dma_sem is None:
            self.deferred.append((engine, fn, reads, writes, cost))
            return None
        i = len(self.ops)
        is_dma = dma_sem is not None
        deps = self._deps(engine, is_dma, reads, writes)
        self.ops.append((engine, is_dma))
        for r in reads:
            self.readers.setdefault(r, []).append(i)
        for w in writes:
            self.last_w[w] = i
            self.readers[w] = []
        for d in deps:
            self.need.add(d)
        if is_dma:
            self.group_total[dma_sem] = self.group_total.get(dma_sem, 0) + 16
        if not self.emit:
            return i
        for d in sorted(deps):
            self._wait(engine, d)
        ins = fn()
        if is_dma:
            ins.then_inc(self.sems[dma_sem], 16)
            self.cnt[dma_sem] += 16
            val = self.plan["group_total"][dma_sem] if group else self.cnt[dma_sem]
            clock = dict(self.wm[engine])
            self.ev[i] = (dma_sem, val, clock)
        elif i in self.plan["need"]:
            ins.then_inc(self.sems[engine], 1)
            self.cnt[engine] += 1
            self.ev[i] = (engine, self.cnt[engine], dict(self.wm[engine]))
        return i

    def defer_begin(self):
        self.deferred = []

    def defer_end(self):
        self.backlog.extend(self.deferred)
        self.deferred = None

    def flush(self, k=None):
        budget = 1e9 if k is None else float(k)
        while self.backlog and budget > 0:
            engine, fn, reads, writes, cost = self.backlog.pop(0)
            self.op(engine, fn, reads, writes, _now=True)
            budget -= cost

    def barrier(self):
        last = {}
        for i, (e, is_dma) in enumerate(self.ops):
            if is_dma:
                last[("dma", i)] = i
            else:
                last[e] = i
        idxs = sorted(set(last.values()))
        for d in idxs:
            self.need.add(d)
        if not self.emit:
            return
        for e in self.ENG:
            for d in idxs:
                pe, pd = self.ops[d]
                if (not pd) and pe == e:
                    continue
                if d in self.ev:
                    self._wait(e, d)

    def finish(self):
        self.barrier()

    def get_plan(self):
        return {"need": set(self.need), "group_total": dict(self.group_total)}


def _pipeline(n_items, stages, hook=None):
    ns = len(stages)
    for it in range(n_items + ns - 1):
        for k, st in enumerate(stages):
            i = it - k
            if 0 <= i < n_items:
                st(i)
        if hook is not None:
            hook(it)


def _build(nc, plan, dbg=()):
    P = Prog(nc, plan)
    es = ExitStack()

    def dram_in(name, shape, dt=F32):
        return nc.dram_tensor(name, list(shape), dt, kind="ExternalInput").ap()

    x_d = dram_in("x", [T, D])
    pos_d = dram_in("pos", [128, NT], I32)
    gpm_d = dram_in("gpm", [128, D])
    gpf_d = dram_in("gpf", [128, D])
    gssm_d = dram_in("gssm", [128, 512])
    bglu_d = dram_in("bglu", [128, 1024])
    win_d = dram_in("win", [128, 8, 1280])
    wglu_d = dram_in("wglu", [128, 4, 1024])
    wout_d = dram_in("wout", [128, 8, 1024])
    wgu_d = dram_in("wgu", [11, 128, 2 * 8 * 256])
    wdn_d = dram_in("wdn", [128, NF, 1024])
    ident_d = dram_in("c_ident", [128, 128])
    pkA_d = dram_in("pkA", [128, 1112])
    pkB_d = dram_in("pkB", [128, 1072])
    amask_d = dram_in("c_amask", [128, 2, 2, 2, 128])
    out_d = nc.dram_tensor("out", [T, D], F32, kind="ExternalOutput").ap()
    dbg_d = {}

    def sb(name, shape, dt=F32, stack=None, side="left"):
        return (stack or es).enter_context(nc.sbuf_tensor("sb_" + name, list(shape), dt, side=side))

    def sbr(name, shape, dt=F32, stack=None):
        return sb(name, shape, dt, stack, side="right")

    def ps(name, shape, dt=F32, stack=None):
        return (stack or es).enter_context(nc.psum_tensor("ps_" + name, list(shape), dt))

    V, S, G, PE_ = nc.vector, nc.scalar, nc.gpsimd, nc.tensor

    def dump(name, ap_sb, shape, dt=F32, region=None):
        if name not in dbg:
            return
        d = nc.dram_tensor("dbg_" + name, list(shape), dt, kind="ExternalOutput").ap()
        dbg_d[name] = d
        sk = P.new_sem("dbg_" + name)
        P.barrier()
        P.op("sp", lambda: nc.sync.dma_start(out=d, in_=ap_sb), reads=[], writes=[("dbg", name)], dma_sem=sk)
        P.barrier()

    PKA = [("ident_f", 128), ("tmask", 128), ("kvc", 24), ("cidx", 256), ("invf", 8), ("gpre", 8), ("gpffn", 8), ("gattn", 512), ("sinkexp", 8), ("dl", 32)]
    pkA_sb = sb("pkA", [128, sum(w for _, w in PKA)])
    _v = {}
    _o = 0
    for _n, _w in PKA:
        _v[_n] = pkA_sb[:, _o:_o + _w]
        _o += _w
    ident_f, tmask, kvc, cidx, invf, gpre, gpffn, gattn, sinkexp, dl = [_v[n] for n, _ in PKA]
    ident_b = sb("ident_b", [128, 128], BF16)
    amask = sb("amask", [128, 2, 2, 2, 128], BF16)
    nhalf = sb("nhalf", [128, 1])
    posi = sb("posi", [128, NT], I32)
    actbuf = sb("actbuf", [128, 8, T], BF16)
    hnT = actbuf
    mergedT = actbuf
    stPO = ExitStack()
    WS_b = sb("WS_b", [128, 16, 2, 2, 64], BF16, stPO)
    Toep_b = sb("Toep_b", [128, 16, 2, 128], BF16, stPO)
    CA_b = sb("CA_b", [128, 16, 2, 128], BF16, stPO)
    Amag = sb("Amag", [128, 16], F32, stPO)
    phi = sb("phi", [128, 16], F32, stPO)
    stU = ExitStack()
    u_tm2 = sbr("u_tm2", [128, 2, 32, 8, 16], BF16, stU)
    stBC = ExitStack()
    qkT = sbr("qkT", [128, 6, T], BF16, stBC)
    Vsb = sbr("Vsb", [128, NT, 2, 65], BF16, stBC)

    sem_par = P.new_sem("par")

    def pload(dst, src, name, eng="sp"):
        P.op(eng, lambda: P.engs[eng].dma_start(out=dst, in_=src), writes=[name], dma_sem=sem_par, group=True)

    P.op("sp", lambda: nc.sync.dma_start(out=pkA_sb[:], in_=pkA_d), writes=[n for n, _ in PKA], dma_sem=sem_par, group=True)
    pload(posi[:], pos_d, "posi")
    sem_pw = P.new_sem("parw")

    def wload(dst, src, name):
        P.op("pool", lambda: nc.gpsimd.dma_start(out=dst, in_=src), writes=[name], dma_sem=sem_pw, group=True)

    wload(ident_b[:], ident_d, "ident_b")
    wload(amask[:], amask_d, "amask")

    stAB = ExitStack()
    win_b = sbr("win_b", [128, 8, 1280], BF16, stAB)
    for k in range(8):
        wload(win_b[:, k, 512:1280], win_d[:, k, 512:1280], ("win_b", k))
    sem_pw2 = P.new_sem("parw2")
    for k in range(8):
        P.op("pool", lambda: nc.gpsimd.dma_start(out=win_b[:, k, 0:512], in_=win_d[:, k, 0:512]), writes=[("win_u", k)], dma_sem=sem_pw2, group=True)

    P.op("dve", lambda: V.memset(nhalf[:], -0.5), writes=["nhalf"])
    P.op("act", lambda: S.activation(out=sinkexp[:], in_=sinkexp[:], func=AF.Exp), reads=["sinkexp"], writes=["sinkexp"])

    cs = sbr("cs", [128, 2, NT, 8], F32, stAB)
    dump("cs", cs[:], [128, 2, NT, 8])

    P.op("pool", lambda: G.memset(Vsb[:], 1.0), writes=["Vsb"])

    sem_pp = P.new_sem("parP")

    def ploadP(dst, src, name):
        P.op("sp", lambda: nc.sync.dma_start(out=dst, in_=src), writes=[name], dma_sem=sem_pp, group=True)

    stP = ExitStack()
    if True:
        pkB_sb = sb("pkB", [128, 1072], F32, stP, side="right")
        lre, lim, ldt = pkB_sb[:, 0:16], pkB_sb[:, 16:32], pkB_sb[:, 32:48]
        bre = pkB_sb[:, 48:304].rearrange("p (q h) -> p q h", q=16)
        bim = pkB_sb[:, 304:560].rearrange("p (q h) -> p q h", q=16)
        cre = pkB_sb[:, 560:816].rearrange("p (q h) -> p q h", q=16)
        cim = pkB_sb[:, 816:1072].rearrange("p (q h) -> p q h", q=16)
        P.op("sp", lambda: nc.sync.dma_start(out=pkB_sb[:], in_=pkB_d), writes=["lre", "lim", "ldt", "bre", "bim", "cre", "cim"], dma_sem=sem_pp, group=True)
        lr = sb("lr", [128, 16], F32, stP, side="right")
        dtt = sb("dtt", [128, 16], F32, stP, side="right")
        ldv = sb("ldv", [128, 16], F32, stP, side="right")
        th = sb("th", [128, 16], F32, stP, side="right")
        den = sb("denP", [128, 16], F32, stP, side="right")
        rdn = sb("rdnP", [128, 16], F32, stP, side="right")
        ski = sb("ski", [128, 16], I32, stP, side="right")
        s16 = [sb(f"s16_{i}", [128, 16], F32, stP, side="right") for i in range(4)]
        argm = sb("argm", [128, 16, 24], F32, stP, side="right")
        Emag = argm
        ett = sb("ett", [128, 2, 16, 24], F32, stP, side="right")
        eki = sb("eki", [128, 2, 16, 24], I32, stP, side="right")
        ekf = sb("ekf", [128, 2, 16, 24], F32, stP, side="right")
        Esc = ett
        Ere = sb("Ere", [128, 16, 24], F32, stP, side="right")
        Eim = sb("Eim", [128, 16, 24], F32, stP, side="right")
        fr = sb("fr", [128, 16], F32, stP, side="right")
        fi = sb("fi", [128, 16], F32, stP, side="right")
        Bre = sb("Bre", [128, 16, 16], F32, stP, side="right")
        Bim = sb("Bim", [128, 16, 16], F32, stP, side="right")
        b16 = [sb(f"b16_{i}", [128, 16, 16], F32, stP, side="right") for i in range(2)]
        t1 = sb("t1P", [128, 16, 8, 16], F32, stP, side="right")
        t2 = sb("t2P", [128, 16, 8, 16], F32, stP, side="right")
        WSt_b = sb("WSt_b", [128, 16, 2, 128], BF16, stP, side="right")
        CN_b = sb("CN_b", [128, 16, 2, 128], BF16, stP, side="right")
        tmpT = sb("tmpT", [128, 4, 128], F32, stP, side="right")

        posf = s16[0]
        tt = t1[:, 0:2].rearrange("p a b c -> p a (b c)").rearrange("p a (n f) -> p a n f", f=8)
        kf = t1[:, 2:4].rearrange("p a b c -> p a (b c)").rearrange("p a (n f) -> p a n f", f=8)
        ki = t2[:, 0:2].rearrange("p a b c -> p a (b c)").rearrange("p a (n f) -> p a n f", f=8).bitcast(I32)
        P.op("dve", lambda: V.tensor_copy(out=posf[:], in_=posi[:]), reads=["posi"], writes=["s16_0"])
        P.op("dve", lambda: V.tensor_tensor(out=tt[:, 0], in0=posf[:, :].unsqueeze(2).to_broadcast([128, NT, 8]),
                                             in1=invf[:, :].unsqueeze(1).to_broadcast([128, NT, 8]), op=ALU.mult),
             reads=["s16_0", "invf"], writes=["t1P"])
        P.op("dve", lambda: V.tensor_scalar(out=tt[:, 0], in0=tt[:, 0], scalar1=float(1.0 / TWO_PI), scalar2=None, op0=ALU.mult),
             reads=["t1P"], writes=["t1P"])
        P.op("dve", lambda: V.tensor_scalar(out=tt[:, 1], in0=tt[:, 0], scalar1=0.25, scalar2=None, op0=ALU.add),
             reads=["t1P"], writes=["t1P"])
        P.op("dve", lambda: V.tensor_copy(out=ki, in_=tt), reads=["t1P"], writes=["t2P"])
        P.op("dve", lambda: V.tensor_copy(out=kf, in_=ki), reads=["t2P"], writes=["t1P"])
        P.op("dve", lambda: V.tensor_tensor(out=tt, in0=tt, in1=kf, op=ALU.subtract), reads=["t1P"], writes=["t1P"])
        P.op("act", lambda: S.activation(out=cs[:], in_=tt, func=AF.Sin, scale=float(TWO_PI)), reads=["t1P"], writes=["cs"])

        P.defer_begin()

        def dv(fn, reads, writes, cost=0.3):
            P.op("dve", fn, reads=reads, writes=writes, cost=cost)

        def tt_(out, a, b, op, reads, writes, cost=0.3):
            dv(lambda: V.tensor_tensor(out=out, in0=a, in1=b, op=op), reads, writes, cost)

        dv(lambda: V.tensor_scalar(out=lr[:], in0=lre[:], scalar1=-1e-4, scalar2=None, op0=ALU.min), ["lre"], ["lr"])
        P.op("act", lambda: S.activation(out=dtt[:], in_=ldt[:], func=AF.Exp), reads=["ldt"], writes=["dtt"])
        tt_(ldv[:], lr[:], dtt[:], ALU.mult, ["lr", "dtt"], ["ldv"])
        tt_(th[:], lim[:], dtt[:], ALU.mult, ["lim", "dtt"], ["th"])
        tt_(s16[0][:], lr[:], lr[:], ALU.mult, ["lr"], ["s16_0"])
        tt_(s16[1][:], lim[:], lim[:], ALU.mult, ["lim"], ["s16_1"])
        tt_(den[:], s16[0][:], s16[1][:], ALU.add, ["s16_0", "s16_1"], ["denP"])
        dv(lambda: V.reciprocal(out=rdn[:], in_=den[:]), ["denP"], ["rdnP"])
        tt_(argm[:], ldv[:, :].unsqueeze(2).to_broadcast([128, 16, 24]), kvc[:, :].unsqueeze(1).to_broadcast([128, 16, 24]), ALU.mult,
            ["ldv", "kvc"], ["argm"])
        P.op("act", lambda: S.activation(out=Emag[:], in_=argm[:], func=AF.Exp), reads=["argm"], writes=["Emag"])
        tt_(ett[:, 0], th[:, :].unsqueeze(2).to_broadcast([128, 16, 24]), kvc[:, :].unsqueeze(1).to_broadcast([128, 16, 24]), ALU.mult,
            ["th", "kvc"], ["ett"])
        dv(lambda: V.tensor_scalar(out=ett[:, 0], in0=ett[:, 0], scalar1=float(1.0 / TWO_PI), scalar2=None, op0=ALU.mult), ["ett"], ["ett"])
        dv(lambda: V.tensor_scalar(out=ett[:, 1], in0=ett[:, 0], scalar1=0.25, scalar2=None, op0=ALU.add), ["ett"], ["ett"])
        dv(lambda: V.tensor_copy(out=eki[:], in_=ett[:]), ["ett"], ["eki"])
        dv(lambda: V.tensor_copy(out=ekf[:], in_=eki[:]), ["eki"], ["ekf"])
        tt_(ett[:], ett[:], ekf[:], ALU.subtract, ["ett", "ekf"], ["ett"])
        P.op("act", lambda: S.activation(out=Esc[:], in_=ett[:], func=AF.Sin, scale=float(TWO_PI)), reads=["ett"], writes=["Esc"])
        tt_(Ere[:], Emag[:], Esc[:, 1], ALU.mult, ["Emag", "Esc"], ["Ere"])
        tt_(Eim[:], Emag[:], Esc[:, 0], ALU.mult, ["Emag", "Esc"], ["Eim"])
        ar = Ere[:, :, 16]
        ai = Eim[:, :, 16]
        dv(lambda: V.tensor_scalar(out=s16[0][:], in0=ar, scalar1=-1.0, scalar2=None, op0=ALU.add), ["Ere"], ["s16_0"])
        tt_(s16[1][:], s16[0][:], lr[:], ALU.mult, ["s16_0", "lr"], ["s16_1"])
        tt_(s16[2][:], ai, lim[:], ALU.mult, ["Eim", "lim"], ["s16_2"])
        tt_(s16[1][:], s16[1][:], s16[2][:], ALU.add, ["s16_1", "s16_2"], ["s16_1"])
        tt_(fr[:], s16[1][:], rdn[:], ALU.mult, ["s16_1", "rdnP"], ["fr"])
        tt_(s16[2][:], ai, lr[:], ALU.mult, ["Eim", "lr"], ["s16_2"])
        tt_(s16[3][:], s16[0][:], lim[:], ALU.mult, ["s16_0", "lim"], ["s16_3"])
        tt_(s16[2][:], s16[2][:], s16[3][:], ALU.subtract, ["s16_2", "s16_3"], ["s16_2"])
        tt_(fi[:], s16[2][:], rdn[:], ALU.mult, ["s16_2", "rdnP"], ["fi"])
        frb = fr[:, :].unsqueeze(2).to_broadcast([128, 16, 16])
        fib = fi[:, :].unsqueeze(2).to_broadcast([128, 16, 16])
        tt_(b16[0][:], bre[:], frb, ALU.mult, ["bre", "fr"], ["b16_0"])
        tt_(b16[1][:], bim[:], fib, ALU.mult, ["bim", "fi"], ["b16_1"])
        tt_(Bre[:], b16[0][:], b16[1][:], ALU.subtract, ["b16_0", "b16_1"], ["Bre"])
        tt_(b16[0][:], bim[:], frb, ALU.mult, ["bim", "fr"], ["b16_0"])
        tt_(b16[1][:], bre[:], fib, ALU.mult, ["bre", "fi"], ["b16_1"])
        tt_(Bim[:], b16[0][:], b16[1][:], ALU.add, ["b16_0", "b16_1"], ["Bim"])

        def cprod(Er, Ei, Xr, Xi, out_re, out_im, neg_im, rn, wn):
            Erb = Er.unsqueeze(3).to_broadcast([128, 16, 8, 16])
            Eib = Ei.unsqueeze(3).to_broadcast([128, 16, 8, 16])
            Xrb = Xr.unsqueeze(2).to_broadcast([128, 16, 8, 16])
            Xib = Xi.unsqueeze(2).to_broadcast([128, 16, 8, 16])
            o_re = out_re.rearrange("p q (k h) -> p q k h", k=8)
            o_im = out_im.rearrange("p q (k h) -> p q k h", k=8)
            tt_(t1[:], Erb, Xrb, ALU.mult, rn, ["t1P"], 2.3)
            tt_(t2[:], Eib, Xib, ALU.mult, rn, ["t2P"], 2.3)
            tt_(o_re, t1[:], t2[:], ALU.subtract, ["t1P", "t2P"], [wn], 2.3)
            tt_(t1[:], Erb, Xib, ALU.mult, rn, ["t1P"], 2.3)
            tt_(t2[:], Eib, Xrb, ALU.mult, rn, ["t2P"], 2.3)
            if neg_im:
                dv(lambda: V.scalar_tensor_tensor(out=o_im, in0=t1[:], scalar=-1.0, in1=t2[:], op0=ALU.mult, op1=ALU.subtract), ["t1P", "t2P"], [wn], 2.3)
            else:
                tt_(o_im, t1[:], t2[:], ALU.add, ["t1P", "t2P"], [wn], 2.3)

        cprod(Ere[:, :, 0:8], Eim[:, :, 0:8], Bre[:], Bim[:], WSt_b[:, :, 0, :], WSt_b[:, :, 1, :], False, ["Ere", "Eim", "Bre", "Bim"], "WSt_b")
        cprod(Ere[:, :, 8:16], Eim[:, :, 8:16], cre[:], cim[:], CN_b[:, :, 0, :], CN_b[:, :, 1, :], True, ["Ere", "Eim", "cre", "cim"], "CN_b")
        cprod(Ere[:, :, 16:24], Eim[:, :, 16:24], cre[:], cim[:], CA_b[:, :, 0, :], CA_b[:, :, 1, :], True, ["Ere", "Eim", "cre", "cim"], "CA_b")
        dv(lambda: V.tensor_copy(out=Amag[:], in_=Emag[:, :, 23]), ["Emag"], ["Amag"])
        dv(lambda: V.tensor_scalar(out=s16[0][:], in0=th[:], scalar1=float(8.0 / TWO_PI), scalar2=None, op0=ALU.mult), ["th"], ["s16_0"])
        dv(lambda: V.tensor_copy(out=ski[:], in_=s16[0][:]), ["s16_0"], ["ski"])
        dv(lambda: V.tensor_copy(out=s16[1][:], in_=ski[:]), ["ski"], ["s16_1"])
        tt_(phi[:], s16[0][:], s16[1][:], ALU.subtract, ["s16_0", "s16_1"], ["phi"])
        P.defer_end()

    stA_ps = ExitStack()
    with ExitStack() as stA:
        NXB = 3
        x_t = [sb(f"x_t{i}", [128, D], F32, stA, side="right") for i in range(NXB)]
        xn = [sb(f"xn{i}", [128, D], BF16, stA, side="right") for i in range(2)]
        junk = sb("junkA", [128, D], BF16, stA, side="right")
        ssq = sb("ssqA", [128, NT], F32, stA, side="right")
        rstd = sb("rstdA", [128, NT], F32, stA, side="right")
        tp_ps = [ps(f"tp_ps{i}", [128, 8, 128], BF16, stA_ps) for i in range(2)]
        qkv_ps = [ps(f"qkv_ps{i}", [128, 1024], F32, stA_ps) for i in range(2)]
        tq_ps = [ps(f"tq_ps{i}", [128, 8, 128], BF16, stA_ps) for i in range(2)]
        qk_rot = [sb(f"qk_rot{i}", [128, 768], BF16, stA, side="right") for i in range(2)]
        rt = [sb(f"rope_t{i}", [128, 10, 8], F32, stA, side="right") for i in range(4)]
        semx = [P.new_sem(f"x{i}") for i in range(NXB)]

        def p0(n):
            s3 = n % NXB
            P.op("sp", lambda: nc.sync.dma_start(out=x_t[s3][:], in_=x_d[n * 128:(n + 1) * 128, :]),
                 writes=[("x_t", s3)], dma_sem=semx[s3])
            P.op("act", lambda: S.activation(out=junk[:], in_=x_t[s3][:], func=AF.Square, accum_out=ssq[:, n:n + 1]),
                 reads=[("x_t", s3)], writes=["junkA", ("ssqA", n)])

        def p0b(n):
            P.op("pool", lambda: G.tensor_scalar(out=rstd[:, n:n + 1], in0=ssq[:, n:n + 1], scalar1=float(1.0 / D), scalar2=float(EPS),
                                                  op0=ALU.mult, op1=ALU.add), reads=[("ssqA", n)], writes=[("rstdA", n)])
            P.op("pool", lambda: G.tensor_tensor(out=rstd[:, n:n + 1], in0=rstd[:, n:n + 1], in1=nhalf[:, 0:1], op=ALU.pow),
                 reads=[("rstdA", n), "nhalf"], writes=[("rstdA", n)])

        def p1(n):
            s3 = n % NXB
            s2 = n % 2
            P.op("act", lambda: S.activation(out=xn[s2][:], in_=x_t[s3][:], func=AF.Copy, scale=rstd[:, n:n + 1]),
                 reads=[("x_t", s3), ("rstdA", n)], writes=[("xn", s2)])

        def p1b(n):
            s2 = n % 2
            for k in range(8):
                P.op("pe", lambda: PE_.transpose(out=tp_ps[s2][:, k, :], in_=xn[s2][:, k * 128:(k + 1) * 128], identity=ident_b[:]),
                     reads=[("xn", s2), "ident_b"], writes=[("tp_ps", s2)])

        def p2(n):
            s2 = n % 2
            P.op("dve", lambda: V.tensor_tensor(out=hnT[:, :, n * 128:(n + 1) * 128], in0=tp_ps[s2][:],
                                                 in1=gpre[:, :].unsqueeze(2).to_broadcast([128, 8, 128]), op=ALU.mult),
                 reads=[("tp_ps", s2), "gpre"], writes=[("hnT", n)])

        def p3(n):
            s2 = n % 2
            tl = slice(n * 128, (n + 1) * 128)
            for k in range(8):
                P.op("pe", lambda: PE_.matmul(qkv_ps[s2][:, 0:512], lhsT=hnT[:, k, tl], rhs=win_b[:, k, 512:1024], start=(k == 0), stop=(k == 7)),
                     reads=[("hnT", n), ("win_b", k)], writes=[("qkv_ps", s2)])
                P.op("pe", lambda: PE_.matmul(qkv_ps[s2][:, 512:768], lhsT=hnT[:, k, tl], rhs=win_b[:, k, 1024:1280], start=(k == 0), stop=(k == 7)),
                     reads=[("hnT", n), ("win_b", k)], writes=[("qkv_ps", s2)])

        def p4(n):
            s2 = n % 2
            qk = qkv_ps[s2][:, 0:640].rearrange("p (h d) -> p h d", h=10)
            qr = qk_rot[s2][:, 0:640].rearrange("p (h d) -> p h d", h=10)
            cosb = cs[:, 1, n, :].unsqueeze(1).to_broadcast([128, 10, 8])
            sinb = cs[:, 0, n, :].unsqueeze(1).to_broadcast([128, 10, 8])
            rd = [("qkv_ps", s2), "cs"]
            P.op("dve", lambda: V.tensor_tensor(out=rt[0][:], in0=qk[:, :, 0:8], in1=cosb, op=ALU.mult), reads=rd, writes=["rt0"])
            P.op("dve", lambda: V.tensor_tensor(out=rt[1][:], in0=qk[:, :, 8:16], in1=sinb, op=ALU.mult), reads=rd, writes=["rt1"])
            P.op("dve", lambda: V.tensor_tensor(out=rt[2][:], in0=qk[:, :, 8:16], in1=cosb, op=ALU.mult), reads=rd, writes=["rt2"])
            P.op("dve", lambda: V.tensor_tensor(out=rt[3][:], in0=qk[:, :, 0:8], in1=sinb, op=ALU.mult), reads=rd, writes=["rt3"])
            P.op("dve", lambda: V.tensor_tensor(out=qr[:, :, 0:8], in0=rt[0][:], in1=rt[1][:], op=ALU.subtract),
                 reads=["rt0", "rt1"], writes=[("qk_rotA", s2)])
            P.op("dve", lambda: V.tensor_tensor(out=qr[:, :, 8:16], in0=rt[2][:], in1=rt[3][:], op=ALU.add),
                 reads=["rt2", "rt3"], writes=[("qk_rotB", s2)])
            P.op("act", lambda: S.activation(out=qr[:, :, 16:64], in_=qk[:, :, 16:64], func=AF.Copy),
                 reads=[("qkv_ps", s2), "rt0", "rt1", "rt2", "rt3"], writes=[("qk_rotC", s2)])
            P.op("act", lambda: S.activation(out=qk_rot[s2][:, 640:704], in_=qk_rot[s2][:, 512:576], func=AF.Copy),
                 reads=[("qk_rotA", s2), ("qk_rotB", s2), ("qk_rotC", s2)], writes=[("qk_rotD", s2)])
            P.op("act", lambda: S.activation(out=Vsb[:, n, :, 0:64], in_=qkv_ps[s2][:, 640:768].rearrange("p (h d) -> p h d", h=2), func=AF.Copy),
                 reads=[("qkv_ps", s2), "rt0", "rt1", "rt2", "rt3"], writes=["Vsb"])

        def p5a(n):
            s2 = n % 2
            tl = slice(n * 128, (n + 1) * 128)
            rdq = [("qk_rotA", s2), ("qk_rotB", s2), ("qk_rotC", s2), ("qk_rotD", s2), "ident_b"]
            for c in range(4):
                P.op("pe", lambda: PE_.transpose(out=tq_ps[s2][:, c, :], in_=qk_rot[s2][:, c * 128:(c + 1) * 128], identity=ident_b[:]),
                     reads=rdq, writes=[("tq_ps", s2)])
            P.op("pe", lambda: PE_.transpose(out=tq_ps[s2][:, 4, :], in_=qk_rot[s2][:, 512:640], identity=ident_b[:]),
                 reads=rdq, writes=[("tq_ps", s2)])
            P.op("pe", lambda: PE_.transpose(out=tq_ps[s2][:, 5, :], in_=qk_rot[s2][:, 576:704], identity=ident_b[:]),
                 reads=rdq, writes=[("tq_ps", s2)])

        def p5b(n):
            s2 = n % 2
            tl = slice(n * 128, (n + 1) * 128)
            P.op("act", lambda: S.activation(out=qkT[:, :, tl], in_=tq_ps[s2][:, 0:6, :], func=AF.Copy), reads=[("tq_ps", s2)], writes=[("qkT", n)])

        _pipeline(NT, [p0, p0b, p1, p1b, p2, p3, p4, p5a, p5b], hook=lambda it: P.flush(2.4))

        def u0(it):
            H, i = it // 8, it % 8
            s2 = it % 2
            for k in range(8):
                P.op("pe", lambda: PE_.matmul(qkv_ps[s2][:, 0:512], lhsT=hnT[:, k, H * 1024 + i:(H + 1) * 1024:8], rhs=win_b[:, k, 0:512],
                                              start=(k == 0), stop=(k == 7)),
                     reads=[("hnT", n2) for n2 in range(H * 8, H * 8 + 8)] + [("win_u", k)], writes=[("qkv_ps", s2)])

        def u1(it):
            H, i = it // 8, it % 8
            s2 = it % 2
            P.op("act", lambda: S.activation(out=u_tm2[:, H, :, i, :], in_=qkv_ps[s2][:, 0:512].rearrange("p (g h) -> p g h", g=32), func=AF.Copy),
                 reads=[("qkv_ps", s2)], writes=[("u_tm2", H)])

        _pipeline(16, [u0, u1], hook=lambda it: P.flush(2.6))
        P.flush()
        P.barrier()
    dump("hnT", hnT[:], [128, 8, T], BF16)
    stA_ps.close()
    with ExitStack() as stPP:
        T_ps = ps("T_ps", [128, 2, 4, 128], F32, stPP)
        W_ps = ps("W_ps", [128, 2, 8, 2, 64], BF16, stPP)
        for q0 in range(0, 16, 4):
            for ql in range(4):
                q = q0 + ql
                for e in range(2):
                    pp = slice(e * 64, (e + 1) * 64)
                    P.op("pe", lambda: PE_.matmul(T_ps[:, e, ql, :], lhsT=WSt_b[pp, q, 0, :], rhs=CN_b[pp, q, 0, :], start=True, stop=False),
                         reads=["WSt_b", "CN_b"], writes=["T_ps"])
                    P.op("pe", lambda: PE_.matmul(T_ps[:, e, ql, :], lhsT=WSt_b[pp, q, 1, :], rhs=CN_b[pp, q, 1, :], start=False, stop=True),
                         reads=["WSt_b", "CN_b"], writes=["T_ps"])
            for e in range(2):
                tt_(tmpT[:], T_ps[:, e], tmask[:, :].unsqueeze(1).to_broadcast([128, 4, 128]), ALU.mult, ["T_ps", "tmask"], ["tmpT"])
                for ql in range(4):
                    q = q0 + ql
                    g = 2 * q + e
                    dv(lambda: V.scalar_tensor_tensor(out=Toep_b[:, q, e, :], in0=ident_f[:], scalar=dl[:, g:g + 1], in1=tmpT[:, ql, :],
                                                      op0=ALU.mult, op1=ALU.add), ["tmpT", "ident_f", "dl"], ["Toep_b"])
        for q0 in range(0, 16, 8):
            for ql in range(8):
                q = q0 + ql
                for ri in range(2):
                    for e in range(2):
                        pp = slice(e * 64, (e + 1) * 64)
                        P.op("pe", lambda: PE_.transpose(out=W_ps[:, e, ql, ri, :], in_=WSt_b[pp, q, ri, :], identity=ident_b[pp, pp]),
                             reads=["WSt_b", "ident_b"], writes=["W_ps"])
            for e in range(2):
                P.op("act", lambda: S.activation(out=WS_b[:, q0:q0 + 8, e, :, :], in_=W_ps[:, e], func=AF.Copy), reads=["W_ps"], writes=["WS_b"])
        P.barrier()
    stP.close()
    dump("qkT", qkT[:], [128, 6, T], BF16)
    dump("Vsb", Vsb[:], [128, NT, 2, 65], BF16)
    dump("u_tm2", u_tm2[:], [128, 2, 32, 8, 16], BF16)
    stAB.close()

    stD = ExitStack()
    wglu_b = sb("wglu_b", [128, 4, 1024], BF16, stD)
    bglu = sb("bglu", [128, 1024], F32, stD)
    gssm = sb("gssm", [128, 512], F32, stD)
    sem_pd = P.new_sem("parD")
    sem_pdw = P.new_sem("parDw")

    def ploadD(dst, src, name):
        P.op("sp", lambda: nc.sync.dma_start(out=dst, in_=src), writes=[name], dma_sem=sem_pd, group=True)

    for k in range(4):
        P.op("pool", lambda: nc.gpsimd.dma_start(out=wglu_b[:, k, :], in_=wglu_d[:, k, :]), writes=[("wglu_b", k)], dma_sem=sem_pdw, group=True)
    ploadD(bglu[:], bglu_d, "bglu")
    ploadD(gssm[:], gssm_d, "gssm")

    stYT = ExitStack()
    yT = sb("yT", [128, 4, T], BF16, stYT)

    stTab = ExitStack()
    tab = sb("tab", [128, 2, 16, 256], F32, stTab)
    stTT = ExitStack()
    pki = sb("pki", [128, 2, 2, 256], I32, stTT)
    pkf = sb("pkf", [128, 2, 2, 256], F32, stTT)

    def tab_round(q0):
        tq = tab[:, :, q0:q0 + 2, :]
        rg = ("tab", q0)
        P.op("dve", lambda: V.tensor_tensor(out=tab[:, 0, q0:q0 + 2, :], in0=phi[:, q0:q0 + 2].unsqueeze(2).to_broadcast([128, 2, 256]),
                                             in1=cidx[:, :].unsqueeze(1).to_broadcast([128, 2, 256]), op=ALU.mult), reads=["phi", "cidx"], writes=[rg], cost=0.8)
        P.op("dve", lambda: V.tensor_scalar(out=tab[:, 1, q0:q0 + 2, :], in0=tab[:, 0, q0:q0 + 2, :], scalar1=0.25, scalar2=None, op0=ALU.add), reads=[rg], writes=[rg], cost=0.8)
        P.op("dve", lambda: V.tensor_copy(out=pki[:], in_=tq), reads=[rg], writes=["pki"], cost=1.1)
        P.op("dve", lambda: V.tensor_copy(out=pkf[:], in_=pki[:]), reads=["pki"], writes=["pkf"], cost=1.1)
        P.op("dve", lambda: V.tensor_tensor(out=tq, in0=tq, in1=pkf[:], op=ALU.subtract), reads=[rg, "pkf"], writes=[rg], cost=0.8)

    P.defer_begin()
    for q0_ in range(0, 16, 2):
        tab_round(q0_)
    P.defer_end()

    with ExitStack() as stC:
        st_ps = [ps(f"st_ps{i}", [128, 2, 2, 2, 128], F32, stC) for i in range(2)]
        o_ps = ps("o_ps", [128, 2, 512], F32, stC)
        ta_ps = [ps(f"ta_ps{i}", [128, 8, 128], BF16, stC) for i in range(2)]
        p_sb = [sb(f"p_sb{i}", [128, 2, 2, 2, 128], BF16, stC) for i in range(2)]
        den = sb("denC", [128, 8], F32, stC)
        rden = sb("rdenC", [128, NT, 8], F32, stC)
        ya = [sb(f"yaC{i}", [128, 512], F32, stC) for i in range(5)]
        yab = [sb(f"yabC{i}", [128, 512], BF16, stC) for i in range(2)]
        junkC = sb("junkC", [128, 512], BF16, stC)
        ssqC = sb("ssqC", [128, NT], F32, stC)
        rstdC = sb("rstdC", [128, NT], F32, stC)

        def c_scores(s_):
            n, Gk = s_ // 2, s_ % 2
            slot = s_ % 2
            tl = slice(n * 128, (n + 1) * 128)
            tp = slice((n - 1) * 128, n * 128)
            for hh in range(4):
                h = 4 * Gk + hh
                b0 = (h % 2) * 64
                kc = 4 if Gk == (h % 2) else 5
                first = (hh < 2)
                P.op("pe", lambda: PE_.matmul(st_ps[slot][:, hh % 2, 0, hh // 2, :], lhsT=qkT[b0:b0 + 64, kc, tl], rhs=qkT[b0:b0 + 64, h // 2, tl], start=first, stop=False,
                                              skip_group_check=True),
                     reads=[("qkT", n)], writes=[("st_ps", slot)])
                if n > 0:
                    P.op("pe", lambda: PE_.matmul(st_ps[slot][:, hh % 2, 1, hh // 2, :], lhsT=qkT[b0:b0 + 64, kc, tp], rhs=qkT[b0:b0 + 64, h // 2, tl], start=False, stop=False,
                                                  skip_group_check=True),
                         reads=[("qkT", n), ("qkT", n - 1)], writes=[("st_ps", slot)])
            for par in range(2):
                P.op("pe", lambda: PE_.matmul(st_ps[slot][:, par].rearrange("p c r q -> p (c r q)"), lhsT=ident_b[:], rhs=amask[:, par].rearrange("p c r q -> p (c r q)"),
                                              start=False, stop=True, skip_group_check=True),
                     reads=["amask", "ident_b"], writes=[("st_ps", slot)])

        def c_exp(s_):
            n, Gk = s_ // 2, s_ % 2
            slot = s_ % 2
            ncur = 1 if n == 0 else 2
            P.op("act", lambda: S.activation(out=p_sb[slot][:, :, 0:ncur], in_=st_ps[slot][:, :, 0:ncur], func=AF.Exp, scale=0.125),
                 reads=[("st_ps", slot)], writes=[("p_sb", slot)])

        def c_pv(s_):
            n, Gk = s_ // 2, s_ % 2
            slot = s_ % 2
            for hh in range(4):
                oo = o_ps[:, Gk, hh * 65:(hh + 1) * 65]
                P.op("pe", lambda: PE_.matmul(oo, lhsT=p_sb[slot][:, hh % 2, 0, hh // 2, :], rhs=Vsb[:, n, Gk, :], start=True, stop=(n == 0)),
                     reads=[("p_sb", slot), "Vsb"], writes=["o_ps"])
                if n > 0:
                    P.op("pe", lambda: PE_.matmul(oo, lhsT=p_sb[slot][:, hh % 2, 1, hh // 2, :], rhs=Vsb[:, n - 1, Gk, :], start=False, stop=True),
                         reads=[("p_sb", slot), "Vsb"], writes=["o_ps"])
            if Gk == 1:
                y5 = n % 5
                o4 = o_ps[:, :, 0:260].rearrange("p g (h d) -> p g h d", h=4)
                P.op("dve", lambda: V.tensor_tensor(out=den[:, :].rearrange("p (g h) -> p g h", g=2), in0=o4[:, :, :, 64],
                                                     in1=sinkexp[:, :].rearrange("p (g h) -> p g h", g=2), op=ALU.add),
                     reads=["o_ps", "sinkexp"], writes=["denC"])
                P.op("dve", lambda: V.reciprocal(out=rden[:, n, :], in_=den[:]), reads=["denC"], writes=[("rdenC", n)])
                P.op("dve", lambda: V.tensor_tensor(out=ya[y5][:, :].rearrange("p (g h d) -> p g h d", g=2, h=4), in0=o4[:, :, :, 0:64],
                                                     in1=rden[:, n, :].rearrange("p (g h) -> p g h", g=2).unsqueeze(3).to_broadcast([128, 2, 4, 64]), op=ALU.mult),
                     reads=["o_ps", ("rdenC", n)], writes=[("yaC", y5)])

        def odd(fn):
            def w(s_):
                if s_ % 2 == 1:
                    fn(s_ // 2)
            return w

        def c_sq(n):
            P.op("act", lambda: S.activation(out=junkC[:], in_=ya[n % 5][:], func=AF.Square, accum_out=ssqC[:, n:n + 1]),
                 reads=[("yaC", n % 5)], writes=["junkC", ("ssqC", n)])

        def c_r1(n):
            P.op("dve", lambda: V.tensor_scalar(out=rstdC[:, n:n + 1], in0=ssqC[:, n:n + 1], scalar1=float(1.0 / 512), scalar2=float(EPS),
                                                 op0=ALU.mult, op1=ALU.add), reads=[("ssqC", n)], writes=[("rstdC", n)])

        def c_r2(n):
            P.op("pool", lambda: G.tensor_tensor(out=rstdC[:, n:n + 1], in0=rstdC[:, n:n + 1], in1=nhalf[:, 0:1], op=ALU.pow),
                 reads=[("rstdC", n), "nhalf"], writes=[("rstdC", n)])

        def c_n(n):
            P.op("dve", lambda: V.scalar_tensor_tensor(out=yab[n % 2][:], in0=ya[n % 5][:], scalar=rstdC[:, n:n + 1], in1=gattn[:], op0=ALU.mult, op1=ALU.mult),
                 reads=[("yaC", n % 5), ("rstdC", n), "gattn"], writes=[("yabC", n % 2)])

        def c_t(n):
            y2 = n % 2
            for c in range(4):
                P.op("pe", lambda: PE_.transpose(out=ta_ps[y2][:, c, :], in_=yab[y2][:, c * 128:(c + 1) * 128], identity=ident_b[:]),
                     reads=[("yabC", y2), "ident_b"], writes=[("ta_ps", y2)])

        def c_e(n):
            y2 = n % 2
            tl = slice(n * 128, (n + 1) * 128)
            P.op("dve", lambda: V.tensor_copy(out=mergedT[:, 4:8, tl], in_=ta_ps[y2][:, 0:4, :]), reads=[("ta_ps", y2)], writes=[("mergedT_a", n)])

        _pipeline(2 * NT, [c_scores, c_exp, c_pv, odd(c_sq), odd(c_r1), odd(c_r2), odd(c_n), odd(c_t), odd(c_e)], hook=lambda it: P.flush(1.4))
        P.flush()
        for hh_ in range(2):
            P.op("act", lambda: S.activation(out=tab[:, hh_], in_=tab[:, hh_], func=AF.Sin, scale=float(TWO_PI)),
                 reads=[("tab", q) for q in range(0, 16, 2)], writes=["tab"])
        P.barrier()
    dump("mergedT", mergedT[:], [128, 8, T], BF16)
    stTT.close()
    stBC.close()

    dump("Toep_b", Toep_b[:], [128, 16, 2, 128], BF16)
    dump("WS_b", WS_b[:], [128, 16, 2, 2, 64], BF16)
    dump("CA_b", CA_b[:], [128, 16, 2, 128], BF16)
    dump("tab", tab[:], [128, 2, 16, 256])
    dump("Amag", Amag[:], [128, 16])
    stD2 = ExitStack()
    U8 = sb("U8", [128, 32, 256], BF16, stD2)
    with ExitStack() as stU8:
        tu_ps = [ps(f"tu_ps{i}", [128, 8, 128], BF16, stU8) for i in range(2)]
        cnt = 0
        for H in range(2):
            for g0 in range(0, 32, 8):
                sl = cnt % 2
                cnt += 1
                for gl in range(8):
                    g = g0 + gl
                    P.op("pe", lambda: PE_.transpose(out=tu_ps[sl][:, gl, :], in_=u_tm2[:, H, g, :, :].rearrange("p i h -> p (i h)"), identity=ident_b[:]),
                         reads=[("u_tm2", H), "ident_b"], writes=[("tu_ps", sl)])
                P.op("act", lambda: S.activation(out=U8[:, g0:g0 + 8, H * 128:(H + 1) * 128], in_=tu_ps[sl][:], func=AF.Copy),
                     reads=[("tu_ps", sl)], writes=["U8"])
        P.barrier()
    dump("U8", U8[:], [128, 32, 256], BF16)
    stU.close()
    stEW = ExitStack()
    wout_b = sbr("wout_b", [128, 8, 1024], BF16, stEW)
    gpm = sbr("gpm", [128, D], F32, stEW)
    gpf = sbr("gpf", [128, D], F32, stEW)
    sem_pe_ = P.new_sem("parE")
    sem_pew = P.new_sem("parEw")
    P.op("sp", lambda: nc.sync.dma_start(out=gpm[:], in_=gpm_d), writes=["gpm"], dma_sem=sem_pe_, group=True)
    P.op("sp", lambda: nc.sync.dma_start(out=gpf[:], in_=gpf_d), writes=["gpf"], dma_sem=sem_pe_, group=True)
    for k in range(8):
        P.op("pool", lambda: nc.gpsimd.dma_start(out=wout_b[:, k, :], in_=wout_d[:, k, :]), writes=[("wout_b", k)], dma_sem=sem_pew, group=True)

    with ExitStack() as stS:
        S_ps = [ps(f"S_ps{i}", [128, 2, 256], F32, stS) for i in range(2)]
        Y_ps = [ps(f"Y_ps{i}", [128, 2, 4, 128], F32, stS) for i in range(2)]
        ty_ps = [ps(f"ty_ps{i}", [128, 8, 128], BF16, stS) for i in range(2)]

        def mk(name, dt=F32):
            return [sb(f"{name}{i}", [128, 2, 256], dt, stS) for i in range(2)]

        tA, tB, Sm, Xt, dA, dB = mk("tA"), mk("tB"), mk("Sm"), mk("Xt"), mk("dA"), mk("dB")
        Xprev = mk("Xprev", BF16)
        y_tm2b = [sb(f"y_tm2b{i}", [128, 2, 8, 128], BF16, stS) for i in range(2)]
        for i in range(2):
            P.op("dve", lambda: V.memset(Xprev[i][:], 0.0), writes=[("Xprev", i)])

        def tabs(q):
            return (tab[:, 1, q, :].unsqueeze(1).to_broadcast([128, 2, 256]), tab[:, 0, q, :].unsqueeze(1).to_broadcast([128, 2, 256]))

        def s0(q):
            sl = q % 2
            for ri in range(2):
                for e in range(2):
                    g = 2 * q + e
                    P.op("pe", lambda: PE_.matmul(S_ps[sl][e * 64:(e + 1) * 64, ri, :], lhsT=WS_b[:, q, e, ri, :], rhs=U8[:, g, :], start=True, stop=True),
                         reads=["WS_b", "U8"], writes=[("S_ps", sl)])

        def s1(q):
            sl = q % 2
            cosb, sinb = tabs(q)
            P.op("dve", lambda: V.tensor_tensor(out=tA[sl][:], in0=S_ps[sl][:], in1=cosb, op=ALU.mult), reads=[("S_ps", sl), "tab"], writes=[("tA", sl)])
            P.op("dve", lambda: V.tensor_tensor(out=tB[sl][:], in0=S_ps[sl][:], in1=sinb, op=ALU.mult), reads=[("S_ps", sl), "tab"], writes=[("tB", sl)])

        def s2(q):
            sl = q % 2
            P.op("dve", lambda: V.tensor_tensor(out=Sm[sl][:, 0], in0=tA[sl][:, 0], in1=tB[sl][:, 1], op=ALU.add), reads=[("tA", sl), ("tB", sl)], writes=[("Sm", sl)])
            P.op("dve", lambda: V.tensor_tensor(out=Sm[sl][:, 1], in0=tA[sl][:, 1], in1=tB[sl][:, 0], op=ALU.subtract), reads=[("tA", sl), ("tB", sl)], writes=[("Sm", sl)])

        def s3(q):
            sl = q % 2
            for ri in range(2):
                P.op("dve", lambda: V.tensor_tensor_scan(out=Xt[sl][:, ri, :], data0=Amag[:, q:q + 1].to_broadcast([128, 256]), data1=Sm[sl][:, ri, :],
                                                         initial=0.0, op0=ALU.mult, op1=ALU.add), reads=[("Sm", sl), "Amag"], writes=[("Xt", sl)])

        def s4(q):
            sl = q % 2
            cosb, sinb = tabs(q)
            P.op("dve", lambda: V.tensor_tensor(out=dA[sl][:], in0=Xt[sl][:], in1=cosb, op=ALU.mult), reads=[("Xt", sl), "tab"], writes=[("dA", sl)])
            P.op("dve", lambda: V.tensor_tensor(out=dB[sl][:], in0=Xt[sl][:], in1=sinb, op=ALU.mult), reads=[("Xt", sl), "tab"], writes=[("dB", sl)])

        def s5(q):
            sl = q % 2
            P.op("dve", lambda: V.tensor_tensor(out=Xprev[sl][:, 0, 1:256], in0=dA[sl][:, 0, 0:255], in1=dB[sl][:, 1, 0:255], op=ALU.subtract),
                 reads=[("dA", sl), ("dB", sl)], writes=[("Xprev", sl)])
            P.op("dve", lambda: V.tensor_tensor(out=Xprev[sl][:, 1, 1:256], in0=dA[sl][:, 1, 0:255], in1=dB[sl][:, 0, 0:255], op=ALU.add),
                 reads=[("dA", sl), ("dB", sl)], writes=[("Xprev", sl)])

        def s6(q):
            sl = q % 2
            bt = q // 4
            yb = y_tm2b[bt % 2]
            for H in range(2):
                hs = slice(H * 128, (H + 1) * 128)
                for e in range(2):
                    g = 2 * q + e
                    pp = slice(e * 64, (e + 1) * 64)
                    yo = Y_ps[sl][:, e, H, :]
                    P.op("pe", lambda: PE_.matmul(yo, lhsT=U8[:, g, hs], rhs=Toep_b[:, q, e, :], start=True, stop=False),
                         reads=["U8", "Toep_b"], writes=[("Y_ps", sl)])
                    P.op("pe", lambda: PE_.matmul(yo, lhsT=Xprev[sl][pp, 0, hs], rhs=CA_b[pp, q, 0, :], start=False, stop=False),
                         reads=[("Xprev", sl), "CA_b"], writes=[("Y_ps", sl)])
                    P.op("pe", lambda: PE_.matmul(yo, lhsT=Xprev[sl][pp, 1, hs], rhs=CA_b[pp, q, 1, :], start=False, stop=True),
                         reads=[("Xprev", sl), "CA_b"], writes=[("Y_ps", sl)])
            for e in range(2):
                c0 = (q % 4) * 32 + e * 16
                P.op("act", lambda: S.activation(out=yb[:, :, :, c0:c0 + 16], in_=Y_ps[sl][:, e, 0:2].rearrange("p H (j h) -> p H j h", j=8), func=AF.Gelu_apprx_tanh),
                     reads=[("Y_ps", sl)], writes=[("y_tm2b", bt % 2)])

        def s7(q):
            if q % 4 != 3:
                return
            bt = q // 4
            yb = y_tm2b[bt % 2]
            for H in range(2):
                sl = H
                for j in range(8):
                    P.op("pe", lambda: PE_.transpose(out=ty_ps[sl][:, j, :], in_=yb[:, H, j, :], identity=ident_b[:]),
                         reads=[("y_tm2b", bt % 2), "ident_b"], writes=[("ty_ps", sl)])
                P.op("act", lambda: S.activation(out=yT[:, bt, H * 1024:(H + 1) * 1024], in_=ty_ps[sl][:].rearrange("p j c -> p (j c)"), func=AF.Copy),
                     reads=[("ty_ps", sl)], writes=[("yT", H)])

        _pipeline(16, [s0, s1, s2, s3, s4, s5, s6, s7])
        P.barrier()
    dump("yT", yT[:], [128, 4, T], BF16)
    stD2.close()
    stTab.close()
    stEW2 = ExitStack()
    wdn_b = sbr("wdn_b", [128, NF, 1024], BF16, stEW2)
    sem_wdn = P.new_sem("wdn")
    for f in range(NF):
        P.op("pool", lambda: nc.gpsimd.dma_start(out=wdn_b[:, f, :], in_=wdn_d[:, f, :]), writes=[("wdn_b", f)], dma_sem=sem_wdn, group=True)

    with ExitStack() as stG:
        z_ps = [ps(f"z_ps{i}", [128, 1024], F32, stG) for i in range(2)]
        tm_ps = [ps(f"tm_ps{i}", [128, 8, 128], BF16, stG) for i in range(2)]
        ga = [sb(f"ga{i}", [128, 512], F32, stG) for i in range(3)]
        gb = [sb(f"gb{i}", [128, 512], F32, stG) for i in range(2)]
        sg = [sb(f"sg{i}", [128, 512], F32, stG) for i in range(2)]
        go = [sb(f"go{i}", [128, 512], F32, stG) for i in range(5)]
        gon = [sb(f"gon{i}", [128, 512], BF16, stG) for i in range(2)]
        junkG = sb("junkG", [128, 512], BF16, stG)
        ssqG = sb("ssqG", [128, NT], F32, stG)
        rstdG = sb("rstdG", [128, NT], F32, stG)

        def g_mm(tix):
            H, j = tix // 8, tix % 8
            sl = tix % 2
            cols = slice(H * 1024 + j * 128, H * 1024 + (j + 1) * 128)
            for k in range(4):
                P.op("pe", lambda: PE_.matmul(z_ps[sl][:, 0:512], lhsT=yT[:, k, cols], rhs=wglu_b[:, k, 0:512], start=(k == 0), stop=(k == 3)),
                     reads=[("yT", H), ("wglu_b", k)], writes=[("z_ps", sl)])
                P.op("pe", lambda: PE_.matmul(z_ps[sl][:, 512:1024], lhsT=yT[:, k, cols], rhs=wglu_b[:, k, 512:1024], start=(k == 0), stop=(k == 3)),
                     reads=[("yT", H), ("wglu_b", k)], writes=[("z_ps", sl)])

        def g_a(tix):
            sl = tix % 2
            P.op("dve", lambda: V.tensor_tensor(out=ga[tix % 3][:], in0=z_ps[sl][:, 0:512], in1=bglu[:, 0:512], op=ALU.add), reads=[("z_ps", sl), "bglu"], writes=[("ga", tix % 3)])
            P.op("dve", lambda: V.tensor_tensor(out=gb[tix % 2][:], in0=z_ps[sl][:, 512:1024], in1=bglu[:, 512:1024], op=ALU.add), reads=[("z_ps", sl), "bglu"], writes=[("gb", tix % 2)])

        def g_sig(tix):
            P.op("act", lambda: S.activation(out=sg[tix % 2][:], in_=gb[tix % 2][:], func=AF.Sigmoid), reads=[("gb", tix % 2)], writes=[("sg", tix % 2)])

        def g_mul(tix):
            r = tix % 5
            P.op("dve", lambda: V.tensor_tensor(out=go[r][:], in0=ga[tix % 3][:], in1=sg[tix % 2][:], op=ALU.mult), reads=[("ga", tix % 3), ("sg", tix % 2)], writes=[("go", r)])

        def g_sq(tix):
            r = tix % 5
            P.op("act", lambda: S.activation(out=junkG[:], in_=go[r][:], func=AF.Square, accum_out=ssqG[:, tix:tix + 1]), reads=[("go", r)], writes=["junkG", ("ssqG", tix)])

        def g_r1(tix):
            P.op("dve", lambda: V.tensor_scalar(out=rstdG[:, tix:tix + 1], in0=ssqG[:, tix:tix + 1], scalar1=float(1.0 / 512), scalar2=float(EPS),
                                                 op0=ALU.mult, op1=ALU.add), reads=[("ssqG", tix)], writes=[("rstdG", tix)])

        def g_r2(tix):
            P.op("pool", lambda: G.tensor_tensor(out=rstdG[:, tix:tix + 1], in0=rstdG[:, tix:tix + 1], in1=nhalf[:, 0:1], op=ALU.pow),
                 reads=[("rstdG", tix), "nhalf"], writes=[("rstdG", tix)])

        def g_n(tix):
            r = tix % 5
            P.op("dve", lambda: V.scalar_tensor_tensor(out=gon[tix % 2][:], in0=go[r][:], scalar=rstdG[:, tix:tix + 1], in1=gssm[:], op0=ALU.mult, op1=ALU.mult),
                 reads=[("go", r), ("rstdG", tix), "gssm"], writes=[("gon", tix % 2)])

        def g_t(tix):
            sl = tix % 2
            for c in range(4):
                P.op("pe", lambda: PE_.transpose(out=tm_ps[sl][:, c, :], in_=gon[tix % 2][:, c * 128:(c + 1) * 128], identity=ident_b[:]),
                     reads=[("gon", tix % 2), "ident_b"], writes=[("tm_ps", sl)])

        def g_d(tix):
            H, j = tix // 8, tix % 8
            sl = tix % 2
            P.op("act", lambda: S.activation(out=mergedT[:, 0:4, tix * 128:(tix + 1) * 128], in_=tm_ps[sl][:, 0:4, :], func=AF.Copy),
                 reads=[("tm_ps", sl)], writes=[("mergedT_s", tix)])

        _pipeline(NT, [g_mm, g_a, g_sig, g_mul, g_sq, g_r1, g_r2, g_n, g_t, g_d])
        P.barrier()
    dump("mergedT2", mergedT[:], [128, 8, T], BF16)
    stYT.close()
    stD.close()
    stPO.close()

    stE = ExitStack()
    wgu_s = [sb(f"wgu_s{i}", [128, 2, 8, 256], BF16, stE) for i in range(3)]
    h1 = sb("h1", [128, 5, D], F32, stE)
    hn2 = [sb(f"hn2_{i}", [128, D], BF16, stE) for i in range(2)]
    hn2T = sb("hn2T", [128, 8, 512], BF16, stE)
    actT = sb("actT", [128, NF, 512], BF16, stE)
    sgt = [sb(f"sgt{i}", [128, 512], F32, stE) for i in range(2)]
    junkE = sb("junkE", [128, D], BF16, stE)
    ot = [sb(f"ot{i}", [128, D], F32, stE) for i in range(2)]
    ssqE = sb("ssqE", [128, 3 * NT], F32, stE)
    rstdE = sb("rstdE", [128, 3 * NT], F32, stE)
    eps_ = ps("eps", [128, 7, 512], F32, stE)
    tp2_ps = ps("tp2_ps", [128, 8, 128], BF16, stE)

    def tok(j):
        return eps_[:, 2 * j:2 * j + 2, :].rearrange("p a c -> p (a c)"), [("bank", 2 * j), ("bank", 2 * j + 1)]

    def gub(i):
        return eps_[:, 4 + i, :], [("bank", 4 + i)]

    sem_wgu = [P.new_sem(f"wgu{i}") for i in range(3)]
    sem_h1 = [P.new_sem(f"h1_{i}") for i in range(5)]
    sem_out = [P.new_sem(f"outst{i}") for i in range(2)]
    wgu_cnt = [0]
    wgu_q = []

    def issue_wgu(fp):
        slot = wgu_cnt[0] % 3
        wgu_cnt[0] += 1
        P.op("pool", lambda: nc.gpsimd.dma_start(out=wgu_s[slot][:].rearrange("p a k c -> p (a k c)"), in_=wgu_d[fp]),
             writes=[("wgu_s", slot)], dma_sem=sem_wgu[slot])
        wgu_q.append(slot)

    def rstd_of(col, dim):
        P.op("pool", lambda: G.tensor_scalar(out=rstdE[:, col:col + 1], in0=ssqE[:, col:col + 1], scalar1=float(1.0 / dim), scalar2=float(EPS),
                                              op0=ALU.mult, op1=ALU.add), reads=[("ssqE", col)], writes=[("rstdE", col)])
        P.op("pool", lambda: G.tensor_tensor(out=rstdE[:, col:col + 1], in0=rstdE[:, col:col + 1], in1=nhalf[:, 0:1], op=ALU.pow),
             reads=[("rstdE", col), "nhalf"], writes=[("rstdE", col)])

    tok_rot = [0]
    tok_of = {}
    ot_rot = [0]
    ot_of = {}
    hn_rot = [0]
    hn_of = {}
    gu_rot = [0]
    sg_rot = [0]

    def a1(B, t):
        n = 4 * B + t
        hs = n % 5
        H, j = n // 8, n % 8
        tl = slice(n * 128, (n + 1) * 128)
        ta = slice(H * 1024 + j, (H + 1) * 1024, 8)
        j3 = tok_rot[0] % 3
        tok_rot[0] += 1
        tok_of[("a", n)] = j3
        mo, mreg = tok(j3)
        P.op("sp", lambda: nc.sync.dma_start(out=h1[:, hs, :], in_=x_d[ta, :]), writes=[("h1", hs)], dma_sem=sem_h1[hs])
        rda = [("mergedT_a", n2) for n2 in range(H * 8, H * 8 + 8)]
        for k in range(8):
            cols = tl if k < 4 else ta
            rd = ([("mergedT_s", n)] if k < 4 else rda) + [("wout_b", k)]
            P.op("pe", lambda: PE_.matmul(mo[:, 0:512], lhsT=mergedT[:, k, cols], rhs=wout_b[:, k, 0:512], start=(k == 0), stop=(k == 7)),
                 reads=rd, writes=mreg)
            P.op("pe", lambda: PE_.matmul(mo[:, 512:1024], lhsT=mergedT[:, k, cols], rhs=wout_b[:, k, 512:1024], start=(k == 0), stop=(k == 7)),
                 reads=rd, writes=mreg)

    def a2(B, t):
        n = 4 * B + t
        hs = n % 5
        mo, mreg = tok(tok_of[("a", n)])
        r = ot_rot[0] % 2
        ot_rot[0] += 1
        c1 = n
        P.op("act", lambda: S.activation(out=junkE[:], in_=mo, func=AF.Square, accum_out=ssqE[:, c1:c1 + 1]),
             reads=mreg, writes=["junkE", ("ssqE", c1)])
        rstd_of(c1, D)
        P.op("dve", lambda: V.scalar_tensor_tensor(out=ot[r][:], in0=mo, scalar=rstdE[:, c1:c1 + 1], in1=gpm[:], op0=ALU.mult, op1=ALU.mult),
             reads=mreg + [("rstdE", c1), "gpm"], writes=[("ot", r)])
        P.op("dve", lambda: V.tensor_tensor(out=h1[:, hs, :], in0=h1[:, hs, :], in1=ot[r][:], op=ALU.add), reads=[("h1", hs), ("ot", r)], writes=[("h1", hs)])

    def a3(B, t):
        n = 4 * B + t
        hs = n % 5
        r = hn_rot[0] % 2
        hn_rot[0] += 1
        hn_of[n] = r
        c2 = NT + n
        P.op("act", lambda: S.activation(out=junkE[:], in_=h1[:, hs, :], func=AF.Square, accum_out=ssqE[:, c2:c2 + 1]),
             reads=[("h1", hs)], writes=["junkE", ("ssqE", c2)])
        rstd_of(c2, D)
        P.op("act", lambda: S.activation(out=hn2[r][:], in_=h1[:, hs, :], func=AF.Copy, scale=rstdE[:, c2:c2 + 1]),
             reads=[("h1", hs), ("rstdE", c2)], writes=[("hn2", r)])

    def a4(B, t):
        n = 4 * B + t
        r = hn_of[n]
        for k in range(8):
            P.op("pe", lambda: PE_.transpose(out=tp2_ps[:, k, :], in_=hn2[r][:, k * 128:(k + 1) * 128], identity=ident_b[:]),
                 reads=[("hn2", r), "ident_b"], writes=["tp2_ps"])
        P.op("dve", lambda: V.tensor_tensor(out=hn2T[:, :, t * 128:(t + 1) * 128], in0=tp2_ps[:],
                                             in1=gpffn[:, :].unsqueeze(2).to_broadcast([128, 8, 128]), op=ALU.mult),
             reads=["tp2_ps", "gpffn"], writes=[("hn2T", t)])

    def b1(B, t):
        n = 4 * B + t
        j = tok_rot[0] % 3
        tok_rot[0] += 1
        tok_of[("b", n)] = j
        dn, dreg = tok(j)
        for f in range(NF):
            P.op("pe", lambda: PE_.matmul(dn[:, 0:512], lhsT=actT[:, f, t * 128:(t + 1) * 128], rhs=wdn_b[:, f, 0:512], start=(f == 0), stop=(f == NF - 1)),
                 reads=[("actT", f), ("wdn_b", f)], writes=dreg)
            P.op("pe", lambda: PE_.matmul(dn[:, 512:1024], lhsT=actT[:, f, t * 128:(t + 1) * 128], rhs=wdn_b[:, f, 512:1024], start=(f == 0), stop=(f == NF - 1)),
                 reads=[("actT", f), ("wdn_b", f)], writes=dreg)

    def b2(B, t):
        n = 4 * B + t
        hs = n % 5
        tl = slice((n // 8) * 1024 + (n % 8), (n // 8 + 1) * 1024, 8)
        dn, dreg = tok(tok_of[("b", n)])
        r = ot_rot[0] % 2
        ot_rot[0] += 1
        c3 = 2 * NT + n
        P.op("act", lambda: S.activation(out=junkE[:], in_=dn, func=AF.Square, accum_out=ssqE[:, c3:c3 + 1]),
             reads=dreg, writes=["junkE", ("ssqE", c3)])
        rstd_of(c3, D)
        P.op("dve", lambda: V.scalar_tensor_tensor(out=ot[r][:], in0=dn, scalar=rstdE[:, c3:c3 + 1], in1=gpf[:], op0=ALU.mult, op1=ALU.mult),
             reads=dreg + [("rstdE", c3), "gpf"], writes=[("ot", r)])
        P.op("dve", lambda: V.tensor_tensor(out=ot[r][:], in0=ot[r][:], in1=h1[:, hs, :], op=ALU.add), reads=[("h1", hs), ("ot", r)], writes=[("ot", r)])
        P.op("sp", lambda: nc.sync.dma_start(out=out_d[tl, :], in_=ot[r][:]), reads=[("ot", r)], writes=[("out", n)], dma_sem=sem_out[r])

    def e2(B):
        for fp in range(11):
            if fp + 2 < 11:
                issue_wgu(fp + 2)
            elif B < 3:
                issue_wgu(fp + 2 - 11)
            slot = wgu_q.pop(0)
            for fl in range(2):
                f = 2 * fp + fl
                gi = gu_rot[0] % 3
                ui = (gu_rot[0] + 1) % 3
                gu_rot[0] += 2
                gps, greg = gub(gi)
                ups, ureg = gub(ui)
                for k in range(8):
                    P.op("pe", lambda: PE_.matmul(gps, lhsT=wgu_s[slot][:, fl, k, 0:128], rhs=hn2T[:, k, :], start=(k == 0), stop=(k == 7)),
                         reads=[("wgu_s", slot)] + [("hn2T", t) for t in range(4)], writes=greg)
                for k in range(8):
                    P.op("pe", lambda: PE_.matmul(ups, lhsT=wgu_s[slot][:, fl, k, 128:256], rhs=hn2T[:, k, :], start=(k == 0), stop=(k == 7)),
                         reads=[("wgu_s", slot)] + [("hn2T", t) for t in range(4)], writes=ureg)
                si = sg_rot[0] % 2
                sg_rot[0] += 1
                P.op("act", lambda: S.activation(out=sgt[si][:], in_=gps, func=AF.Silu), reads=greg, writes=[("sgt", si)])
                P.op("dve", lambda: V.tensor_tensor(out=actT[:, f, :], in0=ups, in1=sgt[si][:], op=ALU.mult),
                     reads=ureg + [("sgt", si)], writes=[("actT", f)])

    issue_wgu(0)
    issue_wgu(1)
    _pipeline(4, [lambda t: a1(0, t), lambda t: a2(0, t), lambda t: a3(0, t), lambda t: a4(0, t)])
    for B in range(4):
        e2(B)
        if B < 3:
            for it in range(6):
                if 1 <= it < 5:
                    b2(B, it - 1)
                if it < 4:
                    a1(B + 1, it)
                    b1(B, it)
                    a2(B + 1, it)
                if 1 <= it < 5:
                    a3(B + 1, it - 1)
                if 2 <= it < 6:
                    a4(B + 1, it - 2)
        else:
            _pipeline(4, [lambda t: b1(B, t), lambda t: b2(B, t)])
    P.barrier()
    stE.close()
    stEW2.close()
    stEW.close()

    P.finish()
    es.close()
    P.es.close()
    return P, dbg_d


def _prep_shared(inp):
    f = np.float32
    sh = {}

    def fm(v, k):
        return np.ascontiguousarray(np.asarray(v, f).reshape(k, 128).T)

    def rep(v):
        v = np.asarray(v, f).reshape(1, -1)
        return np.ascontiguousarray(np.broadcast_to(v, (128, v.shape[1])))

    gpre_h = fm(inp["g_pre_mix"][0], 8)
    gpffn_h = fm(inp["g_pre_ffn"][0], 8)
    sh["gpm"] = rep(inp["g_post_mix"][0])
    sh["gpf"] = rep(inp["g_post_ffn"][0])
    sh["gssm"] = rep(inp["g_ssm_out"][0])
    gattn_h = rep(inp["g_attn_out"][0])
    sh["bglu"] = rep(inp["b_glu"][0])
    sink_h = rep(inp["attn_sinks"][0])
    sh["win"] = np.ascontiguousarray(np.asarray(inp["w_in"][0], f).reshape(8, 128, 1280).transpose(1, 0, 2))
    sh["wglu"] = np.ascontiguousarray(np.asarray(inp["w_glu"][0], f).reshape(4, 128, 1024).transpose(1, 0, 2))
    sh["wout"] = np.ascontiguousarray(np.asarray(inp["w_out"][0], f).reshape(8, 128, 1024).transpose(1, 0, 2))
    wgu = np.asarray(inp["w_gate_up"][0], f)
    wg = wgu[:, :DFF].reshape(8, 128, NF, 128)
    wu = wgu[:, DFF:].reshape(8, 128, NF, 128)
    w2 = np.concatenate([wg, wu], axis=3)
    w2 = w2.transpose(2, 1, 0, 3).reshape(11, 2, 128, 8, 256)
    sh["wgu"] = np.ascontiguousarray(w2.transpose(0, 2, 1, 3, 4).reshape(11, 128, 2 * 8 * 256))
    sh["wdn"] = np.ascontiguousarray(np.asarray(inp["w_down"][0], f).reshape(NF, 128, 1024).transpose(1, 0, 2))

    def gl(v):
        v = np.asarray(v, f)
        rest = v.shape[2:]
        v = v.reshape(16, 2, 64, *rest)
        v = np.moveaxis(v, 0, 2)
        return np.ascontiguousarray(v.reshape(128, 16, *rest))

    lre_h = gl(inp["ssm_lambda_re"][0])
    lim_h = gl(inp["ssm_lambda_im"][0])
    ldt_h = gl(np.broadcast_to(np.asarray(inp["ssm_log_dt"][0], f)[:, None], (32, 64)))
    bre_h = gl(inp["ssm_b_re"][0])
    bim_h = gl(inp["ssm_b_im"][0])
    cre_h = gl(np.asarray(inp["ssm_c_re"][0], f).transpose(0, 2, 1))
    cim_h = gl(np.asarray(inp["ssm_c_im"][0], f).transpose(0, 2, 1))
    d = np.asarray(inp["ssm_d"][0], f)
    dl_h = np.ascontiguousarray(np.broadcast_to(d.T[None, :, :], (8, 16, 32)).reshape(128, 32))
    sh["pkB"] = np.ascontiguousarray(np.concatenate([lre_h, lim_h, ldt_h, bre_h.reshape(128, 256), bim_h.reshape(128, 256),
                                                     cre_h.reshape(128, 256), cim_h.reshape(128, 256)], axis=1))
    sh["c_ident"] = np.eye(128, dtype=f)
    kk = np.arange(128)[:, None]
    qq = np.arange(128)[None, :]
    am = np.zeros((128, 2, 2, 2, 128), f)
    NEG = -30000.0
    am[:, :, 0, :, :] = np.where(qq >= kk, 0.0, NEG)[:, None, None, :]
    am[:, :, 1, :, :] = np.where(kk > qq, 0.0, NEG)[:, None, None, :]
    sh["c_amask"] = am
    ii = (np.arange(128) // 16)[:, None]
    jj = (np.arange(128) // 16)[None, :]
    tmask_h = (jj >= ii).astype(f)
    kv = np.concatenate([7 - np.arange(8), np.arange(8) - 7, np.arange(8) + 1]).astype(f)
    kv_h = np.broadcast_to(kv[None, :], (128, 24))
    cidx_h = np.broadcast_to(np.arange(256, dtype=f)[None, :], (128, 256))
    invf = (np.float32(500000.0) ** (-(np.arange(8, dtype=f) * np.float32(2.0) / np.float32(16.0)))).astype(f)
    invf_h = np.broadcast_to(invf[None, :], (128, 8))
    sh["pkA"] = np.ascontiguousarray(np.concatenate([sh["c_ident"], tmask_h, kv_h, cidx_h, invf_h, gpre_h, gpffn_h, gattn_h, sink_h, dl_h], axis=1).astype(f))
    return sh


def _in_maps(inp):
    sh = _prep_shared(inp)
    x = np.asarray(inp["x"], np.float32)
    pos = np.asarray(inp["positions"], np.int32)
    maps = []
    for b in range(8):
        m = dict(sh)
        m["x"] = np.ascontiguousarray(x[b])
        m["pos"] = np.ascontiguousarray(pos[b].reshape(NT, 128).T)
        maps.append(m)
    return maps


_CACHE = {}


def _get_nc(dbg=()):
    key = tuple(dbg)
    if key not in _CACHE:
        nc0 = bass.Bass("TRN2", target_bir_lowering=False)
        P0, _ = _build(nc0, None, dbg)
        plan = P0.get_plan()
        nc = bass.Bass("TRN2", target_bir_lowering=False)
        _, dbg_d = _build(nc, plan, dbg)
        _CACHE[key] = (nc, dbg_d)
    return _CACHE[key]


def kernel(**inputs):
    nc, _ = _get_nc()
    maps = _in_maps(inputs)
    res = run_bass_kernel_spmd(nc, maps, core_ids=list(range(8)))
    out = np.stack([np.asarray(r["out"], np.float32) for r in res.results], axis=0)
    return out
```
